# Optimizing a Trainium2 kernel written in Bass

```python
import math, functools
import jax, jax.numpy as jnp
from jax import lax
import numpy as np

D_MODEL = 1024
BATCH = 16
SEQ = 256
DEPTH = 4
DEC_BATCH = 4
DEC_SEQ = 2048
PAST_LEN = 256

GRID_W = 64
MIX_HALF = D_MODEL // 2
H_RET = 4
DV_RET = MIX_HALF // H_RET
DK_RET = DV_RET // 2
RET_CHUNK = 128
H_DIFF = 4
DV_DIFF = MIX_HALF // H_DIFF
DH_DIFF = DV_DIFF // 2
Q_BLOCK = 128
H_SWA = 8
KV_SWA = 2
SWA_GROUP = H_SWA // KV_SWA
DH_SWA = MIX_HALF // H_SWA
WINDOW = 128
SWA_BLOCK = 128
H_GLA = 4
DV_GLA = MIX_HALF // H_GLA
DK_GLA = DV_GLA // 2
GLA_RANK = 16
GLA_TAU = 16.0
GLA_CHUNK = 16
D_FF = 4 * D_MODEL
ROPE_BASE = 10000.0
N_EVEN = (DEPTH + 1) // 2
N_ODD = DEPTH // 2
ALPHA = (2 * DEPTH) ** 0.25
BETA = (8 * DEPTH) ** -0.25
EVEN_SPLITS = (H_RET * DK_RET, H_RET * DK_RET, H_RET * DV_RET, H_RET * DV_RET,
               H_DIFF * 2 * DH_DIFF, H_DIFF * 2 * DH_DIFF, H_DIFF * DV_DIFF)
ODD_SPLITS = (H_SWA * DH_SWA, KV_SWA * DH_SWA, KV_SWA * DH_SWA, H_GLA * DK_GLA, H_GLA * DK_GLA,
              H_GLA * DV_GLA, H_GLA * DV_GLA, 2 * GLA_RANK)
EVEN_IN = sum(EVEN_SPLITS)
ODD_IN = sum(ODD_SPLITS)
EVEN_OUT = H_RET * DV_RET + H_DIFF * DV_DIFF
ODD_OUT = H_SWA * DH_SWA + H_GLA * DV_GLA

kernel_name = 'hybrid_diffusion_trunk_step'


def _split(t, sizes):
    return jnp.split(t, [int(i) for i in np.cumsum(sizes)[:-1]], axis=-1)


def _heads(t, n):
    B, T, _ = t.shape
    return t.reshape(B, T, n, -1).transpose(0, 2, 1, 3)


def _merge_heads(t):
    B, H, T, d = t.shape
    return t.transpose(0, 2, 1, 3).reshape(B, T, H * d)


def _diff_heads(t):
    B, T, _ = t.shape
    return t.reshape(B, T, H_DIFF, 2, DH_DIFF).transpose(0, 2, 3, 1, 4)


def _layer_norm(x, g, b, eps=1e-5):
    xf = x.astype(jnp.float32)
    mu = jnp.mean(xf, axis=-1, keepdims=True)
    var = jnp.mean(jnp.square(xf - mu), axis=-1, keepdims=True)
    y = (xf - mu) * lax.rsqrt(var + eps) * g.astype(jnp.float32) + b.astype(jnp.float32)
    return y.astype(x.dtype)


def _head_norm(t, eps=1e-5):
    tf = t.astype(jnp.float32)
    mu = jnp.mean(tf, axis=-1, keepdims=True)
    var = jnp.mean(jnp.square(tf - mu), axis=-1, keepdims=True)
    return (tf - mu) * lax.rsqrt(var + eps)


def _axial_rope(rows, dim):
    row = jnp.repeat(jnp.arange(rows), GRID_W).astype(jnp.float32)
    col = jnp.tile(jnp.arange(GRID_W), rows).astype(jnp.float32)
    half = dim // 2
    freqs = ROPE_BASE ** (-jnp.arange(0, half, 2, dtype=jnp.float32) / half)
    ar = row[:, None] * freqs
    ac = col[:, None] * freqs
    return (jnp.cos(ar), jnp.sin(ar), jnp.cos(ac), jnp.sin(ac))


def _rot_half(x, cos, sin):
    x1, x2 = jnp.split(x, 2, axis=-1)
    return jnp.concatenate([x1 * cos - x2 * sin, x2 * cos + x1 * sin], axis=-1)


def _apply_axial_rope(x, tabs):
    cr, sr, cc, sc = tabs
    xf = x.astype(jnp.float32)
    xr, xc = jnp.split(xf, 2, axis=-1)
    return jnp.concatenate([_rot_half(xr, cr, sr), _rot_half(xc, cc, sc)], axis=-1).astype(x.dtype)


def _sink_softmax(logits, sink):
    full = jnp.concatenate([logits, jnp.broadcast_to(sink, logits.shape[:-1] + (1,))], axis=-1)
    return jax.nn.softmax(full, axis=-1)[..., :-1]


def _retention_dir(q, k, v, log_gamma, s0):
    B, H, T, dk = q.shape
    dv = v.shape[-1]
    C = RET_CHUNK
    N = T // C
    qc = q.reshape(B, H, N, C, dk)
    kc = k.reshape(B, H, N, C, dk)
    vc = v.reshape(B, H, N, C, dv)
    pos = jnp.arange(C, dtype=jnp.float32)
    dist = pos[:, None] - pos[None, :]
    dmat = jnp.where(dist >= 0, jnp.exp(log_gamma[:, None, None] * jnp.maximum(dist, 0.0)), 0.0)
    scores = jnp.einsum('bhnid,bhnjd->bhnij', qc, kc) * dmat[None, :, None]
    o_intra = jnp.einsum('bhnij,bhnje->bhnie', scores, vc)
    k_w = jnp.exp(log_gamma[:, None] * (C - 1 - pos))
    u = jnp.einsum('bhnjd,hj,bhnje->bhnde', kc, k_w, vc)
    chunk_decay = jnp.exp(log_gamma * C)[None, :, None, None]

    def step(s, uu):
        return chunk_decay * s + uu, s

    s_fin, s_prev = lax.scan(step, s0, jnp.moveaxis(u, 2, 0))
    q_w = jnp.exp(log_gamma[:, None] * (pos + 1.0))
    o_inter = jnp.einsum('bhnid,hi,nbhde->bhnie', qc, q_w, s_prev)
    return (o_intra + o_inter).reshape(B, H, T, dv), s_fin


def _bidir_retention(q, k, v, log_g, s0f, s0b):
    f32 = jnp.float32
    q, k, v = q.astype(f32), k.astype(f32), v.astype(f32)
    o_f, s_f = _retention_dir(q, k, v, log_g[0], s0f.astype(f32))
    fl = lambda t: jnp.flip(t, axis=2)
    o_b, s_b = _retention_dir(fl(q), fl(k), fl(v), log_g[1], s0b.astype(f32))
    return o_f + fl(o_b), s_f, s_b


def _gla_dir(q, k, v, log_a, s0):
    B, H, T, dk = q.shape
    dv = v.shape[-1]
    C = GLA_CHUNK
    N = T // C
    qc = q.reshape(B, H, N, C, dk)
    kc = k.reshape(B, H, N, C, dk)
    vc = v.reshape(B, H, N, C, dv)
    b = jnp.cumsum(log_a.reshape(B, H, N, C, dk), axis=3)
    causal = jnp.tril(jnp.ones((C, C), dtype=bool))
    rel = jnp.where(causal[:, :, None], b[:, :, :, :, None, :] - b[:, :, :, None, :, :], -jnp.inf)
    attn = jnp.einsum('bhnid,bhnjd,bhnijd->bhnij', qc, kc, jnp.exp(rel))
    o_intra = jnp.einsum('bhnij,bhnje->bhnie', attn, vc)
    b_last = b[:, :, :, -1:, :]
    u = jnp.einsum('bhnjd,bhnje->bhnde', kc * jnp.exp(b_last - b), vc)
    decay = jnp.exp(b_last[:, :, :, 0, :])

    def step(s, xs):
        d, uu = xs
        return d[..., None] * s + uu, s

    s_fin, s_prev = lax.scan(step, s0, (jnp.moveaxis(decay, 2, 0), jnp.moveaxis(u, 2, 0)))
    o_inter = jnp.einsum('bhnid,nbhde->bhnie', qc * jnp.exp(b), s_prev)
    return (o_intra + o_inter).reshape(B, H, T, dv), s_fin


def _gla_log_gates(glr, w2, b2):
    B, T, _ = glr.shape
    z = jnp.einsum('btdr,drk->dbtk', glr.reshape(B, T, 2, GLA_RANK), w2) + b2[:, None, None, :]
    log_a = jax.nn.log_sigmoid(z.astype(jnp.float32)) / GLA_TAU
    return log_a.reshape(2, B, T, H_GLA, DK_GLA).transpose(0, 1, 3, 2, 4)


def _bidir_gla(q, k, v, log_a, s0f, s0b):
    f32 = jnp.float32
    q, k, v = q.astype(f32), k.astype(f32), v.astype(f32)
    o_f, s_f = _gla_dir(q, k, v, log_a[0], s0f.astype(f32))
    fl = lambda t: jnp.flip(t, axis=2)
    o_b, s_b = _gla_dir(fl(q), fl(k), fl(v), fl(log_a[1]), s0b.astype(f32))
    return o_f + fl(o_b), s_f, s_b


def _diff_lambda(p, lam_init):
    p = p.astype(jnp.float32)
    return jnp.exp(jnp.sum(p[0] * p[1])) - jnp.exp(jnp.sum(p[2] * p[3])) + lam_init


def _diff_core(q, k, v, lam):
    s = jnp.einsum('bhmqd,bhmkd->bhmqk', q, k).astype(jnp.float32) * DH_DIFF ** -0.5
    p = jax.nn.softmax(s, axis=-1)
    w = p[:, :, 0] - lam * p[:, :, 1]
    return jnp.einsum('bhqk,bhkd->bhqd', w.astype(v.dtype), v)


def _sink_attention_ctx(q, k, v, sink):
    B, KV, G, L, dh = q.shape
    s = jnp.einsum('bkgqd,bkld->bkgql', q, k).astype(jnp.float32) * dh ** -0.5
    p = _sink_softmax(s, sink.astype(jnp.float32).reshape(KV, G)[None, :, :, None, None])
    o = jnp.einsum('bkgql,bkld->bkgqd', p.astype(v.dtype), v)
    return o.reshape(B, KV * G, L, dh)


def _banded_sink_attention(q, k, v, k_ctx, v_ctx, sink):
    B, KV, G, T, dh = q.shape
    W = SWA_BLOCK
    NB = T // W
    L = k_ctx.shape[2]

    def band(t):
        tp = jnp.pad(t, ((0, 0), (0, 0), (W, W), (0, 0))).reshape(B, KV, NB + 2, W, dh)
        return jnp.concatenate([tp[:, :, :-2], tp[:, :, 1:-1], tp[:, :, 2:]], axis=3)

    kb, vb = band(k), band(v)
    qb = q.reshape(B, KV, G, NB, W, dh)
    scale = dh ** -0.5
    s_loc = jnp.einsum('bkgnqd,bknjd->bkgnqj', qb, kb).astype(jnp.float32) * scale
    s_ctx = jnp.einsum('bkgnqd,bkld->bkgnql', qb, k_ctx).astype(jnp.float32) * scale
    blk = jnp.arange(NB)[:, None, None]
    qpos = blk * W + jnp.arange(W)[None, :, None]
    kpos = (blk - 1) * W + jnp.arange(3 * W)[None, None, :]
    allowed = (kpos >= 0) & (kpos < T) & (jnp.abs(qpos - kpos) <= WINDOW)
    s_loc = jnp.where(allowed, s_loc, -jnp.inf)
    p = _sink_softmax(jnp.concatenate([s_ctx, s_loc], axis=-1),
                      sink.astype(jnp.float32).reshape(KV, G)[None, :, :, None, None, None])
    p = p.astype(v.dtype)
    o = (jnp.einsum('bkgnql,bkld->bkgnqd', p[..., :L], v_ctx)
         + jnp.einsum('bkgnqj,bknjd->bkgnqd', p[..., L:], vb))
    return o.reshape(B, KV * G, T, dh)


def _even_ctx(h, w_in, w_out, log_g, lam, lam_init):
    B, L, _ = h.shape
    rq, rk, rv, rg, dq, dk, dv = _split(h @ w_in, EVEN_SPLITS)
    zero = jnp.zeros((B, H_RET, DK_RET, DV_RET), jnp.float32)
    o_r, s_f, s_b = _bidir_retention(_heads(rq, H_RET), _heads(rk, H_RET) * DK_RET ** -0.5,
                                     _heads(rv, H_RET), log_g, zero, zero)
    ret = _merge_heads(_head_norm(o_r)).astype(h.dtype) * jax.nn.silu(rg)
    q2, k2, v2 = _diff_heads(dq), _diff_heads(dk), _heads(dv, H_DIFF)
    o_d = _diff_core(q2, k2, v2, lam)
    diff = _merge_heads(_head_norm(o_d)).astype(h.dtype) * (1.0 - lam_init)
    out = jnp.concatenate([ret, diff], axis=-1) @ w_out
    return out, (jnp.stack([s_f, s_b], axis=1), k2, v2)


def _even_latent(h, rope, w_in, w_out, log_g, lam, lam_init, st_ret, ck, cv):
    B, T, _ = h.shape
    rq, rk, rv, rg, dq, dk, dv = _split(h @ w_in, EVEN_SPLITS)
    o_r, _, _ = _bidir_retention(_heads(rq, H_RET), _heads(rk, H_RET) * DK_RET ** -0.5,
                                 _heads(rv, H_RET), log_g, st_ret[:, 0], st_ret[:, 1])
    ret = _merge_heads(_head_norm(o_r)).astype(h.dtype) * jax.nn.silu(rg)
    q2 = _apply_axial_rope(_diff_heads(dq), rope)
    k_all = jnp.concatenate([ck, _apply_axial_rope(_diff_heads(dk), rope)], axis=3)
    v_all = jnp.concatenate([cv, _heads(dv, H_DIFF)], axis=2)
    qb = jnp.moveaxis(q2.reshape(B, H_DIFF, 2, T // Q_BLOCK, Q_BLOCK, DH_DIFF), 3, 0)
    ob = lax.map(lambda qq: _diff_core(qq, k_all, v_all, lam), qb)
    o_d = jnp.moveaxis(ob, 0, 2).reshape(B, H_DIFF, T, DV_DIFF)
    diff = _merge_heads(_head_norm(o_d)).astype(h.dtype) * (1.0 - lam_init)
    return jnp.concatenate([ret, diff], axis=-1) @ w_out, ()


def _odd_ctx(h, w_in, w_out, sink, w2, b2):
    B, L, _ = h.shape
    sq, sk, sv, gq, gk, gv, gr, glr = _split(h @ w_in, ODD_SPLITS)
    q = _heads(sq, H_SWA).reshape(B, KV_SWA, SWA_GROUP, L, DH_SWA)
    k, v = _heads(sk, KV_SWA), _heads(sv, KV_SWA)
    swa = _merge_heads(_sink_attention_ctx(q, k, v, sink))
    zero = jnp.zeros((B, H_GLA, DK_GLA, DV_GLA), jnp.float32)
    o_g, s_f, s_b = _bidir_gla(_heads(gq, H_GLA) * DK_GLA ** -0.5, _heads(gk, H_GLA), _heads(gv, H_GLA),
                               _gla_log_gates(glr, w2, b2), zero, zero)
    gla = _merge_heads(_head_norm(o_g)).astype(h.dtype) * jax.nn.silu(gr)
    out = jnp.concatenate([swa, gla], axis=-1) @ w_out
    return out, (k, v, jnp.stack([s_f, s_b], axis=1))


def _odd_latent(h, rope, w_in, w_out, sink, w2, b2, ck, cv, st_gla):
    B, T, _ = h.shape
    sq, sk, sv, gq, gk, gv, gr, glr = _split(h @ w_in, ODD_SPLITS)
    q = _apply_axial_rope(_heads(sq, H_SWA), rope).reshape(B, KV_SWA, SWA_GROUP, T, DH_SWA)
    k = _apply_axial_rope(_heads(sk, KV_SWA), rope)
    v = _heads(sv, KV_SWA)
    swa = _merge_heads(_banded_sink_attention(q, k, v, ck, cv, sink))
    o_g, _, _ = _bidir_gla(_heads(gq, H_GLA) * DK_GLA ** -0.5, _heads(gk, H_GLA), _heads(gv, H_GLA),
                           _gla_log_gates(glr, w2, b2), st_gla[:, 0], st_gla[:, 1])
    gla = _merge_heads(_head_norm(o_g)).astype(h.dtype) * jax.nn.silu(gr)
    return jnp.concatenate([swa, gla], axis=-1) @ w_out, ()


def _block(x, mod, mixer, g, b, w1, w2):
    sh1, sc1, gt1, sh2, sc2, gt2 = jnp.split(mod, 6, axis=-1)
    mix, aux = mixer(x * (1 + sc1) + sh1)
    x = _layer_norm(ALPHA * x + gt1 * mix, g[0], b[0])
    hf = x * (1 + sc2) + sh2
    ff = jnp.square(jax.nn.relu(hf @ w1)) @ w2
    x = _layer_norm(ALPHA * x + gt2 * ff, g[1], b[1])
    return x, aux


def setup_inputs(seed: int = 0) -> dict:
    key = jax.random.key(seed)
    ks = jax.random.split(key, 26)
    f32 = jnp.float32
    nrm = lambda k, shape, s: jax.random.normal(k, shape, f32) * s
    D, L = D_MODEL, PAST_LEN
    ret_base = jnp.asarray(np.log(-np.log1p(-(2.0 ** (-5.0 - np.arange(H_RET)))))).astype(f32)
    return {
        'x_prompt': nrm(ks[0], (BATCH, SEQ, D), 1.0),
        'x_sample': nrm(ks[1], (DEC_BATCH, DEC_SEQ, D), 1.0),
        'c': nrm(ks[2], (DEC_BATCH, D), 1.0),
        'state_ret': nrm(ks[3], (DEC_BATCH, N_EVEN, 2, H_RET, DK_RET, DV_RET), 0.5),
        'cache_diff_k': nrm(ks[4], (DEC_BATCH, N_EVEN, H_DIFF, 2, L, DH_DIFF), 1.0),
        'cache_diff_v': nrm(ks[5], (DEC_BATCH, N_EVEN, H_DIFF, L, DV_DIFF), 1.0),
        'cache_swa_k': nrm(ks[6], (DEC_BATCH, N_ODD, KV_SWA, L, DH_SWA), 1.0),
        'cache_swa_v': nrm(ks[7], (DEC_BATCH, N_ODD, KV_SWA, L, DH_SWA), 1.0),
        'state_gla': nrm(ks[8], (DEC_BATCH, N_ODD, 2, H_GLA, DK_GLA, DV_GLA), 0.5),
        'c_ctx': nrm(ks[9], (D,), 1.0),
        'w_mod': nrm(ks[10], (DEPTH, D, 6 * D), 0.5 * D ** -0.5),
        'b_mod': nrm(ks[11], (DEPTH, 6 * D), 0.02),
        'ln_g': 1.0 + nrm(ks[12], (DEPTH, 2, D), 0.02),
        'ln_b': nrm(ks[13], (DEPTH, 2, D), 0.02),
        'w_in_even': nrm(ks[14], (N_EVEN, D, EVEN_IN), D ** -0.5),
        'w_out_even': nrm(ks[15], (N_EVEN, EVEN_OUT, D), BETA * EVEN_OUT ** -0.5),
        'ret_decay': ret_base[None, None, :] + nrm(ks[16], (N_EVEN, 2, H_RET), 0.05),
        'diff_lam': nrm(ks[17], (N_EVEN, 4, DH_DIFF), 0.1),
        'w_in_odd': nrm(ks[18], (N_ODD, D, ODD_IN), D ** -0.5),
        'w_out_odd': nrm(ks[19], (N_ODD, ODD_OUT, D), BETA * ODD_OUT ** -0.5),
        'swa_sink': nrm(ks[20], (N_ODD, H_SWA), 0.5),
        'gla_w2': nrm(ks[21], (N_ODD, 2, GLA_RANK, H_GLA * DK_GLA), GLA_RANK ** -0.5),
        'gla_b': nrm(ks[22], (N_ODD, 2, H_GLA * DK_GLA), 0.1),
        'w_ff1': nrm(ks[23], (DEPTH, D, D_FF), D ** -0.5),
        'w_ff2': nrm(ks[24], (DEPTH, D_FF, D), BETA * D_FF ** -0.5),
    }


def reference(x_prompt, x_sample, c, state_ret, cache_diff_k, cache_diff_v, cache_swa_k, cache_swa_v,
              state_gla, c_ctx, w_mod, b_mod, ln_g, ln_b, w_in_even, w_out_even, ret_decay, diff_lam,
              w_in_odd, w_out_odd, swa_sink, gla_w2, gla_b, w_ff1, w_ff2):
    T = x_sample.shape[1]
    rows = T // GRID_W
    rope_diff = _axial_rope(rows, DH_DIFF)
    rope_swa = _axial_rope(rows, DH_SWA)
    mod_ctx = jnp.einsum('d,lde->le', jax.nn.silu(c_ctx), w_mod) + b_mod
    mod_lat = jnp.einsum('bd,lde->lbe', jax.nn.silu(c), w_mod) + b_mod[:, None, :]
    xp, xs = x_prompt, x_sample
    new_ret, new_dk, new_dv, new_sk, new_sv, new_gla = [], [], [], [], [], []
    for l in range(DEPTH):
        mc = mod_ctx[l][None, None, :]
        ml = mod_lat[l][:, None, :]
        if l % 2 == 0:
            e = l // 2
            log_g = -jnp.exp(ret_decay[e].astype(jnp.float32))
            lam_init = 0.8 - 0.6 * math.exp(-0.3 * l)
            lam = _diff_lambda(diff_lam[e], lam_init)
            ctx_mix = functools.partial(_even_ctx, w_in=w_in_even[e], w_out=w_out_even[e], log_g=log_g,
                                        lam=lam, lam_init=lam_init)
            xp, (s_r, k_c, v_c) = _block(xp, mc, ctx_mix, ln_g[l], ln_b[l], w_ff1[l], w_ff2[l])
            new_ret.append(s_r)
            new_dk.append(k_c)
            new_dv.append(v_c)
            lat_mix = functools.partial(_even_latent, rope=rope_diff, w_in=w_in_even[e], w_out=w_out_even[e],
                                        log_g=log_g, lam=lam, lam_init=lam_init, st_ret=state_ret[:, e],
                                        ck=cache_diff_k[:, e], cv=cache_diff_v[:, e])
            xs, _ = _block(xs, ml, lat_mix, ln_g[l], ln_b[l], w_ff1[l], w_ff2[l])
        else:
            o = l // 2
            ctx_mix = functools.partial(_odd_ctx, w_in=w_in_odd[o], w_out=w_out_odd[o], sink=swa_sink[o],
                                        w2=gla_w2[o], b2=gla_b[o])
            xp, (k_c, v_c, s_g) = _block(xp, mc, ctx_mix, ln_g[l], ln_b[l], w_ff1[l], w_ff2[l])
            new_sk.append(k_c)
            new_sv.append(v_c)
            new_gla.append(s_g)
            lat_mix = functools.partial(_odd_latent, rope=rope_swa, w_in=w_in_odd[o], w_out=w_out_odd[o],
                                        sink=swa_sink[o], w2=gla_w2[o], b2=gla_b[o], ck=cache_swa_k[:, o],
                                        cv=cache_swa_v[:, o], st_gla=state_gla[:, o])
            xs, _ = _block(xs, ml, lat_mix, ln_g[l], ln_b[l], w_ff1[l], w_ff2[l])
    state_ret_new = jnp.stack(new_ret, axis=1)
    cache_diff_k_new = jnp.stack(new_dk, axis=1)
    cache_diff_v_new = jnp.stack(new_dv, axis=1)
    cache_swa_k_new = jnp.stack(new_sk, axis=1)
    cache_swa_v_new = jnp.stack(new_sv, axis=1)
    state_gla_new = jnp.stack(new_gla, axis=1)
    return (xp, xs, state_ret_new, cache_diff_k_new, cache_diff_v_new, cache_swa_k_new, cache_swa_v_new, state_gla_new)
```

```python
import numpy as np
from contextlib import ExitStack
import concourse.bass as bass
import concourse.mybir as mybir
from concourse.bass_utils import run_bass_kernel_spmd

F32 = mybir.dt.float32
F32R = mybir.dt.float32r
AF = mybir.ActivationFunctionType
ALU = mybir.AluOpType


class Buf:
    __slots__ = ("name", "ap", "w", "r")

    def __init__(self, name, ap):
        self.name = name
        self.ap = ap
        self.w = []
        self.r = []


class Sched:
    ENG = ('pe', 'act', 'dve', 'pool', 'sp')
    NS = 8

    def __init__(self, nc):
        self.nc = nc
        self.es = ExitStack()
        self.streams = {e: [] for e in self.ENG}
        self.cnt = {e: 0 for e in self.ENG}
        self.seen = {e: {} for e in self.ENG}
        self.dma_n = {}
        self.sems = {}
        self.nbuf = 0

    def dram_in(self, name, shape):
        return Buf(name, self.nc.dram_tensor(name, list(shape), F32, kind="ExternalInput").ap())

    def dram_out(self, name, shape):
        return Buf(name, self.nc.dram_tensor(name, list(shape), F32, kind="ExternalOutput").ap())

    def dram_scratch(self, name, shape):
        return Buf(name, self.nc.dram_tensor(name, list(shape), F32, kind="Internal").ap())

    def sbuf(self, name, shape):
        t = self.es.enter_context(self.nc.sbuf_tensor(name, list(shape), F32))
        return Buf(name, t[:])

    def psum(self, name):
        t = self.es.enter_context(self.nc.psum_tensor(name, [128, 512], F32))
        return Buf(name, t[:])

    def view(self, name, ap, olds=()):
        b = Buf(name, ap)
        for o in olds:
            b.r += o.w + o.r
        return b

    def _op(self, eng, fn, reads, writes, dma_q=None):
        deps = {}

        def add(tok):
            k, v = tok
            if deps.get(k, 0) < v:
                deps[k] = v
        for b in reads:
            for t in b.w:
                add(t)
        for b in writes:
            for t in b.w:
                add(t)
            for t in b.r:
                add(t)
        if dma_q is not None:
            n = self.dma_n.get(dma_q, 0)
            self.dma_n[dma_q] = n + 1
            slot, k = n % self.NS, n // self.NS
            key = ('d', dma_q, slot)
            if k > 0:
                add((key, 16 * k))
            tok = (key, 16 * (k + 1))
            inc = 16
        else:
            self.cnt[eng] += 1
            tok = (eng, self.cnt[eng])
            inc = 1
        seen = self.seen[eng]
        waits = []
        for k, v in deps.items():
            if eng == 'pe' and k == 'pe':
                continue
            if seen.get(k, 0) >= v:
                continue
            seen[k] = v
            waits.append((k, v))
        self.streams[eng].append((waits, fn, tok[0], inc))
        for b in writes:
            b.w = [tok]
            b.r = []
        for b in reads:
            if any(b is x for x in writes):
                continue
            b.r = [t for t in b.r if t[0] != tok[0]] + [tok]
        return tok

    def mm(self, pb, out_ap, lb, lhsT, rb, rhs, start=True, stop=True, r=False):
        if r:
            lhsT = lhsT.bitcast(F32R)
            rhs = rhs.bitcast(F32R)
        self._op('pe', lambda e: e.matmul(out_ap, lhsT, rhs, start=start, stop=stop), [lb, rb], [pb])

    def transpose(self, pb, out_ap, ib, in_ap, idb, id_ap):
        self._op('pe', lambda e: e.transpose(out_ap, in_ap, id_ap), [ib, idb], [pb])

    def act(self, ob, out_ap, ib, in_ap, func, scale=None, bias=None, reads=()):
        kw = {}
        if scale is not None:
            kw['scale'] = scale
        if bias is not None:
            kw['bias'] = bias
        self._op('act', lambda e: e.activation(out_ap, in_ap, func, **kw), [ib] + list(reads), [ob])

    def dve(self, fn, reads, writes):
        self._op('dve', fn, list(reads), list(writes))

    def pool(self, fn, reads, writes):
        self._op('pool', fn, list(reads), list(writes))

    def dma(self, q, out_ap, in_ap, reads, writes, **kw):
        self._op(q, lambda e: e.dma_start(out=out_ap, in_=in_ap, **kw), list(reads), list(writes), dma_q=q)

    def finish(self):
        nc = self.nc
        fw = []
        for q, n in self.dma_n.items():
            for slot in range(min(n, self.NS)):
                cnt = (n - slot + self.NS - 1) // self.NS
                fw.append((('d', q, slot), 16 * cnt))
        self.streams['sp'].append((fw, None, None, 0))
        keys = set()
        for e in self.ENG:
            for waits, fn, key, inc in self.streams[e]:
                if key is not None:
                    keys.add(key)
        for i, k in enumerate(sorted(keys, key=str)):
            self.sems[k] = self.es.enter_context(nc.semaphore("sem%d" % i))
        with nc.Block() as block:
            def mk(ename):
                def body(e):
                    for waits, fn, key, inc in self.streams[ename]:
                        for k, v in waits:
                            e.wait_ge(self.sems[k], v)
                        if fn is not None:
                            fn(e).then_inc(self.sems[key], inc)
                return body
            block.tensor(mk('pe'))
            block.scalar(mk('act'))
            block.vector(mk('dve'))
            block.gpsimd(mk('pool'))
            block.sync(mk('sp'))
        self.es.close()

    def op(self, eng, name, reads, writes, **kw):
        self._op(eng, lambda e: getattr(e, name)(**kw), list(reads), list(writes))


class Rot:
    def __init__(self, items):
        self.items = list(items)
        self.i = 0

    def next(self):
        x = self.items[self.i % len(self.items)]
        self.i += 1
        return x


AX = mybir.AxisListType.X
D = 1024
LCTX = 256
G = 256
ALPHA = 8.0 ** 0.25
EPS = 1e-5


def host_consts(T):
    ident = np.eye(128, dtype=np.float32)
    onesd = np.full((128, 128), 1.0 / 1024.0, dtype=np.float32)
    j = np.arange(128)[:, None]
    i = np.arange(128)[None, :]
    trif = (j <= i).astype(np.float32)
    trib = (j >= i).astype(np.float32)
    tri2 = np.concatenate([np.tile(trif, (1, 2)), np.tile(trib, (1, 2))], axis=1)
    rows = T // 64
    row = np.repeat(np.arange(rows), 64).astype(np.float32)
    col = np.tile(np.arange(64), rows).astype(np.float32)
    half = 32
    freqs = (np.float32(10000.0) ** (-np.arange(0, half, 2, dtype=np.float32) / np.float32(half))).astype(np.float32)
    ar = (row[:, None] * freqs).astype(np.float32)
    ac = (col[:, None] * freqs).astype(np.float32)
    cosT = np.zeros((64, T), np.float32)
    sinT = np.zeros((64, T), np.float32)
    perm = np.zeros((64, 64), np.float32)
    for d in range(64):
        blk, e = d // 32, d % 32
        ang = ar if blk == 0 else ac
        f = e % 16
        cosT[d] = np.cos(ang[:, f])
        sinT[d] = (-np.sin(ang[:, f])) if e < 16 else np.sin(ang[:, f])
        partner = d + 16 if e < 16 else d - 16
        perm[partner, d] = 1.0
    cos2 = np.concatenate([cosT, cosT], axis=0)
    sin2 = np.concatenate([sinT, sinT], axis=0)
    perm2 = np.zeros((128, 128), np.float32)
    perm2[:64, :64] = perm
    perm2[64:, 64:] = perm
    return dict(ident=ident, onesd=onesd, tri2=tri2, cos2=cos2, sin2=sin2, perm2=perm2)


ODD_W = 2464


def odd_col_index():
    sq = list(range(0, 512))
    sk = list(range(512, 640))
    sv = list(range(640, 768))
    gq = list(range(768, 1024))
    gk = list(range(1024, 1280))
    gv = list(range(1280, 1792))
    gr = list(range(1792, 2304))
    glr = list(range(2304, 2336))
    skdup = sk[0:64] + sk[0:64] + sk[64:128] + sk[64:128]
    idx = sq + skdup + gq + gk + glr + sv + gv + gr
    assert len(idx) == ODD_W
    return np.array(idx)


import math


def build(T=2048, NL=4, AR=22528):
    nc = bass.Bass("TRN2", target_bir_lowering=False)
    S = Sched(nc)
    NGS = T // G
    NGRP = NGS + 2
    xs = S.dram_in("xs", [T, D])
    xp = S.dram_in("xp", [2, 256, D])
    cond = S.dram_in("cond", [2, D])
    st_ret = S.dram_in("st_ret", [2, 2, 4, 64, 128])
    cdk_i = S.dram_in("cdk_i", [2, 4, 2, 256, 64])
    cdv_i = S.dram_in("cdv_i", [2, 4, 256, 128])
    csk_i = S.dram_in("csk_i", [2, 2, 256, 64])
    csv_i = S.dram_in("csv_i", [2, 2, 256, 64])
    st_gla = S.dram_in("st_gla", [2, 2, 4, 64, 128])
    w_mod = S.dram_in("w_mod", [4, D, 6144])
    b_mod = S.dram_in("b_mod", [4, 6144])
    ln_g = S.dram_in("ln_g", [4, 2, D])
    ln_b = S.dram_in("ln_b", [4, 2, D])
    w_in_even = S.dram_in("w_in_even", [2, D, 3072])
    w_out_even = S.dram_in("w_out_even", [2, D, D])
    ret_decay = S.dram_in("ret_decay", [2, 2, 4])
    diff_lam = S.dram_in("diff_lam", [2, 256])
    w_in_odd = S.dram_in("w_in_odd", [2, D, ODD_W])
    w_out_odd = S.dram_in("w_out_odd", [2, D, D])
    swa_sink = S.dram_in("swa_sink", [2, 8])
    gla_w2 = S.dram_in("gla_w2", [2, 2, 16, 256])
    gla_b = S.dram_in("gla_b", [2, 2, 256])
    w_ff1 = S.dram_in("w_ff1", [4, D, 4096])
    w_ff2 = S.dram_in("w_ff2", [4, 4096, D])
    c_ident = S.dram_in("ident", [128, 128])
    c_onesd = S.dram_in("onesd", [128, 128])
    c_tri2 = S.dram_in("tri2", [128, 512])
    c_cos = S.dram_in("cos2", [128, T])
    c_sin = S.dram_in("sin2", [128, T])
    c_perm = S.dram_in("perm2", [128, 128])
    yp = S.dram_out("yp", [2, 256, D])
    ys = S.dram_out("ys", [T, D])
    o_ret = S.dram_out("o_ret", [2, 2, 2, 4, 64, 128])
    o_cdk = S.dram_out("o_cdk", [2, 2, 4, 2, 256, 64])
    o_cdv = S.dram_out("o_cdv", [2, 2, 4, 256, 128])
    o_csk = S.dram_out("o_csk", [2, 2, 2, 256, 64])
    o_csv = S.dram_out("o_csv", [2, 2, 2, 256, 64])
    o_gla = S.dram_out("o_gla", [2, 2, 2, 4, 64, 128])

    class St:
        pass
    streams = []
    for i in range(3):
        st = St()
        st.i = i
        st.sample = (i == 0)
        st.T = T if i == 0 else 256
        st.c = 0 if i == 0 else 1
        st.g0 = 0 if i == 0 else NGS + (i - 1)
        st.ng = st.T // G
        st.pi = i - 1
        st.koff = 256 if i == 0 else 0
        st.FA = S.dram_scratch("FA%d" % i, [12, 128, st.T])
        st.TA = S.dram_scratch("TA%d" % i, [st.T, 1536])
        st.MIX = S.dram_scratch("MIX%d" % i, [8, 128, st.T])
        st.FAb = [[Buf("FA", None) for _ in range(st.ng)] for _ in range(12)]
        st.TAb = [[Buf("TA", None) for _ in range(st.ng)] for _ in range(3)]
        st.MIXb = [[Buf("MIX", None) for _ in range(st.ng)] for _ in range(8)]
        streams.append(st)

    xT = [S.sbuf("xT%d" % g, [128, 8, G]) for g in range(NGRP)]
    WB = [S.sbuf("wb%d" % i, [128, 4096]) for i in range(2)]
    wrot = Rot(WB)
    arena = S.sbuf("arena", [128, AR])
    ident_t = S.sbuf("ident_t", [128, 128])
    onesd_t = S.sbuf("onesd_t", [128, 128])
    tri2_t = S.sbuf("tri2_t", [128, 512])
    perm_t = S.sbuf("perm_t", [128, 128])
    ones_t = S.sbuf("ones_t", [128, 128])
    eps_t = S.sbuf("eps_t", [128, 1])
    stg0 = S.sbuf("stg0", [128, 128])
    sc_t = S.sbuf("sc_t", [128, 8, 2])
    modts = [S.sbuf("modt%d" % i, [128, 2, 48]) for i in range(2)]
    bm_t = S.sbuf("bm_t", [128, 48])
    lng_t = S.sbuf("lng_t", [128, 64])
    lnb_t = S.sbuf("lnb_t", [128, 64])
    P = [S.psum("ps%d" % i) for i in range(8)]

    ar = dict(off=0, views=[], summ={})

    def new_phase():
        for v in ar['views']:
            for k, val in v.w + v.r:
                if ar['summ'].get(k, 0) < val:
                    ar['summ'][k] = val
        ar['views'] = []
        ar['off'] = 0

    def alloc(name, shape):
        n = 1
        for s_ in shape[1:]:
            n *= s_
        off = ar['off']
        ar['off'] += n
        assert ar['off'] <= AR, (name, ar['off'])
        ap = arena.ap[0:shape[0], off:off + n]
        if len(shape) == 3:
            ap = ap.rearrange("p (a b) -> p a b", a=shape[1])
        b = Buf(name, ap)
        b.r = list(ar['summ'].items())
        ar['views'].append(b)
        return b

    def arot(name, shape, n):
        return Rot([alloc("%s%d" % (name, i), shape) for i in range(n)])

    def wload(src_buf, src_ap, shape):
        wb = wrot.next()
        n = 1
        for s_ in shape[1:]:
            n *= s_
        ap = wb.ap[:, 0:n]
        if len(shape) == 3:
            ap = ap.rearrange("p (a b) -> p a b", a=shape[1])
        S.dma('sp', ap, src_ap, [src_buf], [wb])
        return wb, ap

    def tt(eng, ob, out, ab, a, bb, b, op):
        S.op(eng, 'tensor_tensor', [ab, bb], [ob], out=out, in0=a, in1=b, op=op)

    def ts(eng, ob, out, ib, in0, s1, s2=None, op0=ALU.mult, op1=None, reads=()):
        kw = dict(out=out, in0=in0, scalar1=s1, scalar2=s2, op0=op0)
        if op1 is not None:
            kw['op1'] = op1
        S.op(eng, 'tensor_scalar', [ib] + list(reads), [ob], **kw)

    def load_cols(dst, dst_ap, src, src_rows_ap, R, func=AF.Identity):
        S.dma('sp', stg0.ap[0:R, :], src_rows_ap, [src], [stg0])
        S.transpose(P[7], P[7].ap[:, 0:R], stg0, stg0.ap[0:R, :], ident_t, ident_t.ap[0:R, 0:R])
        S.act(dst, dst_ap, P[7], P[7].ap[:, 0:R], func)

    for tb, cb in ((ident_t, c_ident), (onesd_t, c_onesd), (tri2_t, c_tri2), (perm_t, c_perm)):
        S.dma('sp', tb.ap, cb.ap, [cb], [tb])
    S.op('dve', 'memset', [], [ones_t], ap=ones_t.ap, constant=1.0)
    S.op('dve', 'memset', [], [eps_t], ap=eps_t.ap, constant=EPS)
    for c in range(2):
        load_cols(sc_t, sc_t.ap[:, :, c], cond, cond.ap[c].rearrange("(k p) -> k p", p=128), 8, AF.Silu)
    load_cols(lng_t, lng_t.ap, ln_g, ln_g.ap.rearrange("l i (k p) -> (l i k) p", p=128), 64)
    load_cols(lnb_t, lnb_t.ap, ln_b, ln_b.ap.rearrange("l i (k p) -> (l i k) p", p=128), 64)

    def load_input():
        new_phase()
        xr = arot("xin", [128, 1024], 2)
        pr = Rot(P[0:6])
        for st in streams:
            for gi in range(st.ng):
                g = st.g0 + gi
                for t2 in range(2):
                    r0 = gi * G + t2 * 128
                    src = xs.ap[r0:r0 + 128, :] if st.sample else xp.ap[st.pi, r0:r0 + 128, :]
                    sb = xs if st.sample else xp
                    xt_ = xr.next()
                    S.dma('sp', xt_.ap, src, [sb], [xt_])
                    for hf in range(2):
                        ps = pr.next()
                        for c4 in range(4):
                            cc = hf * 4 + c4
                            S.transpose(ps, ps.ap[:, c4 * 128:(c4 + 1) * 128], xt_, xt_.ap[:, cc * 128:(cc + 1) * 128],
                                        ident_t, ident_t.ap)
                        S.act(xT[g], xT[g].ap[:, hf * 4:(hf + 1) * 4, t2 * 128:(t2 + 1) * 128], ps,
                              ps.ap.rearrange("p (c t) -> p c t", c=4), AF.Identity)

    def store_output():
        new_phase()
        orot = arot("xout", [128, 1024], 2)
        pr = Rot(P[0:6])
        for st in streams:
            for gi in range(st.ng):
                g = st.g0 + gi
                for t2 in range(2):
                    r0 = gi * G + t2 * 128
                    ot_ = orot.next()
                    for hf in range(2):
                        ps = pr.next()
                        for c4 in range(4):
                            cc = hf * 4 + c4
                            S.transpose(ps, ps.ap[:, c4 * 128:(c4 + 1) * 128], xT[g],
                                        xT[g].ap[:, cc, t2 * 128:(t2 + 1) * 128], ident_t, ident_t.ap)
                        S.act(ot_, ot_.ap[:, hf * 512:(hf + 1) * 512], ps, ps.ap, AF.Identity)
                    if st.sample:
                        S.dma('pool', ys.ap[r0:r0 + 128, :], ot_.ap, [ot_], [ys])
                    else:
                        S.dma('pool', yp.ap[st.pi, r0:r0 + 128, :], ot_.ap, [ot_], [yp])

    def compute_mod_gen(l, pm_):
        modt = modts[l % 2]
        S.dma('sp', stg0.ap[0:48, :], b_mod.ap[l].rearrange("(j p) -> j p", p=128), [b_mod], [stg0])
        S.transpose(pm_, pm_.ap[:, 0:48], stg0, stg0.ap[0:48, :], ident_t, ident_t.ap[0:48, 0:48])
        S.act(bm_t, bm_t.ap, pm_, pm_.ap[:, 0:48], AF.Identity)
        yield
        for blk in range(12):
            wb, wap = wload(w_mod, w_mod.ap[l, :, blk * 512:(blk + 1) * 512].rearrange("(k p) n -> p k n", p=128),
                            [128, 8, 512])
            for e4 in range(4):
                ec = blk * 4 + e4
                for kc in range(8):
                    S.mm(pm_, pm_.ap[:, 2 * ec:2 * ec + 2], wb, wap[:, kc, e4 * 128:(e4 + 1) * 128], sc_t,
                         sc_t.ap[:, kc, :], start=(kc == 0), stop=(kc == 7))
                yield
        pmv = pm_.ap[:, 0:96].rearrange("p (e c) -> p e c", c=2)
        for c in range(2):
            tt('dve', modt, modt.ap[:, c, :], pm_, pmv[:, :, c], bm_t, bm_t.ap, ALU.add)
        for lo in (8, 32):
            ts('dve', modt, modt.ap[:, :, lo:lo + 8], modt, modt.ap[:, :, lo:lo + 8], 1.0, None, ALU.add)

    def phaseA(l, st):
        modt = modts[l % 2]
        even = (l % 2 == 0)
        li = l // 2
        new_phase()
        hTs = [alloc("hT%d" % i, [128, 8, G]) for i in range(2)]
        sfm = arot("sfm", [128, G], 4)
        stm = arot("stm", [128, 512], 3)
        cosg = arot("cosg", [128, G], 2)
        sing = arot("sing", [128, G], 2)
        t1r = arot("t1r", [128, G], 2)
        t2r = arot("t2r", [128, G], 2)
        pr = Rot(P[0:6])
        pp = Rot(P[6:8])
        c = st.c
        W = w_in_even if even else w_in_odd
        if even:
            blocks = [
                (0, 512, [('fm', 0, 128, 0, 1.0, False), ('fm', 128, 128, 1, 1.0, False),
                          ('fm', 256, 128, 2, 0.125, False), ('fm', 384, 128, 3, 0.125, False)]),
                (512, 512, [('tm', 0, 512, 0, 0, AF.Identity, None)]),
                (1024, 512, [('tm', 0, 512, 1, 0, AF.Silu, None)]),
                (1536, 512, [('fm', h * 128, 128, 4 + h, 1.0, True) for h in range(4)]),
                (2048, 512, [('fm', h * 128, 128, 8 + h, 1.0, True) for h in range(4)] +
                 ([] if st.sample else [('tm', 0, 512, None, 0, AF.Identity, 'cdk')])),
                (2560, 512, [('tm', 0, 512, 2, 0, AF.Identity, None if st.sample else 'cdv')]),
            ]
        else:
            blocks = [
                (0, 512, [('fm', p_ * 128, 128, p_, 1.0, True) for p_ in range(4)]),
                (512, 512, [('fm', 0, 128, 4, 1.0, True), ('fm', 128, 128, 5, 1.0, True),
                            ('fm', 256, 128, 6, 0.125, False), ('fm', 384, 128, 7, 0.125, False)] +
                 ([] if st.sample else [('tmk',)])),
                (1024, 416, [('fm', 0, 128, 8, 1.0, False), ('fm', 128, 128, 9, 1.0, False),
                             ('fm', 256, 16, 10, 1.0, False), ('fm', 272, 16, 11, 1.0, False),
                             ('tm', 288, 128, 2, 0, AF.Identity, None if st.sample else 'csv')]),
                (1440, 512, [('tm', 0, 512, 0, 0, AF.Identity, None)]),
                (1952, 512, [('tm', 0, 512, 1, 0, AF.Silu, None)]),
            ]
        pend_rope = [None]

        def flush_rope():
            if pend_rope[0] is not None:
                f_ = pend_rope[0]
                pend_rope[0] = None
                f_()
        if l == 0:
            xr = arot("xin", [128, 1024], 2)

        def load_group(gi_):
            g_ = st.g0 + gi_
            for t2 in range(2):
                r0 = gi_ * G + t2 * 128
                src = xs.ap[r0:r0 + 128, :] if st.sample else xp.ap[st.pi, r0:r0 + 128, :]
                sb = xs if st.sample else xp
                xt_ = xr.next()
                S.dma('sp', xt_.ap, src, [sb], [xt_])
                for hf in range(2):
                    ps = pr.next()
                    for c4 in range(4):
                        cc = hf * 4 + c4
                        S.transpose(ps, ps.ap[:, c4 * 128:(c4 + 1) * 128], xt_, xt_.ap[:, cc * 128:(cc + 1) * 128],
                                    ident_t, ident_t.ap)
                    S.act(xT[g_], xT[g_].ap[:, hf * 4:(hf + 1) * 4, t2 * 128:(t2 + 1) * 128], ps,
                          ps.ap.rearrange("p (c t) -> p c t", c=4), AF.Identity)
        if l == 0:
            load_group(0)
        for gi in range(st.ng):
            g = st.g0 + gi
            t0 = gi * G
            h = hTs[gi % 2]
            for kc in range(8):
                S.act(h, h.ap[:, kc, :], xT[g], xT[g].ap[:, kc, :], AF.Identity,
                      scale=modt.ap[:, c, 8 + kc:9 + kc], bias=modt.ap[:, c, kc:kc + 1], reads=[modt])
            if l == 0 and gi + 1 < st.ng:
                load_group(gi + 1)
            if st.sample:
                cg = cosg.next()
                sg_ = sing.next()
                S.dma('sp', cg.ap, c_cos.ap[:, t0:t0 + G], [c_cos], [cg])
                S.dma('sp', sg_.ap, c_sin.ap[:, t0:t0 + G], [c_sin], [sg_])
            for (wc0, wn, jobs) in blocks:
                wb, wap = wload(W, W.ap[li, :, wc0:wc0 + wn].rearrange("(k p) n -> p k n", p=128), [128, 8, wn])
                for job in jobs:
                    if job[0] == 'fm':
                        _, c0, M, slot, scale, rope = job
                        ps = pr.next()
                        for kc in range(8):
                            S.mm(ps, ps.ap[0:M, 0:G], wb, wap[:, kc, c0:c0 + M], h, h.ap[:, kc, :],
                                 start=(kc == 0), stop=(kc == 7))
                        flush_rope()
                        sg = sfm.next()
                        S.act(sg, sg.ap[0:M, :], ps, ps.ap[0:M, 0:G], AF.Identity, scale=float(scale))
                        if rope and st.sample:
                            def rope_tail(sg=sg, slot=slot, M=M, cg=cg, sg_=sg_, gi=gi, t0=t0):
                                pq = pp.next()
                                S.mm(pq, pq.ap[:, 0:G], perm_t, perm_t.ap, sg, sg.ap)
                                t1 = t1r.next()
                                t2 = t2r.next()
                                tt('dve', t1, t1.ap, sg, sg.ap, cg, cg.ap, ALU.mult)
                                tt('dve', t2, t2.ap, pq, pq.ap[:, 0:G], sg_, sg_.ap, ALU.mult)
                                sg2 = sfm.next()
                                tt('pool', sg2, sg2.ap, t1, t1.ap, t2, t2.ap, ALU.add)
                                S.dma('pool', st.FA.ap[slot, 0:M, t0:t0 + G], sg2.ap[0:M, :], [sg2], [st.FAb[slot][gi]])
                            pend_rope[0] = rope_tail
                        else:
                            S.dma('pool', st.FA.ap[slot, 0:M, t0:t0 + G], sg.ap[0:M, :], [sg], [st.FAb[slot][gi]])
                    elif job[0] == 'tm':
                        _, c0, N, tblk, _, func, outk = job
                        for t2_ in range(2):
                            ps = pr.next()
                            for kc in range(8):
                                S.mm(ps, ps.ap[:, 0:N], h, h.ap[:, kc, t2_ * 128:(t2_ + 1) * 128], wb,
                                     wap[:, kc, c0:c0 + N], start=(kc == 0), stop=(kc == 7))
                            flush_rope()
                            sg = stm.next()
                            S.act(sg, sg.ap[:, 0:N], ps, ps.ap[:, 0:N], func)
                            r0 = t0 + t2_ * 128
                            if tblk is not None:
                                dc = {0: 0, 1: 512, 2: 1024}[tblk]
                                S.dma('pool', st.TA.ap[r0:r0 + 128, dc:dc + N], sg.ap[:, 0:N], [sg],
                                      [st.TAb[tblk][gi]])
                            if outk == 'cdk':
                                S.dma('pool', o_cdk.ap[st.pi, li].rearrange("h m t d -> t (h m) d")[r0:r0 + 128, :, :],
                                      sg.ap[:, 0:512].rearrange("t (a d) -> t a d", d=64), [sg], [o_cdk])
                            elif outk == 'cdv':
                                S.dma('pool', o_cdv.ap[st.pi, li].rearrange("h t d -> t h d")[r0:r0 + 128, :, :],
                                      sg.ap[:, 0:512].rearrange("t (a d) -> t a d", d=128), [sg], [o_cdv])
                            elif outk == 'csv':
                                S.dma('pool', o_csv.ap[st.pi, li].rearrange("k t d -> t k d")[r0:r0 + 128, :, :],
                                      sg.ap[:, 0:128].rearrange("t (a d) -> t a d", d=64), [sg], [o_csv])
                    else:
                        for t2_ in range(2):
                            ps = pr.next()
                            for kv in range(2):
                                for kc in range(8):
                                    S.mm(ps, ps.ap[:, kv * 64:(kv + 1) * 64], h, h.ap[:, kc, t2_ * 128:(t2_ + 1) * 128],
                                         wb, wap[:, kc, kv * 128:kv * 128 + 64], start=(kc == 0), stop=(kc == 7))
                            sg = stm.next()
                            S.act(sg, sg.ap[:, 0:128], ps, ps.ap[:, 0:128], AF.Identity)
                            r0 = t0 + t2_ * 128
                            S.dma('pool', o_csk.ap[st.pi, li].rearrange("k t d -> t k d")[r0:r0 + 128, :, :],
                                  sg.ap[:, 0:128].rearrange("t (a d) -> t a d", d=64), [sg], [o_csk])

        flush_rope()

    def head_norm(o_t, o_n, stt, stt2, h):
        hs = slice(h * 128, (h + 1) * 128)
        S.op('dve', 'bn_stats', [o_t], [stt], out=stt.ap[:, h * 8:h * 8 + 6], in_=o_t.ap[:, hs])
        S.op('dve', 'bn_aggr', [stt], [stt], out=stt.ap[:, h * 8 + 6:h * 8 + 8], in_=stt.ap[:, h * 8:h * 8 + 6])
        S.act(stt2, stt2.ap[:, h:h + 1], stt, stt.ap[:, h * 8 + 7:h * 8 + 8], AF.Sqrt, bias=eps_t.ap[:, 0:1],
              reads=[eps_t])
        S.op('dve', 'reciprocal', [stt2], [stt2], out=stt2.ap[:, 2 + h:3 + h], in_=stt2.ap[:, h:h + 1])
        ts('dve', o_n, o_n.ap[:, hs], o_t, o_t.ap[:, hs], stt.ap[:, h * 8 + 6:h * 8 + 7], stt2.ap[:, 2 + h:3 + h],
           ALU.subtract, ALU.mult, reads=[stt, stt2])

    def phaseB_lin(l, st, kind):
        li = l // 2
        T_ = st.T
        nch = T_ // 128
        ret = (kind == 'ret')
        qslot, kslot = (0, 2) if ret else (6, 8)
        mix0 = 0 if ret else 4
        st_in = st_ret if ret else st_gla
        o_st = o_ret if ret else o_gla
        sc = 1.0 if ret else 1.0 / 16.0
        lin_tiles = {}
        for pc in range(2):
            if pc == 0:
                lin_tiles['Of'] = [alloc("Of%d" % n_, [128, 256]) for n_ in range(nch)]
                lin_tiles['Sst'] = alloc("Sst", [128, 128])
                lin_tiles['rots'] = dict(
                    QTc=arot("QTc", [128, 128], 3), KTc=arot("KTc", [128, 128], 3), Vc=arot("Vc", [128, 256], 2),
                    Gc=arot("Gc", [128, 256], 2), EQr=arot("EQ", [128, 128], 2), EKr=arot("EK", [128, 128], 2),
                    qzr=[arot("qz%d" % h_, [128, 128], 2) for h_ in range(2)], kdr=arot("kd", [128, 128], 2),
                    ktr=arot("kt", [128, 128], 2), WTr=arot("WT", [128, 256], 2), Usr=arot("Us", [128, 256], 2),
                    otr=arot("ot", [128, 256], 2), onr=arot("on", [128, 256], 2), mfr=arot("mf", [128, 128], 2),
                    lapr=arot("lap", [128, 128], 2), e1r=arot("e1", [128, 128], 2), glr_r=arot("glrc", [17, 128], 2),
                    stt=alloc("stt", [128, 16]), stt2=alloc("stt2", [128, 4]), rd=alloc("rd", [128, 4]),
                    w2a=[alloc("w2a%d" % d, [17, 128]) for d in range(2)])
            Of = lin_tiles['Of']
            Sst = lin_tiles['Sst']
            R_ = lin_tiles['rots']
            QTc, KTc, Vc = R_['QTc'], R_['KTc'], R_['Vc']
            Gc, EQr, EKr, qzr, kdr, ktr = R_['Gc'], R_['EQr'], R_['EKr'], R_['qzr'], R_['kdr'], R_['ktr']
            WTr, Usr, otr, onr, mfr = R_['WTr'], R_['Usr'], R_['otr'], R_['onr'], R_['mfr']
            lapr, e1r, glr_r, stt, stt2, rd, w2a = R_['lapr'], R_['e1r'], R_['glr_r'], R_['stt'], R_['stt2'], R_['rd'], R_['w2a']
            if pc == 0:
                for h_ in range(2):
                    for b_ in qzr[h_].items:
                        S.op('dve', 'memset', [], [b_], ap=b_.ap, constant=0.0)
            pA = Rot([P[6]])
            pO = Rot([P[7]])
            if not ret:
                if pc == 0:
                    for b_ in glr_r.items:
                        S.op('dve', 'memset', [], [b_], ap=b_.ap, constant=1.0)
                for d in range(2):
                    S.dma('sp', w2a[d].ap[0:16, :], gla_w2.ap[li, d, :, pc * 128:(pc + 1) * 128], [gla_w2], [w2a[d]])
                    S.dma('sp', w2a[d].ap[16:17, :], gla_b.ap[li, d:d + 1, pc * 128:(pc + 1) * 128], [gla_b], [w2a[d]])
            else:
                for d in range(2):
                    S.dma('sp', rd.ap[:, d * 2:d * 2 + 2], ret_decay.ap[li, d, 2 * pc:2 * pc + 2].partition_broadcast(128),
                          [ret_decay], [rd])
                S.act(rd, rd.ap, rd, rd.ap, AF.Exp)
            for d in range(2):
                tri_d = tri2_t.ap[:, d * 256:d * 256 + 128]
                tri_d2 = tri2_t.ap[:, d * 256:(d + 1) * 256]
                deccol = 127 if d == 0 else 0
                if st.sample:
                    for h in range(2):
                        S.dma('sp', Sst.ap[h * 64:(h + 1) * 64, :], st_in.ap[li, d, 2 * pc + h], [st_in], [Sst])
                else:
                    S.op('dve', 'memset', [], [Sst], ap=Sst.ap, constant=0.0)

                def decay_tiles(lap):
                    pb_ = pA.next()
                    S.mm(pb_, pb_.ap[:, 0:128], lap, lap.ap, tri2_t, tri_d)
                    EQ = EQr.next()
                    EK = EKr.next()
                    S.act(EQ, EQ.ap, pb_, pb_.ap[:, 0:128], AF.Exp, scale=-sc)
                    S.act(EK, EK.ap, pb_, pb_.ap[:, 0:128], AF.Exp, scale=sc)
                    return EQ, EK
                const_dec = None
                if ret:
                    lap = lapr.next()
                    for h in range(2):
                        ts('dve', lap, lap.ap[:, h * 64:(h + 1) * 64], ones_t, ones_t.ap[:, 0:64],
                           rd.ap[:, d * 2 + h:d * 2 + h + 1], None, ALU.mult, reads=[rd])
                    const_dec = decay_tiles(lap)

                def front(n, out):
                    cs = slice(n * 128, (n + 1) * 128)
                    gi = n // 2
                    qc = QTc.next()
                    kc_ = KTc.next()
                    vc = Vc.next()
                    S.dma('sp', qc.ap, st.FA.ap[qslot + pc, :, cs], [st.FAb[qslot + pc][gi]], [qc])
                    S.dma('sp', kc_.ap, st.FA.ap[kslot + pc, :, cs], [st.FAb[kslot + pc][gi]], [kc_])
                    S.dma('sp', vc.ap, st.TA.ap[cs, pc * 256:(pc + 1) * 256], [st.TAb[0][gi]], [vc])
                    if not ret:
                        gl = glr_r.next()
                        S.dma('sp', gl.ap[0:16, :], st.FA.ap[10 + d, 0:16, cs], [st.FAb[10 + d][gi]], [gl])
                        yield
                        pz = pA.next()
                        S.mm(pz, pz.ap[:, 0:128], gl, gl.ap[0:17, :], w2a[d], w2a[d].ap[0:17, :])
                        yield
                        e1 = e1r.next()
                        S.act(e1, e1.ap, pz, pz.ap[:, 0:128], AF.Exp, scale=-1.0)
                        yield
                        lap_ = lapr.next()
                        S.act(lap_, lap_.ap, e1, e1.ap, AF.Ln, bias=ones_t.ap[:, 0:1], reads=[ones_t])
                        yield
                        EQ, EK = decay_tiles(lap_)
                    else:
                        EQ, EK = const_dec
                    yield
                    kd = kdr.next()
                    qz = [qzr[0].next(), qzr[1].next()]
                    for h in range(2):
                        rs = slice(h * 64, (h + 1) * 64)
                        tt('dve', qz[h], qz[h].ap[rs, :], qc, qc.ap[rs, :], EQ, EQ.ap[rs, :], ALU.mult)
                    tt('pool', kd, kd.ap, kc_, kc_.ap, EK, EK.ap, ALU.mult)
                    yield
                    pk = pA.next()
                    S.transpose(pk, pk.ap[:, 0:128], kd, kd.ap, ident_t, ident_t.ap)
                    yield
                    kt = ktr.next()
                    S.act(kt, kt.ap, pk, pk.ap[:, 0:128], AF.Identity)
                    yield
                    pa = pA.next()
                    for h in range(2):
                        S.mm(pa, pa.ap[:, h * 128:(h + 1) * 128], kd, kd.ap, qz[h], qz[h].ap)
                    yield
                    wt = WTr.next()
                    tt('dve', wt, wt.ap, pa, pa.ap[:, 0:256], tri2_t, tri_d2, ALU.mult)
                    yield
                    pu = pA.next()
                    S.mm(pu, pu.ap[:, 0:256], kt, kt.ap, vc, vc.ap)
                    yield
                    us = Usr.next()
                    S.act(us, us.ap, pu, pu.ap[:, 0:256], AF.Identity)
                    out.update(dict(EQ=EQ, qz=qz, wt=wt, vc=vc, us=us))

                def back(n, tl, first_pass):
                    cs = slice(n * 128, (n + 1) * 128)
                    gi = n // 2
                    EQ, qz, wt, vc, us = tl['EQ'], tl['qz'], tl['wt'], tl['vc'], tl['us']
                    po = pO.next()
                    for h in range(2):
                        hs = slice(h * 128, (h + 1) * 128)
                        S.mm(po, po.ap[:, hs], wt, wt.ap[:, hs], vc, vc.ap[:, hs], start=True, stop=False)
                        S.mm(po, po.ap[:, hs], qz[h], qz[h].ap, Sst, Sst.ap, start=False, stop=True)
                    yield
                    for h in range(2):
                        rs = slice(h * 64, (h + 1) * 64)
                        tt('dve', Sst, Sst.ap[rs, :], Sst, Sst.ap[rs, :], us, us.ap[rs, h * 128:(h + 1) * 128], ALU.add)
                        ts('dve', Sst, Sst.ap[rs, :], Sst, Sst.ap[rs, :], EQ.ap[rs, deccol:deccol + 1], None, ALU.mult,
                           reads=[EQ])
                    yield
                    if first_pass:
                        S.act(Of[n], Of[n].ap, po, po.ap[:, 0:256], AF.Identity)
                    else:
                        o_t = otr.next()
                        o_n = onr.next()
                        tt('dve', o_t, o_t.ap, po, po.ap[:, 0:256], Of[n], Of[n].ap, ALU.add)
                        gc = Gc.next()
                        S.dma('sp', gc.ap, st.TA.ap[cs, 512 + pc * 256:512 + (pc + 1) * 256], [st.TAb[1][gi]], [gc])
                        yield
                        for h in range(2):
                            head_norm(o_t, o_n, stt, stt2, h)
                            yield
                        tt('pool', o_n, o_n.ap, o_n, o_n.ap, gc, gc.ap, ALU.mult)
                        yield
                        for h in range(2):
                            pt_ = pO.next()
                            S.transpose(pt_, pt_.ap[:, 0:128], o_n, o_n.ap[:, h * 128:(h + 1) * 128], ident_t, ident_t.ap)
                            mf = mfr.next()
                            S.act(mf, mf.ap, pt_, pt_.ap[:, 0:128], AF.Identity)
                            ch = mix0 + 2 * pc + h
                            S.dma('pool', st.MIX.ap[ch, :, cs], mf.ap, [mf], [st.MIXb[ch][gi]])
                            yield

                order = list(range(nch)) if d == 0 else list(range(nch - 1, -1, -1))
                tl = {}
                for _ in front(order[0], tl):
                    yield
                for i_, n in enumerate(order):
                    gb = back(n, tl, d == 0)
                    tl2 = {}
                    gf = front(order[i_ + 1], tl2) if i_ + 1 < nch else None
                    while gb is not None or gf is not None:
                        if gb is not None and next(gb, 'END') == 'END':
                            gb = None
                        if gf is not None and next(gf, 'END') == 'END':
                            gf = None
                        yield
                    tl = tl2
                if not st.sample:
                    for h in range(2):
                        S.dma('pool', o_st.ap[st.pi, li, d, 2 * pc + h], Sst.ap[h * 64:(h + 1) * 64, :], [Sst], [o_st])

    def run_attn(QT, q0, nqb, entries, KT, V, dv2, PTr, pS, acc):
        cnt = [0] * nqb
        tot = [0] * nqb
        for (kb, qlo, qhi, masks) in entries:
            for qb in range(qlo, qhi):
                tot[qb] += 1

        def pv(ent, pt):
            kb, qlo, qhi, masks = ent
            for qb in range(qlo, qhi):
                o_ = (qb - qlo) * 128
                S.mm(acc[qb], acc[qb].ap[:, 0:dv2], pt, pt.ap[:, o_:o_ + 128], V, V.ap[:, kb, :],
                     start=(cnt[qb] == 0), stop=(cnt[qb] == tot[qb] - 1))
                cnt[qb] += 1
        pend = None
        for ent in entries:
            kb, qlo, qhi, masks = ent
            ps = pS.next()
            w = (qhi - qlo) * 128
            S.mm(ps, ps.ap[:, 0:w], KT, KT.ap[:, kb * 128:(kb + 1) * 128], QT,
                 QT.ap[:, qlo * 128:qhi * 128])
            pt = PTr.next()
            S.act(pt, pt.ap[:, 0:w], ps, ps.ap[:, 0:w], AF.Exp, scale=0.125)
            for qb, mk in masks.items():
                o_ = (qb - qlo) * 128
                tt('pool', pt, pt.ap[:, o_:o_ + 128], pt, pt.ap[:, o_:o_ + 128], tri2_t, mk, ALU.mult)
            if pend is not None:
                pv(*pend)
            pend = (ent, pt)
            yield
        if pend is not None:
            pv(*pend)
        yield

    accsets = [P[0:2], P[2:4]]
    ucnt = [0]

    def phaseB_diff(l, st):
        li = l // 2
        lam_init = 0.8 - 0.6 * math.exp(-0.3 * l)
        T_ = st.T
        nq = T_ // 128
        koff = st.koff
        nkb = (koff + T_) // 128
        kb0 = koff // 128
        QZ = [[alloc("QZ%d_%d" % (m_, b_), [128, 512]) for b_ in range(2)] for m_ in range(2)]
        for m_ in range(2):
            for b_ in range(2):
                S.op('dve', 'memset', [], [QZ[m_][b_]], ap=QZ[m_][b_].ap, constant=0.0)
        KT = alloc("KT", [128, koff + T_])
        V = alloc("V", [128, nkb, 130])
        PTr = arot("PT", [128, 512], 3)
        o1n = alloc("o1n", [128, 4, 128])
        tr_ = arot("tq", [128, 128], 4)
        or_ = arot("oq", [128, 128], 2)
        onr = arot("onq", [128, 128], 2)
        stgr = arot("stgq", [128, 512], 2)
        kst = arot("kst", [128, 128], 2)
        sm = alloc("sm", [128, 8])
        stt = alloc("stt", [128, 16])
        stt2 = alloc("stt2", [128, 4])
        dl = alloc("dl", [128, 256])
        pr_ = alloc("prd", [128, 128])
        lam = alloc("lam", [128, 4])
        pS = Rot(P[4:6])
        S.dma('sp', dl.ap, diff_lam.ap[li].partition_broadcast(128), [diff_lam], [dl])
        dl4 = dl.ap.rearrange("p (a b d) -> p a b d", a=2, b=2)
        tt('dve', pr_, pr_.ap.rearrange("p (a d) -> p a d", a=2), dl, dl4[:, :, 0, :], dl, dl4[:, :, 1, :], ALU.mult)
        S.op('dve', 'tensor_reduce', [pr_], [lam], out=lam.ap[:, 0:2], in_=pr_.ap.rearrange("p (a d) -> p a d", a=2),
             axis=AX, op=ALU.add)
        S.act(lam, lam.ap[:, 0:2], lam, lam.ap[:, 0:2], AF.Exp)
        tt('dve', lam, lam.ap[:, 2:3], lam, lam.ap[:, 1:2], lam, lam.ap[:, 0:1], ALU.subtract)
        ts('dve', lam, lam.ap[:, 3:4], lam, lam.ap[:, 2:3], -lam_init, None, ALU.add)
        S.op('dve', 'memset', [], [V], ap=V.ap[:, :, 128:130], constant=1.0)
        for h in range(4):
            yield
            S.dma('sp', KT.ap[:, koff:koff + T_], st.FA.ap[8 + h], st.FAb[8 + h], [KT])
            S.dma('sp', V.ap[:, kb0:nkb, 0:128],
                  st.TA.ap[:, 1024 + h * 128:1024 + (h + 1) * 128].rearrange("(n t) d -> t n d", t=128),
                  st.TAb[2], [V])
            if st.sample:
                S.dma('sp', V.ap[:, 0:2, 0:128], cdv_i.ap[li, h].rearrange("(n t) d -> t n d", t=128), [cdv_i], [V])
                for lb in range(2):
                    ks = kst.next()
                    S.dma('sp', ks.ap.rearrange("t (m d) -> t m d", m=2),
                          cdk_i.ap[li, h, :, lb * 128:(lb + 1) * 128, :].rearrange("m t d -> t m d"), [cdk_i], [ks])
                    pb7 = pS.next()
                    S.transpose(pb7, pb7.ap[:, 0:128], ks, ks.ap, ident_t, ident_t.ap)
                    S.act(KT, KT.ap[:, lb * 128:(lb + 1) * 128], pb7, pb7.ap[:, 0:128], AF.Identity)
                    yield
            for qg in range((nq + 3) // 4):
                nqb = min(4, nq - 4 * qg)
                stg = stgr.next()
                c0 = qg * 512
                ggs = list(range(c0 // G, (c0 + nqb * 128 + G - 1) // G))
                for m_ in range(2):
                    qz_ = QZ[m_][qg % 2]
                    S.dma('sp', qz_.ap[m_ * 64:(m_ + 1) * 64, 0:nqb * 128],
                          st.FA.ap[4 + h, m_ * 64:(m_ + 1) * 64, c0:c0 + nqb * 128], [st.FAb[4 + h][gg] for gg in ggs], [qz_])
                for m in range(2):
                    acc = P[0:nqb]
                    entries = [(kb, 0, nqb, {}) for kb in range(nkb)]
                    yield from run_attn(QZ[m][qg % 2], 0, nqb, entries, KT, V, 130, PTr, pS, acc)
                    t_s = []
                    for qb in range(nqb):
                        a = acc[qb]
                        S.op('dve', 'reciprocal', [a], [sm], out=sm.ap[:, m * 4 + qb:m * 4 + qb + 1], in_=a.ap[:, 128:129])
                        if m == 0:
                            ts('dve', o1n, o1n.ap[:, qb, :], a, a.ap[:, 0:128], sm.ap[:, qb:qb + 1], None, ALU.mult,
                               reads=[sm])
                        else:
                            t_ = tr_.next()
                            ts('dve', t_, t_.ap, a, a.ap[:, 0:128], sm.ap[:, 4 + qb:5 + qb], lam.ap[:, 3:4], ALU.mult,
                               ALU.mult, reads=[sm, lam])
                            t_s.append(t_)
                    for qb in range(nqb):
                        if m == 1:
                            t_ = t_s[qb]
                            o_ = or_.next()
                            tt('dve', o_, o_.ap, t_, t_.ap, o1n, o1n.ap[:, qb, :], ALU.add)
                            on_ = onr.next()
                            head_norm(o_, on_, stt, stt2, 0)
                            pb7 = pS.next()
                            S.transpose(pb7, pb7.ap[:, 0:128], on_, on_.ap, ident_t, ident_t.ap)
                            S.act(stg, stg.ap[:, qb * 128:(qb + 1) * 128], pb7, pb7.ap[:, 0:128], AF.Identity,
                                  scale=float(1.0 - lam_init))
                            yield
                c0 = qg * 512
                S.dma('pool', st.MIX.ap[4 + h, :, c0:c0 + nqb * 128], stg.ap[:, 0:nqb * 128], [stg],
                      [st.MIXb[4 + h][gg] for gg in range(c0 // G, (c0 + nqb * 128 + G - 1) // G)])

    def phaseB_swa(l, st):
        li = l // 2
        T_ = st.T
        nq = T_ // 128
        koff = st.koff
        nkb = (koff + T_) // 128
        kb0 = koff // 128
        QZ = [[alloc("QZ%d_%d" % (m_, b_), [128, 512]) for b_ in range(2)] for m_ in range(2)]
        for m_ in range(2):
            for b_ in range(2):
                S.op('dve', 'memset', [], [QZ[m_][b_]], ap=QZ[m_][b_].ap, constant=0.0)
        KT = alloc("KT", [128, koff + T_])
        V = alloc("V", [128, nkb, 66])
        PTr = arot("PT", [128, 512], 3)
        mixtm = alloc("mixtm", [128, 4, 128])
        stgr = arot("stgq", [128, 512], 2)
        kst = arot("kst", [128, 128], 2)
        sm = alloc("sm", [128, 8])
        esink = alloc("esink", [128, 8])
        pS = Rot(P[4:6])
        S.dma('sp', esink.ap, swa_sink.ap[li].partition_broadcast(128), [swa_sink], [esink])
        S.act(esink, esink.ap, esink, esink.ap, AF.Exp)
        S.op('dve', 'memset', [], [V], ap=V.ap[:, :, 64:66], constant=1.0)
        trif = tri2_t.ap[:, 0:128]
        trib = tri2_t.ap[:, 256:384]
        for pc in range(4):
            kv = pc // 2
            if pc % 2 == 0:
                S.dma('sp', KT.ap[:, koff:koff + T_], st.FA.ap[4 + kv], st.FAb[4 + kv], [KT])
                S.dma('sp', V.ap[:, kb0:nkb, 0:64],
                      st.TA.ap[:, 1024 + kv * 64:1024 + (kv + 1) * 64].rearrange("(n t) d -> t n d", t=128),
                      st.TAb[2], [V])
                if st.sample:
                    S.dma('sp', V.ap[:, 0:2, 0:64], csv_i.ap[li, kv].rearrange("(n t) d -> t n d", t=128), [csv_i], [V])
                    for lb in range(2):
                        ks = kst.next()
                        for dup in range(2):
                            S.dma('sp', ks.ap[:, dup * 64:(dup + 1) * 64], csk_i.ap[li, kv, lb * 128:(lb + 1) * 128, :],
                                  [csk_i], [ks])
                        pb7 = pS.next()
                        S.transpose(pb7, pb7.ap[:, 0:128], ks, ks.ap, ident_t, ident_t.ap)
                        S.act(KT, KT.ap[:, lb * 128:(lb + 1) * 128], pb7, pb7.ap[:, 0:128], AF.Identity)
                        yield
            for qg in range((nq + 3) // 4):
                nqb = min(4, nq - 4 * qg)
                q0 = 4 * qg
                c0 = qg * 512
                ggs = list(range(c0 // G, (c0 + nqb * 128 + G - 1) // G))
                for m_ in range(2):
                    qz_ = QZ[m_][(pc * 4 + qg) % 2]
                    S.dma('sp', qz_.ap[m_ * 64:(m_ + 1) * 64, 0:nqb * 128],
                          st.FA.ap[pc, m_ * 64:(m_ + 1) * 64, c0:c0 + nqb * 128], [st.FAb[pc][gg] for gg in ggs], [qz_])
                for hh in range(2):
                    acc = P[0:nqb]
                    h = 2 * pc + hh
                    entries = []
                    if st.sample:
                        entries += [(kb, 0, nqb, {}) for kb in range(kb0)]
                        for j in range(q0 - 1, q0 + nqb + 1):
                            if j < 0 or j >= nq:
                                continue
                            qlo = max(j - 1, q0) - q0
                            qhi = min(j + 1, q0 + nqb - 1) - q0 + 1
                            masks = {}
                            if q0 <= j + 1 < q0 + nqb:
                                masks[j + 1 - q0] = trib
                            if q0 <= j - 1 < q0 + nqb:
                                masks[j - 1 - q0] = trif
                            entries.append((kb0 + j, qlo, qhi, masks))
                    else:
                        entries += [(kb, 0, nqb, {}) for kb in range(nkb)]
                    yield from run_attn(QZ[hh][(pc * 4 + qg) % 2], 0, nqb, entries, KT, V, 66, PTr, pS, acc)
                    for qb in range(nqb):
                        a = acc[qb]
                        tt('dve', sm, sm.ap[:, qb:qb + 1], a, a.ap[:, 64:65], esink, esink.ap[:, h:h + 1], ALU.add)
                        S.op('dve', 'reciprocal', [sm], [sm], out=sm.ap[:, 4 + qb:5 + qb], in_=sm.ap[:, qb:qb + 1])
                        ts('dve', mixtm, mixtm.ap[:, qb, hh * 64:(hh + 1) * 64], a, a.ap[:, 0:64], sm.ap[:, 4 + qb:5 + qb],
                           None, ALU.mult, reads=[sm])
                stg = stgr.next()
                for qb in range(nqb):
                    pb7 = pS.next()
                    S.transpose(pb7, pb7.ap[:, 0:128], mixtm, mixtm.ap[:, qb, :], ident_t, ident_t.ap)
                    S.act(stg, stg.ap[:, qb * 128:(qb + 1) * 128], pb7, pb7.ap[:, 0:128], AF.Identity)
                    yield
                c0 = qg * 512
                S.dma('pool', st.MIX.ap[pc, :, c0:c0 + nqb * 128], stg.ap[:, 0:nqb * 128], [stg],
                      [st.MIXb[pc][gg] for gg in range(c0 // G, (c0 + nqb * 128 + G - 1) // G)])

    def phaseC_layer(l):
        modt = modts[l % 2]
        li = l // 2
        Wout = w_out_even if l % 2 == 0 else w_out_odd
        new_phase()
        M8s = [alloc("M8_%d" % i, [128, 8, G]) for i in range(2)]
        zTs = [alloc("zT_%d" % i, [128, 8, G]) for i in range(2)]
        U = alloc("U", [128, 32, G])
        sq = alloc("sq", [128, 8, G])
        rstds = [alloc("rstd%d" % i, [128, G]) for i in range(2)]
        rr = arot("rr", [128, G], 2)
        pr = Rot(P[0:6])
        pm, pv = P[6], P[7]
        groups = [(st, gi) for st in streams for gi in range(st.ng)]
        n = len(groups)

        def xg(k):
            st, gi = groups[k]
            return xT[st.g0 + gi], st.c

        def partA(k):
            st, gi = groups[k]
            x, c = xg(k)
            t0 = gi * G
            M8 = M8s[k % 2]
            zT = zTs[0]
            S.dma('sp', M8.ap, st.MIX.ap[:, :, t0:t0 + G].rearrange("c p t -> p c t"),
                  [st.MIXb[cc][gi] for cc in range(8)], [M8])
            S.act(x, x.ap, x, x.ap, AF.Identity, scale=float(ALPHA))
            for ob in range(2):
                wb, wap = wload(Wout, Wout.ap[li, :, ob * 512:(ob + 1) * 512].rearrange("(k p) n -> p k n", p=128),
                                [128, 8, 512])
                for o4 in range(4):
                    oc = ob * 4 + o4
                    ps = pr.next()
                    for kc in range(8):
                        S.mm(ps, ps.ap[:, 0:G], wb, wap[:, kc, o4 * 128:(o4 + 1) * 128], M8, M8.ap[:, kc, :],
                             start=(kc == 0), stop=(kc == 7))
                    S.op('dve', 'scalar_tensor_tensor', [ps, modt, x], [zT], out=zT.ap[:, oc, :], in0=ps.ap[:, 0:G],
                         scalar=modt.ap[:, c, 16 + oc:17 + oc], in1=x.ap[:, oc, :], op0=ALU.mult, op1=ALU.add)

        def ln_gen(zT, rstd, dst, lc, post=None):
            for kc in range(8):
                S.mm(pm, pm.ap[:, 0:G], onesd_t, onesd_t.ap, zT, zT.ap[:, kc, :], start=(kc == 0), stop=(kc == 7))
            yield
            tt('dve', zT, zT.ap, zT, zT.ap, pm, pm.ap[:, 0:G].unsqueeze(1).broadcast_to([128, 8, G]), ALU.subtract)
            S.act(sq, sq.ap, zT, zT.ap, AF.Square)
            yield
            for kc in range(8):
                S.mm(pv, pv.ap[:, 0:G], onesd_t, onesd_t.ap, sq, sq.ap[:, kc, :], start=(kc == 0), stop=(kc == 7))
            yield
            S.act(rstd, rstd.ap, pv, pv.ap[:, 0:G], AF.Sqrt, bias=eps_t.ap[:, 0:1], reads=[eps_t])
            S.op('dve', 'reciprocal', [rstd], [rstd], out=rstd.ap, in_=rstd.ap)
            tt('dve', zT, zT.ap, zT, zT.ap, rstd, rstd.ap.unsqueeze(1).broadcast_to([128, 8, G]), ALU.mult)
            for kc in range(8):
                S.act(dst, dst.ap[:, kc, :], zT, zT.ap[:, kc, :], AF.Identity, scale=lng_t.ap[:, lc + kc:lc + kc + 1],
                      bias=lnb_t.ap[:, lc + kc:lc + kc + 1], reads=[lng_t, lnb_t])
            if post is not None:
                post()

        def step(gen):
            if gen is not None:
                next(gen, None)

        def finish(gen):
            if gen is not None:
                for _ in gen:
                    pass

        def make_ln1(k):
            x, c = xg(k)
            M8 = M8s[k % 2]

            def post():
                for kc in range(8):
                    S.act(M8, M8.ap[:, kc, :], x, x.ap[:, kc, :], AF.Identity, scale=modt.ap[:, c, 32 + kc:33 + kc],
                          bias=modt.ap[:, c, 24 + kc:25 + kc], reads=[modt])
            return ln_gen(zTs[0], rstds[0], x, (l * 2 + 0) * 8, post)

        def make_ln2(k):
            x, c = xg(k)
            return ln_gen(zTs[1], rstds[1], x, (l * 2 + 1) * 8)

        def ff1(k, filler):
            M8 = M8s[k % 2]
            for jb in range(8):
                wb, wap = wload(w_ff1, w_ff1.ap[l, :, jb * 512:(jb + 1) * 512].rearrange("(k p) n -> p k n", p=128),
                                [128, 8, 512])
                for jj in range(4):
                    j = jb * 4 + jj
                    ps = pr.next()
                    for kc in range(8):
                        S.mm(ps, ps.ap[:, 0:G], wb, wap[:, kc, jj * 128:(jj + 1) * 128], M8, M8.ap[:, kc, :],
                             start=(kc == 0), stop=(kc == 7))
                    r_ = rr.next()
                    S.act(r_, r_.ap, ps, ps.ap[:, 0:G], AF.Relu)
                    tt('pool', U, U.ap[:, j, :], r_, r_.ap, r_, r_.ap, ALU.mult)
                if jb % 2 == 0:
                    step(filler)

        def ff2(k, filler):
            x, c = xg(k)
            zT = zTs[1]
            for oc in range(8):
                wb, wap = wload(w_ff2, w_ff2.ap[l, :, oc * 128:(oc + 1) * 128].rearrange("(j p) n -> p j n", p=128),
                                [128, 32, 128])
                ps = pr.next()
                for j in range(32):
                    S.mm(ps, ps.ap[:, 0:G], wb, wap[:, j, :], U, U.ap[:, j, :], start=(j == 0), stop=(j == 31))
                S.op('dve', 'scalar_tensor_tensor', [ps, modt, x], [zT], out=zT.ap[:, oc, :], in0=ps.ap[:, 0:G],
                     scalar=modt.ap[:, c, 40 + oc:41 + oc], in1=x.ap[:, oc, :], op0=ALU.mult, op1=ALU.add)
                if oc % 2 == 0:
                    step(filler)

        last = (l == NL - 1)
        if last:
            orot = arot("xout", [128, 1024], 2)

        def store_group(k):
            st, gi = groups[k]
            x = xT[st.g0 + gi]
            for t2 in range(2):
                r0 = gi * G + t2 * 128
                ot_ = orot.next()
                for hf in range(2):
                    ps = pr.next()
                    for c4 in range(4):
                        cc = hf * 4 + c4
                        S.transpose(ps, ps.ap[:, c4 * 128:(c4 + 1) * 128], x, x.ap[:, cc, t2 * 128:(t2 + 1) * 128],
                                    ident_t, ident_t.ap)
                    S.act(ot_, ot_.ap[:, hf * 512:(hf + 1) * 512], ps, ps.ap, AF.Identity)
                if st.sample:
                    S.dma('pool', ys.ap[r0:r0 + 128, :], ot_.ap, [ot_], [ys])
                else:
                    S.dma('pool', yp.ap[st.pi, r0:r0 + 128, :], ot_.ap, [ot_], [yp])

        partA(0)
        finish(make_ln1(0))
        ln2_prev = None
        for k in range(n):
            ff1(k, ln2_prev)
            finish(ln2_prev)
            if last and k >= 1:
                store_group(k - 1)
            if k + 1 < n:
                partA(k + 1)
            x, c = xg(k)
            S.act(x, x.ap, x, x.ap, AF.Identity, scale=float(ALPHA))
            ln1_next = make_ln1(k + 1) if k + 1 < n else None
            ff2(k, ln1_next)
            finish(ln1_next)
            ln2_prev = make_ln2(k)
        finish(ln2_prev)
        if last:
            store_group(n - 1)

    import os
    stop = os.environ.get("KSTOP", "")
    if stop == "load":
        load_input()
    for l in range(NL):
        if stop == "load":
            break
        if l == 0:
            for _ in compute_mod_gen(0, P[6]):
                pass
        if stop == "mod":
            break
        for st in streams:
            phaseA(l, st)
        if stop == "A":
            break
        gm_ = compute_mod_gen(l + 1, P[3]) if l + 1 < NL else None
        for st in streams:
            new_phase()
            if l % 2 == 0:
                ga, gl_, ratio = phaseB_diff(l, st), phaseB_lin(l, st, 'ret'), 1
            else:
                ga, gl_, ratio = phaseB_swa(l, st), phaseB_lin(l, st, 'gla'), 3
            if os.environ.get("KSEQ"):
                for _ in ga:
                    pass
                ga = None
            while ga is not None or gl_ is not None:
                if ga is not None and next(ga, 'END') == 'END':
                    ga = None
                for _ in range(ratio):
                    if gl_ is not None and next(gl_, 'END') == 'END':
                        gl_ = None
                if gm_ is not None and not st.sample and next(gm_, 'END') == 'END':
                    gm_ = None
        if gm_ is not None:
            for _ in gm_:
                pass
        if stop in ("B", "B1"):
            break
        phaseC_layer(l)
    if stop:
        store_output()
    S.finish()
    return nc, S


OUT_NAMES = ["yp", "ys", "o_ret", "o_cdk", "o_cdv", "o_csk", "o_csv", "o_gla"]


def make_in_maps(inp, T=2048):
    hc = host_consts(T)
    oidx = odd_col_index()
    w_in_odd_x = np.ascontiguousarray(inp['w_in_odd'][:, :, oidx])
    maps = []
    f = lambda a: np.ascontiguousarray(np.asarray(a, dtype=np.float32))
    for i in range(8):
        b = i % 4
        m = dict(
            xs=f(inp['x_sample'][b]), xp=f(inp['x_prompt'][2 * i:2 * i + 2]),
            cond=f(np.stack([inp['c'][b], inp['c_ctx']])),
            st_ret=f(inp['state_ret'][b]), cdk_i=f(inp['cache_diff_k'][b]), cdv_i=f(inp['cache_diff_v'][b]),
            csk_i=f(inp['cache_swa_k'][b]), csv_i=f(inp['cache_swa_v'][b]), st_gla=f(inp['state_gla'][b]),
            w_mod=f(inp['w_mod']), b_mod=f(inp['b_mod']), ln_g=f(inp['ln_g']), ln_b=f(inp['ln_b']),
            w_in_even=f(inp['w_in_even']), w_out_even=f(inp['w_out_even']), ret_decay=f(inp['ret_decay']),
            diff_lam=f(np.reshape(inp['diff_lam'], (2, 256))), w_in_odd=w_in_odd_x, w_out_odd=f(inp['w_out_odd']),
            swa_sink=f(inp['swa_sink']), gla_w2=f(inp['gla_w2']), gla_b=f(inp['gla_b']),
            w_ff1=f(inp['w_ff1']), w_ff2=f(inp['w_ff2']), **hc)
        maps.append(m)
    return maps


_CACHE = {}


def kernel(**inputs):
    inp = {k: np.asarray(v) for k, v in inputs.items()}
    T = inp['x_sample'].shape[1]
    if T not in _CACHE:
        _CACHE[T] = build(T)[0]
    nc = _CACHE[T]
    maps = make_in_maps(inp, T)
    res = run_bass_kernel_spmd(nc, maps, core_ids=list(range(8)))
    R = res.results
    y_p = np.concatenate([R[i]['yp'] for i in range(8)], axis=0)
    y_s = np.stack([R[b]['ys'] for b in range(4)], axis=0)
    outs = [y_p, y_s]
    for nm in OUT_NAMES[2:]:
        outs.append(np.concatenate([R[i][nm] for i in range(8)], axis=0))
    return tuple(np.ascontiguousarray(o, dtype=np.float32) for o in outs)
```

```python
import numpy as np
from contextlib import ExitStack
import concourse.bass as bass
import concourse.mybir as mybir
from concourse.bass_utils import run_bass_kernel_spmd

F32 = mybir.dt.float32
F32R = mybir.dt.float32r
AF = mybir.ActivationFunctionType
ALU = mybir.AluOpType


class Buf:
    __slots__ = ("name", "ap", "w", "r")

    def __init__(self, name, ap):
        self.name = name
        self.ap = ap
        self.w = []
        self.r = []


class Sched:
    ENG = ('pe', 'act', 'dve', 'pool', 'sp')
    NS = 8

    def __init__(self, nc):
        self.nc = nc
        self.es = ExitStack()
        self.streams = {e: [] for e in self.ENG}
        self.cnt = {e: 0 for e in self.ENG}
        self.seen = {e: {} for e in self.ENG}
        self.dma_n = {}
        self.sems = {}
        self.nbuf = 0

    def dram_in(self, name, shape):
        return Buf(name, self.nc.dram_tensor(name, list(shape), F32, kind="ExternalInput").ap())

    def dram_out(self, name, shape):
        return Buf(name, self.nc.dram_tensor(name, list(shape), F32, kind="ExternalOutput").ap())

    def dram_scratch(self, name, shape):
        return Buf(name, self.nc.dram_tensor(name, list(shape), F32, kind="Internal").ap())

    def sbuf(self, name, shape):
        t = self.es.enter_context(self.nc.sbuf_tensor(name, list(shape), F32))
        return Buf(name, t[:])

    def psum(self, name):
        t = self.es.enter_context(self.nc.psum_tensor(name, [128, 512], F32))
        return Buf(name, t[:])

    def view(self, name, ap, olds=()):
        b = Buf(name, ap)
        for o in olds:
            b.r += o.w + o.r
        return b

    def _op(self, eng, fn, reads, writes, dma_q=None):
        deps = {}

        def add(tok):
            k, v = tok
            if deps.get(k, 0) < v:
                deps[k] = v
        for b in reads:
            for t in b.w:
                add(t)
        for b in writes:
            for t in b.w:
                add(t)
            for t in b.r:
                add(t)
        if dma_q is not None:
            n = self.dma_n.get(dma_q, 0)
            self.dma_n[dma_q] = n + 1
            slot, k = n % self.NS, n // self.NS
            key = ('d', dma_q, slot)
            if k > 0:
                add((key, 16 * k))
            tok = (key, 16 * (k + 1))
            inc = 16
        else:
            self.cnt[eng] += 1
            tok = (eng, self.cnt[eng])
            inc = 1
        seen = self.seen[eng]
        waits = []
        for k, v in deps.items():
            if eng == 'pe' and k == 'pe':
                continue
            if seen.get(k, 0) >= v:
                continue
            seen[k] = v
            waits.append((k, v))
        self.streams[eng].append((waits, fn, tok[0], inc))
        for b in writes:
            b.w = [tok]
            b.r = []
        for b in reads:
            if any(b is x for x in writes):
                continue
            b.r = [t for t in b.r if t[0] != tok[0]] + [tok]
        return tok

    def mm(self, pb, out_ap, lb, lhsT, rb, rhs, start=True, stop=True, r=False):
        if r:
            lhsT = lhsT.bitcast(F32R)
            rhs = rhs.bitcast(F32R)
        self._op('pe', lambda e: e.matmul(out_ap, lhsT, rhs, start=start, stop=stop), [lb, rb], [pb])

    def transpose(self, pb, out_ap, ib, in_ap, idb, id_ap):
        self._op('pe', lambda e: e.transpose(out_ap, in_ap, id_ap), [ib, idb], [pb])

    def act(self, ob, out_ap, ib, in_ap, func, scale=None, bias=None, reads=()):
        kw = {}
        if scale is not None:
            kw['scale'] = scale
        if bias is not None:
            kw['bias'] = bias
        self._op('act', lambda e: e.activation(out_ap, in_ap, func, **kw), [ib] + list(reads), [ob])

    def dve(self, fn, reads, writes):
        self._op('dve', fn, list(reads), list(writes))

    def pool(self, fn, reads, writes):
        self._op('pool', fn, list(reads), list(writes))

    def dma(self, q, out_ap, in_ap, reads, writes, **kw):
        self._op(q, lambda e: e.dma_start(out=out_ap, in_=in_ap, **kw), list(reads), list(writes), dma_q=q)

    def finish(self):
        nc = self.nc
        fw = []
        for q, n in self.dma_n.items():
            for slot in range(min(n, self.NS)):
                cnt = (n - slot + self.NS - 1) // self.NS
                fw.append((('d', q, slot), 16 * cnt))
        self.streams['sp'].append((fw, None, None, 0))
        keys = set()
        for e in self.ENG:
            for waits, fn, key, inc in self.streams[e]:
                if key is not None:
                    keys.add(key)
        for i, k in enumerate(sorted(keys, key=str)):
            self.sems[k] = self.es.enter_context(nc.semaphore("sem%d" % i))
        with nc.Block() as block:
            def mk(ename):
                def body(e):
                    for waits, fn, key, inc in self.streams[ename]:
                        for k, v in waits:
                            e.wait_ge(self.sems[k], v)
                        if fn is not None:
                            fn(e).then_inc(self.sems[key], inc)
                return body
            block.tensor(mk('pe'))
            block.scalar(mk('act'))
            block.vector(mk('dve'))
            block.gpsimd(mk('pool'))
            block.sync(mk('sp'))
        self.es.close()

    def op(self, eng, name, reads, writes, **kw):
        self._op(eng, lambda e: getattr(e, name)(**kw), list(reads), list(writes))


class Rot:
    def __init__(self, items):
        self.items = list(items)
        self.i = 0

    def next(self):
        x = self.items[self.i % len(self.items)]
        self.i += 1
        return x


AX = mybir.AxisListType.X
D = 1024
LCTX = 256
G = 256
ALPHA = 8.0 ** 0.25
EPS = 1e-5


def host_consts(T):
    ident = np.eye(128, dtype=np.float32)
    onesd = np.full((128, 128), 1.0 / 1024.0, dtype=np.float32)
    j = np.arange(128)[:, None]
    i = np.arange(128)[None, :]
    trif = (j <= i).astype(np.float32)
    trib = (j >= i).astype(np.float32)
    tri2 = np.concatenate([np.tile(trif, (1, 2)), np.tile(trib, (1, 2))], axis=1)
    rows = T // 64
    row = np.repeat(np.arange(rows), 64).astype(np.float32)
    col = np.tile(np.arange(64), rows).astype(np.float32)
    half = 32
    freqs = (np.float32(10000.0) ** (-np.arange(0, half, 2, dtype=np.float32) / np.float32(half))).astype(np.float32)
    ar = (row[:, None] * freqs).astype(np.float32)
    ac = (col[:, None] * freqs).astype(np.float32)
    cosT = np.zeros((64, T), np.float32)
    sinT = np.zeros((64, T), np.float32)
    perm = np.zeros((64, 64), np.float32)
    for d in range(64):
        blk, e = d // 32, d % 32
        ang = ar if blk == 0 else ac
        f = e % 16
        cosT[d] = np.cos(ang[:, f])
        sinT[d] = (-np.sin(ang[:, f])) if e < 16 else np.sin(ang[:, f])
        partner = d + 16 if e < 16 else d - 16
        perm[partner, d] = 1.0
    cos2 = np.concatenate([cosT, cosT], axis=0)
    sin2 = np.concatenate([sinT, sinT], axis=0)
    perm2 = np.zeros((128, 128), np.float32)
    perm2[:64, :64] = perm
    perm2[64:, 64:] = perm
    return dict(ident=ident, onesd=onesd, tri2=tri2, cos2=cos2, sin2=sin2, perm2=perm2)


ODD_W = 2464


def odd_col_index():
    sq = list(range(0, 512))
    sk = list(range(512, 640))
    sv = list(range(640, 768))
    gq = list(range(768, 1024))
    gk = list(range(1024, 1280))
    gv = list(range(1280, 1792))
    gr = list(range(1792, 2304))
    glr = list(range(2304, 2336))
    skdup = sk[0:64] + sk[0:64] + sk[64:128] + sk[64:128]
    idx = sq + skdup + gq + gk + glr + sv + gv + gr
    assert len(idx) == ODD_W
    return np.array(idx)


import math


def build(T=2048, NL=4, AR=22528):
    nc = bass.Bass("TRN2", target_bir_lowering=False)
    S = Sched(nc)
    NGS = T // G
    NGRP = NGS + 2
    xs = S.dram_in("xs", [T, D])
    xp = S.dram_in("xp", [2, 256, D])
    cond = S.dram_in("cond", [2, D])
    st_ret = S.dram_in("st_ret", [2, 2, 4, 64, 128])
    cdk_i = S.dram_in("cdk_i", [2, 4, 2, 256, 64])
    cdv_i = S.dram_in("cdv_i", [2, 4, 256, 128])
    csk_i = S.dram_in("csk_i", [2, 2, 256, 64])
    csv_i = S.dram_in("csv_i", [2, 2, 256, 64])
    st_gla = S.dram_in("st_gla", [2, 2, 4, 64, 128])
    w_mod = S.dram_in("w_mod", [4, D, 6144])
    b_mod = S.dram_in("b_mod", [4, 6144])
    ln_g = S.dram_in("ln_g", [4, 2, D])
    ln_b = S.dram_in("ln_b", [4, 2, D])
    w_in_even = S.dram_in("w_in_even", [2, D, 3072])
    w_out_even = S.dram_in("w_out_even", [2, D, D])
    ret_decay = S.dram_in("ret_decay", [2, 2, 4])
    diff_lam = S.dram_in("diff_lam", [2, 256])
    w_in_odd = S.dram_in("w_in_odd", [2, D, ODD_W])
    w_out_odd = S.dram_in("w_out_odd", [2, D, D])
    swa_sink = S.dram_in("swa_sink", [2, 8])
    gla_w2 = S.dram_in("gla_w2", [2, 2, 16, 256])
    gla_b = S.dram_in("gla_b", [2, 2, 256])
    w_ff1 = S.dram_in("w_ff1", [4, D, 4096])
    w_ff2 = S.dram_in("w_ff2", [4, 4096, D])
    c_ident = S.dram_in("ident", [128, 128])
    c_onesd = S.dram_in("onesd", [128, 128])
    c_tri2 = S.dram_in("tri2", [128, 512])
    c_cos = S.dram_in("cos2", [128, T])
    c_sin = S.dram_in("sin2", [128, T])
    c_perm = S.dram_in("perm2", [128, 128])
    yp = S.dram_out("yp", [2, 256, D])
    ys = S.dram_out("ys", [T, D])
    o_ret = S.dram_out("o_ret", [2, 2, 2, 4, 64, 128])
    o_cdk = S.dram_out("o_cdk", [2, 2, 4, 2, 256, 64])
    o_cdv = S.dram_out("o_cdv", [2, 2, 4, 256, 128])
    o_csk = S.dram_out("o_csk", [2, 2, 2, 256, 64])
    o_csv = S.dram_out("o_csv", [2, 2, 2, 256, 64])
    o_gla = S.dram_out("o_gla", [2, 2, 2, 4, 64, 128])

    class St:
        pass
    streams = []
    for i in range(3):
        st = St()
        st.i = i
        st.sample = (i == 0)
        st.T = T if i == 0 else 256
        st.c = 0 if i == 0 else 1
        st.g0 = 0 if i == 0 else NGS + (i - 1)
        st.ng = st.T // G
        st.pi = i - 1
        st.koff = 256 if i == 0 else 0
        st.FA = S.dram_scratch("FA%d" % i, [12, 128, st.T])
        st.TA = S.dram_scratch("TA%d" % i, [st.T, 1536])
        st.MIX = S.dram_scratch("MIX%d" % i, [8, 128, st.T])
        st.FAb = [[Buf("FA", None) for _ in range(st.ng)] for _ in range(12)]
        st.TAb = [[Buf("TA", None) for _ in range(st.ng)] for _ in range(3)]
        st.MIXb = [[Buf("MIX", None) for _ in range(st.ng)] for _ in range(8)]
        streams.append(st)

    xT = [S.sbuf("xT%d" % g, [128, 8, G]) for g in range(NGRP)]
    WB = [S.sbuf("wb%d" % i, [128, 4096]) for i in range(2)]
    wrot = Rot(WB)
    arena = S.sbuf("arena", [128, AR])
    ident_t = S.sbuf("ident_t", [128, 128])
    onesd_t = S.sbuf("onesd_t", [128, 128])
    tri2_t = S.sbuf("tri2_t", [128, 512])
    perm_t = S.sbuf("perm_t", [128, 128])
    ones_t = S.sbuf("ones_t", [128, 128])
    eps_t = S.sbuf("eps_t", [128, 1])
    stg0 = S.sbuf("stg0", [128, 128])
    sc_t = S.sbuf("sc_t", [128, 8, 2])
    modts = [S.sbuf("modt%d" % i, [128, 2, 48]) for i in range(2)]
    bm_t = S.sbuf("bm_t", [128, 48])
    lng_t = S.sbuf("lng_t", [128, 64])
    lnb_t = S.sbuf("lnb_t", [128, 64])
    P = [S.psum("ps%d" % i) for i in range(8)]

    ar = dict(off=0, views=[], summ={})

    def new_phase():
        for v in ar['views']:
            for k, val in v.w + v.r:
                if ar['summ'].get(k, 0) < val:
                    ar['summ'][k] = val
        ar['views'] = []
        ar['off'] = 0

    def alloc(name, shape):
        n = 1
        for s_ in shape[1:]:
            n *= s_
        off = ar['off']
        ar['off'] += n
        assert ar['off'] <= AR, (name, ar['off'])
        ap = arena.ap[0:shape[0], off:off + n]
        if len(shape) == 3:
            ap = ap.rearrange("p (a b) -> p a b", a=shape[1])
        b = Buf(name, ap)
        b.r = list(ar['summ'].items())
        ar['views'].append(b)
        return b

    def arot(name, shape, n):
        return Rot([alloc("%s%d" % (name, i), shape) for i in range(n)])

    def wload(src_buf, src_ap, shape):
        wb = wrot.next()
        n = 1
        for s_ in shape[1:]:
            n *= s_
        ap = wb.ap[:, 0:n]
        if len(shape) == 3:
            ap = ap.rearrange("p (a b) -> p a b", a=shape[1])
        S.dma('sp', ap, src_ap, [src_buf], [wb])
        return wb, ap

    def tt(eng, ob, out, ab, a, bb, b, op):
        S.op(eng, 'tensor_tensor', [ab, bb], [ob], out=out, in0=a, in1=b, op=op)

    def ts(eng, ob, out, ib, in0, s1, s2=None, op0=ALU.mult, op1=None, reads=()):
        kw = dict(out=out, in0=in0, scalar1=s1, scalar2=s2, op0=op0)
        if op1 is not None:
            kw['op1'] = op1
        S.op(eng, 'tensor_scalar', [ib] + list(reads), [ob], **kw)

    def load_cols(dst, dst_ap, src, src_rows_ap, R, func=AF.Identity):
        S.dma('sp', stg0.ap[0:R, :], src_rows_ap, [src], [stg0])
        S.transpose(P[7], P[7].ap[:, 0:R], stg0, stg0.ap[0:R, :], ident_t, ident_t.ap[0:R, 0:R])
        S.act(dst, dst_ap, P[7], P[7].ap[:, 0:R], func)

    for tb, cb in ((ident_t, c_ident), (onesd_t, c_onesd), (tri2_t, c_tri2), (perm_t, c_perm)):
        S.dma('sp', tb.ap, cb.ap, [cb], [tb])
    S.op('dve', 'memset', [], [ones_t], ap=ones_t.ap, constant=1.0)
    S.op('dve', 'memset', [], [eps_t], ap=eps_t.ap, constant=EPS)
    for c in range(2):
        load_cols(sc_t, sc_t.ap[:, :, c], cond, cond.ap[c].rearrange("(k p) -> k p", p=128), 8, AF.Silu)
    load_cols(lng_t, lng_t.ap, ln_g, ln_g.ap.rearrange("l i (k p) -> (l i k) p", p=128), 64)
    load_cols(lnb_t, lnb_t.ap, ln_b, ln_b.ap.rearrange("l i (k p) -> (l i k) p", p=128), 64)

    def load_input():
        new_phase()
        xr = arot("xin", [128, 1024], 2)
        pr = Rot(P[0:6])
        for st in streams:
            for gi in range(st.ng):
                g = st.g0 + gi
                for t2 in range(2):
                    r0 = gi * G + t2 * 128
                    src = xs.ap[r0:r0 + 128, :] if st.sample else xp.ap[st.pi, r0:r0 + 128, :]
                    sb = xs if st.sample else xp
                    xt_ = xr.next()
                    S.dma('sp', xt_.ap, src, [sb], [xt_])
                    for hf in range(2):
                        ps = pr.next()
                        for c4 in range(4):
                            cc = hf * 4 + c4
                            S.transpose(ps, ps.ap[:, c4 * 128:(c4 + 1) * 128], xt_, xt_.ap[:, cc * 128:(cc + 1) * 128],
                                        ident_t, ident_t.ap)
                        S.act(xT[g], xT[g].ap[:, hf * 4:(hf + 1) * 4, t2 * 128:(t2 + 1) * 128], ps,
                              ps.ap.rearrange("p (c t) -> p c t", c=4), AF.Identity)

    def store_output():
        new_phase()
        orot = arot("xout", [128, 1024], 2)
        pr = Rot(P[0:6])
        for st in streams:
            for gi in range(st.ng):
                g = st.g0 + gi
                for t2 in range(2):
                    r0 = gi * G + t2 * 128
                    ot_ = orot.next()
                    for hf in range(2):
                        ps = pr.next()
                        for c4 in range(4):
                            cc = hf * 4 + c4
                            S.transpose(ps, ps.ap[:, c4 * 128:(c4 + 1) * 128], xT[g],
                                        xT[g].ap[:, cc, t2 * 128:(t2 + 1) * 128], ident_t, ident_t.ap)
                        S.act(ot_, ot_.ap[:, hf * 512:(hf + 1) * 512], ps, ps.ap, AF.Identity)
                    if st.sample:
                        S.dma('pool', ys.ap[r0:r0 + 128, :], ot_.ap, [ot_], [ys])
                    else:
                        S.dma('pool', yp.ap[st.pi, r0:r0 + 128, :], ot_.ap, [ot_], [yp])

    def compute_mod_gen(l, pm_):
        modt = modts[l % 2]
        S.dma('sp', stg0.ap[0:48, :], b_mod.ap[l].rearrange("(j p) -> j p", p=128), [b_mod], [stg0])
        S.transpose(pm_, pm_.ap[:, 0:48], stg0, stg0.ap[0:48, :], ident_t, ident_t.ap[0:48, 0:48])
        S.act(bm_t, bm_t.ap, pm_, pm_.ap[:, 0:48], AF.Identity)
        yield
        for blk in range(12):
            wb, wap = wload(w_mod, w_mod.ap[l, :, blk * 512:(blk + 1) * 512].rearrange("(k p) n -> p k n", p=128),
                            [128, 8, 512])
            for e4 in range(4):
                ec = blk * 4 + e4
                for kc in range(8):
                    S.mm(pm_, pm_.ap[:, 2 * ec:2 * ec + 2], wb, wap[:, kc, e4 * 128:(e4 + 1) * 128], sc_t,
                         sc_t.ap[:, kc, :], start=(kc == 0), stop=(kc == 7))
                yield
        pmv = pm_.ap[:, 0:96].rearrange("p (e c) -> p e c", c=2)
        for c in range(2):
            tt('dve', modt, modt.ap[:, c, :], pm_, pmv[:, :, c], bm_t, bm_t.ap, ALU.add)
        for lo in (8, 32):
            ts('dve', modt, modt.ap[:, :, lo:lo + 8], modt, modt.ap[:, :, lo:lo + 8], 1.0, None, ALU.add)

    def phaseA(l, st):
        modt = modts[l % 2]
        even = (l % 2 == 0)
        li = l // 2
        new_phase()
        hTs = [alloc("hT%d" % i, [128, 8, G]) for i in range(2)]
        sfm = arot("sfm", [128, G], 4)
        stm = arot("stm", [128, 512], 3)
        cosg = arot("cosg", [128, G], 2)
        sing = arot("sing", [128, G], 2)
        t1r = arot("t1r", [128, G], 2)
        t2r = arot("t2r", [128, G], 2)
        pr = Rot(P[0:6])
        pp = Rot(P[6:8])
        c = st.c
        W = w_in_even if even else w_in_odd
        if even:
            blocks = [
                (0, 512, [('fm', 0, 128, 0, 1.0, False), ('fm', 128, 128, 1, 1.0, False),
                          ('fm', 256, 128, 2, 0.125, False), ('fm', 384, 128, 3, 0.125, False)]),
                (512, 512, [('tm', 0, 512, 0, 0, AF.Identity, None)]),
                (1024, 512, [('tm', 0, 512, 1, 0, AF.Silu, None)]),
                (1536, 512, [('fm', h * 128, 128, 4 + h, 1.0, True) for h in range(4)]),
                (2048, 512, [('fm', h * 128, 128, 8 + h, 1.0, True) for h in range(4)] +
                 ([] if st.sample else [('tm', 0, 512, None, 0, AF.Identity, 'cdk')])),
                (2560, 512, [('tm', 0, 512, 2, 0, AF.Identity, None if st.sample else 'cdv')]),
            ]
        else:
            blocks = [
                (0, 512, [('fm', p_ * 128, 128, p_, 1.0, True) for p_ in range(4)]),
                (512, 512, [('fm', 0, 128, 4, 1.0, True), ('fm', 128, 128, 5, 1.0, True),
                            ('fm', 256, 128, 6, 0.125, False), ('fm', 384, 128, 7, 0.125, False)] +
                 ([] if st.sample else [('tmk',)])),
                (1024, 416, [('fm', 0, 128, 8, 1.0, False), ('fm', 128, 128, 9, 1.0, False),
                             ('fm', 256, 16, 10, 1.0, False), ('fm', 272, 16, 11, 1.0, False),
                             ('tm', 288, 128, 2, 0, AF.Identity, None if st.sample else 'csv')]),
                (1440, 512, [('tm', 0, 512, 0, 0, AF.Identity, None)]),
                (1952, 512, [('tm', 0, 512, 1, 0, AF.Silu, None)]),
            ]
        pend_rope = [None]

        def flush_rope():
            if pend_rope[0] is not None:
                f_ = pend_rope[0]
                pend_rope[0] = None
                f_()
        if l == 0:
            xr = arot("xin", [128, 1024], 2)

        def load_group(gi_):
            g_ = st.g0 + gi_
            for t2 in range(2):
                r0 = gi_ * G + t2 * 128
                src = xs.ap[r0:r0 + 128, :] if st.sample else xp.ap[st.pi, r0:r0 + 128, :]
                sb = xs if st.sample else xp
                xt_ = xr.next()
                S.dma('sp', xt_.ap, src, [sb], [xt_])
                for hf in range(2):
                    ps = pr.next()
                    for c4 in range(4):
                        cc = hf * 4 + c4
                        S.transpose(ps, ps.ap[:, c4 * 128:(c4 + 1) * 128], xt_, xt_.ap[:, cc * 128:(cc + 1) * 128],
                                    ident_t, ident_t.ap)
                    S.act(xT[g_], xT[g_].ap[:, hf * 4:(hf + 1) * 4, t2 * 128:(t2 + 1) * 128], ps,
                          ps.ap.rearrange("p (c t) -> p c t", c=4), AF.Identity)
        if l == 0:
            load_group(0)
        for gi in range(st.ng):
            g = st.g0 + gi
            t0 = gi * G
            h = hTs[gi % 2]
            for kc in range(8):
                S.act(h, h.ap[:, kc, :], xT[g], xT[g].ap[:, kc, :], AF.Identity,
                      scale=modt.ap[:, c, 8 + kc:9 + kc], bias=modt.ap[:, c, kc:kc + 1], reads=[modt])
            if l == 0 and gi + 1 < st.ng:
                load_group(gi + 1)
            if st.sample:
                cg = cosg.next()
                sg_ = sing.next()
                S.dma('sp', cg.ap, c_cos.ap[:, t0:t0 + G], [c_cos], [cg])
                S.dma('sp', sg_.ap, c_sin.ap[:, t0:t0 + G], [c_sin], [sg_])
            for (wc0, wn, jobs) in blocks:
                wb, wap = wload(W, W.ap[li, :, wc0:wc0 + wn].rearrange("(k p) n -> p k n", p=128), [128, 8, wn])
                for job in jobs:
                    if job[0] == 'fm':
                        _, c0, M, slot, scale, rope = job
                        ps = pr.next()
                        for kc in range(8):
                            S.mm(ps, ps.ap[0:M, 0:G], wb, wap[:, kc, c0:c0 + M], h, h.ap[:, kc, :],
                                 start=(kc == 0), stop=(kc == 7))
                        flush_rope()
                        sg = sfm.next()
                        S.act(sg, sg.ap[0:M, :], ps, ps.ap[0:M, 0:G], AF.Identity, scale=float(scale))
                        if rope and st.sample:
                            def rope_tail(sg=sg, slot=slot, M=M, cg=cg, sg_=sg_, gi=gi, t0=t0):
                                pq = pp.next()
                                S.mm(pq, pq.ap[:, 0:G], perm_t, perm_t.ap, sg, sg.ap)
                                t1 = t1r.next()
                                t2 = t2r.next()
                                tt('dve', t1, t1.ap, sg, sg.ap, cg, cg.ap, ALU.mult)
                                tt('dve', t2, t2.ap, pq, pq.ap[:, 0:G], sg_, sg_.ap, ALU.mult)
                                sg2 = sfm.next()
                                tt('pool', sg2, sg2.ap, t1, t1.ap, t2, t2.ap, ALU.add)
                                S.dma('pool', st.FA.ap[slot, 0:M, t0:t0 + G], sg2.ap[0:M, :], [sg2], [st.FAb[slot][gi]])
                            pend_rope[0] = rope_tail
                        else:
                            S.dma('pool', st.FA.ap[slot, 0:M, t0:t0 + G], sg.ap[0:M, :], [sg], [st.FAb[slot][gi]])
                    elif job[0] == 'tm':
                        _, c0, N, tblk, _, func, outk = job
                        for t2_ in range(2):
                            ps = pr.next()
                            for kc in range(8):
                                S.mm(ps, ps.ap[:, 0:N], h, h.ap[:, kc, t2_ * 128:(t2_ + 1) * 128], wb,
                                     wap[:, kc, c0:c0 + N], start=(kc == 0), stop=(kc == 7))
                            flush_rope()
                            sg = stm.next()
                            S.act(sg, sg.ap[:, 0:N], ps, ps.ap[:, 0:N], func)
                            r0 = t0 + t2_ * 128
                            if tblk is not None:
                                dc = {0: 0, 1: 512, 2: 1024}[tblk]
                                S.dma('pool', st.TA.ap[r0:r0 + 128, dc:dc + N], sg.ap[:, 0:N], [sg],
                                      [st.TAb[tblk][gi]])
                            if outk == 'cdk':
                                S.dma('pool', o_cdk.ap[st.pi, li].rearrange("h m t d -> t (h m) d")[r0:r0 + 128, :, :],
                                      sg.ap[:, 0:512].rearrange("t (a d) -> t a d", d=64), [sg], [o_cdk])
                            elif outk == 'cdv':
                                S.dma('pool', o_cdv.ap[st.pi, li].rearrange("h t d -> t h d")[r0:r0 + 128, :, :],
                                      sg.ap[:, 0:512].rearrange("t (a d) -> t a d", d=128), [sg], [o_cdv])
                            elif outk == 'csv':
                                S.dma('pool', o_csv.ap[st.pi, li].rearrange("k t d -> t k d")[r0:r0 + 128, :, :],
                                      sg.ap[:, 0:128].rearrange("t (a d) -> t a d", d=64), [sg], [o_csv])
                    else:
                        for t2_ in range(2):
                            ps = pr.next()
                            for kv in range(2):
                                for kc in range(8):
                                    S.mm(ps, ps.ap[:, kv * 64:(kv + 1) * 64], h, h.ap[:, kc, t2_ * 128:(t2_ + 1) * 128],
                                         wb, wap[:, kc, kv * 128:kv * 128 + 64], start=(kc == 0), stop=(kc == 7))
                            sg = stm.next()
                            S.act(sg, sg.ap[:, 0:128], ps, ps.ap[:, 0:128], AF.Identity)
                            r0 = t0 + t2_ * 128
                            S.dma('pool', o_csk.ap[st.pi, li].rearrange("k t d -> t k d")[r0:r0 + 128, :, :],
                                  sg.ap[:, 0:128].rearrange("t (a d) -> t a d", d=64), [sg], [o_csk])

        flush_rope()

    def head_norm(o_t, o_n, stt, stt2, h):
        hs = slice(h * 128, (h + 1) * 128)
        S.op('dve', 'bn_stats', [o_t], [stt], out=stt.ap[:, h * 8:h * 8 + 6], in_=o_t.ap[:, hs])
        S.op('dve', 'bn_aggr', [stt], [stt], out=stt.ap[:, h * 8 + 6:h * 8 + 8], in_=stt.ap[:, h * 8:h * 8 + 6])
        S.act(stt2, stt2.ap[:, h:h + 1], stt, stt.ap[:, h * 8 + 7:h * 8 + 8], AF.Sqrt, bias=eps_t.ap[:, 0:1],
              reads=[eps_t])
        S.op('dve', 'reciprocal', [stt2], [stt2], out=stt2.ap[:, 2 + h:3 + h], in_=stt2.ap[:, h:h + 1])
        ts('dve', o_n, o_n.ap[:, hs], o_t, o_t.ap[:, hs], stt.ap[:, h * 8 + 6:h * 8 + 7], stt2.ap[:, 2 + h:3 + h],
           ALU.subtract, ALU.mult, reads=[stt, stt2])

    def phaseB_lin(l, st, kind):
        li = l // 2
        T_ = st.T
        nch = T_ // 128
        ret = (kind == 'ret')
        qslot, kslot = (0, 2) if ret else (6, 8)
        mix0 = 0 if ret else 4
        st_in = st_ret if ret else st_gla
        o_st = o_ret if ret else o_gla
        sc = 1.0 if ret else 1.0 / 16.0
        lin_tiles = {}
        for pc in range(2):
            if pc == 0:
                lin_tiles['Of'] = [alloc("Of%d" % n_, [128, 256]) for n_ in range(nch)]
                lin_tiles['Sst'] = alloc("Sst", [128, 128])
                lin_tiles['rots'] = dict(
                    QTc=arot("QTc", [128, 128], 3), KTc=arot("KTc", [128, 128], 3), Vc=arot("Vc", [128, 256], 2),
                    Gc=arot("Gc", [128, 256], 2), EQr=arot("EQ", [128, 128], 2), EKr=arot("EK", [128, 128], 2),
                    qzr=[arot("qz%d" % h_, [128, 128], 2) for h_ in range(2)], kdr=arot("kd", [128, 128], 2),
                    ktr=arot("kt", [128, 128], 2), WTr=arot("WT", [128, 256], 2), Usr=arot("Us", [128, 256], 2),
                    otr=arot("ot", [128, 256], 2), onr=arot("on", [128, 256], 2), mfr=arot("mf", [128, 128], 2),
                    lapr=arot("lap", [128, 128], 2), e1r=arot("e1", [128, 128], 2), glr_r=arot("glrc", [17, 128], 2),
                    stt=alloc("stt", [128, 16]), stt2=alloc("stt2", [128, 4]), rd=alloc("rd", [128, 4]),
                    w2a=[alloc("w2a%d" % d, [17, 128]) for d in range(2)])
            Of = lin_tiles['Of']
            Sst = lin_tiles['Sst']
            R_ = lin_tiles['rots']
            QTc, KTc, Vc = R_['QTc'], R_['KTc'], R_['Vc']
            Gc, EQr, EKr, qzr, kdr, ktr = R_['Gc'], R_['EQr'], R_['EKr'], R_['qzr'], R_['kdr'], R_['ktr']
            WTr, Usr, otr, onr, mfr = R_['WTr'], R_['Usr'], R_['otr'], R_['onr'], R_['mfr']
            lapr, e1r, glr_r, stt, stt2, rd, w2a = R_['lapr'], R_['e1r'], R_['glr_r'], R_['stt'], R_['stt2'], R_['rd'], R_['w2a']
            if pc == 0:
                for h_ in range(2):
                    for b_ in qzr[h_].items:
                        S.op('dve', 'memset', [], [b_], ap=b_.ap, constant=0.0)
            pA = Rot([P[6]])
            pO = Rot([P[7]])
            if not ret:
                if pc == 0:
                    for b_ in glr_r.items:
                        S.op('dve', 'memset', [], [b_], ap=b_.ap, constant=1.0)
                for d in range(2):
                    S.dma('sp', w2a[d].ap[0:16, :], gla_w2.ap[li, d, :, pc * 128:(pc + 1) * 128], [gla_w2], [w2a[d]])
                    S.dma('sp', w2a[d].ap[16:17, :], gla_b.ap[li, d:d + 1, pc * 128:(pc + 1) * 128], [gla_b], [w2a[d]])
            else:
                for d in range(2):
                    S.dma('sp', rd.ap[:, d * 2:d * 2 + 2], ret_decay.ap[li, d, 2 * pc:2 * pc + 2].partition_broadcast(128),
                          [ret_decay], [rd])
                S.act(rd, rd.ap, rd, rd.ap, AF.Exp)
            for d in range(2):
                tri_d = tri2_t.ap[:, d * 256:d * 256 + 128]
                tri_d2 = tri2_t.ap[:, d * 256:(d + 1) * 256]
                deccol = 127 if d == 0 else 0
                if st.sample:
                    for h in range(2):
                        S.dma('sp', Sst.ap[h * 64:(h + 1) * 64, :], st_in.ap[li, d, 2 * pc + h], [st_in], [Sst])
                else:
                    S.op('dve', 'memset', [], [Sst], ap=Sst.ap, constant=0.0)

                def decay_tiles(lap):
                    pb_ = pA.next()
                    S.mm(pb_, pb_.ap[:, 0:128], lap, lap.ap, tri2_t, tri_d)
                    EQ = EQr.next()
                    EK = EKr.next()
                    S.act(EQ, EQ.ap, pb_, pb_.ap[:, 0:128], AF.Exp, scale=-sc)
                    S.act(EK, EK.ap, pb_, pb_.ap[:, 0:128], AF.Exp, scale=sc)
                    return EQ, EK
                const_dec = None
                if ret:
                    lap = lapr.next()
                    for h in range(2):
                        ts('dve', lap, lap.ap[:, h * 64:(h + 1) * 64], ones_t, ones_t.ap[:, 0:64],
                           rd.ap[:, d * 2 + h:d * 2 + h + 1], None, ALU.mult, reads=[rd])
                    const_dec = decay_tiles(lap)

                def front(n, out):
                    cs = slice(n * 128, (n + 1) * 128)
                    gi = n // 2
                    qc = QTc.next()
                    kc_ = KTc.next()
                    vc = Vc.next()
                    S.dma('sp', qc.ap, st.FA.ap[qslot + pc, :, cs], [st.FAb[qslot + pc][gi]], [qc])
                    S.dma('sp', kc_.ap, st.FA.ap[kslot + pc, :, cs], [st.FAb[kslot + pc][gi]], [kc_])
                    S.dma('sp', vc.ap, st.TA.ap[cs, pc * 256:(pc + 1) * 256], [st.TAb[0][gi]], [vc])
                    if not ret:
                        gl = glr_r.next()
                        S.dma('sp', gl.ap[0:16, :], st.FA.ap[10 + d, 0:16, cs], [st.FAb[10 + d][gi]], [gl])
                        yield
                        pz = pA.next()
                        S.mm(pz, pz.ap[:, 0:128], gl, gl.ap[0:17, :], w2a[d], w2a[d].ap[0:17, :])
                        yield
                        e1 = e1r.next()
                        S.act(e1, e1.ap, pz, pz.ap[:, 0:128], AF.Exp, scale=-1.0)
                        yield
                        lap_ = lapr.next()
                        S.act(lap_, lap_.ap, e1, e1.ap, AF.Ln, bias=ones_t.ap[:, 0:1], reads=[ones_t])
                        yield
                        EQ, EK = decay_tiles(lap_)
                    else:
                        EQ, EK = const_dec
                    yield
                    kd = kdr.next()
                    qz = [qzr[0].next(), qzr[1].next()]
                    for h in range(2):
                        rs = slice(h * 64, (h + 1) * 64)
                        tt('dve', qz[h], qz[h].ap[rs, :], qc, qc.ap[rs, :], EQ, EQ.ap[rs, :], ALU.mult)
                    tt('pool', kd, kd.ap, kc_, kc_.ap, EK, EK.ap, ALU.mult)
                    yield
                    pk = pA.next()
                    S.transpose(pk, pk.ap[:, 0:128], kd, kd.ap, ident_t, ident_t.ap)
                    yield
                    kt = ktr.next()
                    S.act(kt, kt.ap, pk, pk.ap[:, 0:128], AF.Identity)
                    yield
                    pa = pA.next()
                    for h in range(2):
                        S.mm(pa, pa.ap[:, h * 128:(h + 1) * 128], kd, kd.ap, qz[h], qz[h].ap)
                    yield
                    wt = WTr.next()
                    tt('dve', wt, wt.ap, pa, pa.ap[:, 0:256], tri2_t, tri_d2, ALU.mult)
                    yield
                    pu = pA.next()
                    S.mm(pu, pu.ap[:, 0:256], kt, kt.ap, vc, vc.ap)
                    yield
                    us = Usr.next()
                    S.act(us, us.ap, pu, pu.ap[:, 0:256], AF.Identity)
                    out.update(dict(EQ=EQ, qz=qz, wt=wt, vc=vc, us=us))

                def back(n, tl, first_pass):
                    cs = slice(n * 128, (n + 1) * 128)
                    gi = n // 2
                    EQ, qz, wt, vc, us = tl['EQ'], tl['qz'], tl['wt'], tl['vc'], tl['us']
                    po = pO.next()
                    for h in range(2):
                        hs = slice(h * 128, (h + 1) * 128)
                        S.mm(po, po.ap[:, hs], wt, wt.ap[:, hs], vc, vc.ap[:, hs], start=True, stop=False)
                        S.mm(po, po.ap[:, hs], qz[h], qz[h].ap, Sst, Sst.ap, start=False, stop=True)
                    yield
                    for h in range(2):
                        rs = slice(h * 64, (h + 1) * 64)
                        tt('dve', Sst, Sst.ap[rs, :], Sst, Sst.ap[rs, :], us, us.ap[rs, h * 128:(h + 1) * 128], ALU.add)
                        ts('dve', Sst, Sst.ap[rs, :], Sst, Sst.ap[rs, :], EQ.ap[rs, deccol:deccol + 1], None, ALU.mult,
                           reads=[EQ])
                    yield
                    if first_pass:
                        S.act(Of[n], Of[n].ap, po, po.ap[:, 0:256], AF.Identity)
                    else:
                        o_t = otr.next()
                        o_n = onr.next()
                        tt('dve', o_t, o_t.ap, po, po.ap[:, 0:256], Of[n], Of[n].ap, ALU.add)
                        gc = Gc.next()
                        S.dma('sp', gc.ap, st.TA.ap[cs, 512 + pc * 256:512 + (pc + 1) * 256], [st.TAb[1][gi]], [gc])
                        yield
                        for h in range(2):
                            head_norm(o_t, o_n, stt, stt2, h)
                            yield
                        tt('pool', o_n, o_n.ap, o_n, o_n.ap, gc, gc.ap, ALU.mult)
                        yield
                        for h in range(2):
                            pt_ = pO.next()
                            S.transpose(pt_, pt_.ap[:, 0:128], o_n, o_n.ap[:, h * 128:(h + 1) * 128], ident_t, ident_t.ap)
                            mf = mfr.next()
                            S.act(mf, mf.ap, pt_, pt_.ap[:, 0:128], AF.Identity)
                            ch = mix0 + 2 * pc + h
                            S.dma('pool', st.MIX.ap[ch, :, cs], mf.ap, [mf], [st.MIXb[ch][gi]])
                            yield

                order = list(range(nch)) if d == 0 else list(range(nch - 1, -1, -1))
                tl = {}
                for _ in front(order[0], tl):
                    yield
                for i_, n in enumerate(order):
                    gb = back(n, tl, d == 0)
                    tl2 = {}
                    gf = front(order[i_ + 1], tl2) if i_ + 1 < nch else None
                    while gb is not None or gf is not None:
                        if gb is not None and next(gb, 'END') == 'END':
                            gb = None
                        if gf is not None and next(gf, 'END') == 'END':
                            gf = None
                        yield
                    tl = tl2
                if not st.sample:
                    for h in range(2):
                        S.dma('pool', o_st.ap[st.pi, li, d, 2 * pc + h], Sst.ap[h * 64:(h + 1) * 64, :], [Sst], [o_st])

    def run_attn(QT, q0, nqb, entries, KT, V, dv2, PTr, pS, acc):
        cnt = [0] * nqb
        tot = [0] * nqb
        for (kb, qlo, qhi, masks) in entries:
            for qb in range(qlo, qhi):
                tot[qb] += 1

        def pv(ent, pt):
            kb, qlo, qhi, masks = ent
            for qb in range(qlo, qhi):
                o_ = (qb - qlo) * 128
                S.mm(acc[qb], acc[qb].ap[:, 0:dv2], pt, pt.ap[:, o_:o_ + 128], V, V.ap[:, kb, :],
                     start=(cnt[qb] == 0), stop=(cnt[qb] == tot[qb] - 1))
                cnt[qb] += 1
        pend = None
        for ent in entries:
            kb, qlo, qhi, masks = ent
            ps = pS.next()
            w = (qhi - qlo) * 128
            S.mm(ps, ps.ap[:, 0:w], KT, KT.ap[:, kb * 128:(kb + 1) * 128], QT,
                 QT.ap[:, qlo * 128:qhi * 128])
            pt = PTr.next()
            S.act(pt, pt.ap[:, 0:w], ps, ps.ap[:, 0:w], AF.Exp, scale=0.125)
            for qb, mk in masks.items():
                o_ = (qb - qlo) * 128
                tt('pool', pt, pt.ap[:, o_:o_ + 128], pt, pt.ap[:, o_:o_ + 128], tri2_t, mk, ALU.mult)
            if pend is not None:
                pv(*pend)
            pend = (ent, pt)
            yield
        if pend is not None:
            pv(*pend)
        yield

    accsets = [P[0:2], P[2:4]]
    ucnt = [0]

    def phaseB_diff(l, st):
        li = l // 2
        lam_init = 0.8 - 0.6 * math.exp(-0.3 * l)
        T_ = st.T
        nq = T_ // 128
        koff = st.koff
        nkb = (koff + T_) // 128
        kb0 = koff // 128
        QZ = [[alloc("QZ%d_%d" % (m_, b_), [128, 512]) for b_ in range(2)] for m_ in range(2)]
        for m_ in range(2):
            for b_ in range(2):
                S.op('dve', 'memset', [], [QZ[m_][b_]], ap=QZ[m_][b_].ap, constant=0.0)
        KT = alloc("KT", [128, koff + T_])
        V = alloc("V", [128, nkb, 130])
        PTr = arot("PT", [128, 512], 3)
        o1n = alloc("o1n", [128, 4, 128])
        tr_ = arot("tq", [128, 128], 4)
        or_ = arot("oq", [128, 128], 2)
        onr = arot("onq", [128, 128], 2)
        stgr = arot("stgq", [128, 512], 2)
        kst = arot("kst", [128, 128], 2)
        sm = alloc("sm", [128, 8])
        stt = alloc("stt", [128, 16])
        stt2 = alloc("stt2", [128, 4])
        dl = alloc("dl", [128, 256])
        pr_ = alloc("prd", [128, 128])
        lam = alloc("lam", [128, 4])
        pS = Rot(P[4:6])
        S.dma('sp', dl.ap, diff_lam.ap[li].partition_broadcast(128), [diff_lam], [dl])
        dl4 = dl.ap.rearrange("p (a b d) -> p a b d", a=2, b=2)
        tt('dve', pr_, pr_.ap.rearrange("p (a d) -> p a d", a=2), dl, dl4[:, :, 0, :], dl, dl4[:, :, 1, :], ALU.mult)
        S.op('dve', 'tensor_reduce', [pr_], [lam], out=lam.ap[:, 0:2], in_=pr_.ap.rearrange("p (a d) -> p a d", a=2),
             axis=AX, op=ALU.add)
        S.act(lam, lam.ap[:, 0:2], lam, lam.ap[:, 0:2], AF.Exp)
        tt('dve', lam, lam.ap[:, 2:3], lam, lam.ap[:, 1:2], lam, lam.ap[:, 0:1], ALU.subtract)
        ts('dve', lam, lam.ap[:, 3:4], lam, lam.ap[:, 2:3], -lam_init, None, ALU.add)
        S.op('dve', 'memset', [], [V], ap=V.ap[:, :, 128:130], constant=1.0)
        for h in range(4):
            yield
            S.dma('sp', KT.ap[:, koff:koff + T_], st.FA.ap[8 + h], st.FAb[8 + h], [KT])
            S.dma('sp', V.ap[:, kb0:nkb, 0:128],
                  st.TA.ap[:, 1024 + h * 128:1024 + (h + 1) * 128].rearrange("(n t) d -> t n d", t=128),
                  st.TAb[2], [V])
            if st.sample:
                S.dma('sp', V.ap[:, 0:2, 0:128], cdv_i.ap[li, h].rearrange("(n t) d -> t n d", t=128), [cdv_i], [V])
                for lb in range(2):
                    ks = kst.next()
                    S.dma('sp', ks.ap.rearrange("t (m d) -> t m d", m=2),
                          cdk_i.ap[li, h, :, lb * 128:(lb + 1) * 128, :].rearrange("m t d -> t m d"), [cdk_i], [ks])
                    pb7 = pS.next()
                    S.transpose(pb7, pb7.ap[:, 0:128], ks, ks.ap, ident_t, ident_t.ap)
                    S.act(KT, KT.ap[:, lb * 128:(lb + 1) * 128], pb7, pb7.ap[:, 0:128], AF.Identity)
                    yield
            for qg in range((nq + 3) // 4):
                nqb = min(4, nq - 4 * qg)
                stg = stgr.next()
                c0 = qg * 512
                ggs = list(range(c0 // G, (c0 + nqb * 128 + G - 1) // G))
                for m_ in range(2):
                    qz_ = QZ[m_][qg % 2]
                    S.dma('sp', qz_.ap[m_ * 64:(m_ + 1) * 64, 0:nqb * 128],
                          st.FA.ap[4 + h, m_ * 64:(m_ + 1) * 64, c0:c0 + nqb * 128], [st.FAb[4 + h][gg] for gg in ggs], [qz_])
                for m in range(2):
                    acc = P[0:nqb]
                    entries = [(kb, 0, nqb, {}) for kb in range(nkb)]
                    yield from run_attn(QZ[m][qg % 2], 0, nqb, entries, KT, V, 130, PTr, pS, acc)
                    t_s = []
                    for qb in range(nqb):
                        a = acc[qb]
                        S.op('dve', 'reciprocal', [a], [sm], out=sm.ap[:, m * 4 + qb:m * 4 + qb + 1], in_=a.ap[:, 128:129])
                        if m == 0:
                            ts('dve', o1n, o1n.ap[:, qb, :], a, a.ap[:, 0:128], sm.ap[:, qb:qb + 1], None, ALU.mult,
                               reads=[sm])
                        else:
                            t_ = tr_.next()
                            ts('dve', t_, t_.ap, a, a.ap[:, 0:128], sm.ap[:, 4 + qb:5 + qb], lam.ap[:, 3:4], ALU.mult,
                               ALU.mult, reads=[sm, lam])
                            t_s.append(t_)
                    for qb in range(nqb):
                        if m == 1:
                            t_ = t_s[qb]
                            o_ = or_.next()
                            tt('dve', o_, o_.ap, t_, t_.ap, o1n, o1n.ap[:, qb, :], ALU.add)
                            on_ = onr.next()
                            head_norm(o_, on_, stt, stt2, 0)
                            pb7 = pS.next()
                            S.transpose(pb7, pb7.ap[:, 0:128], on_, on_.ap, ident_t, ident_t.ap)
                            S.act(stg, stg.ap[:, qb * 128:(qb + 1) * 128], pb7, pb7.ap[:, 0:128], AF.Identity,
                                  scale=float(1.0 - lam_init))
                            yield
                c0 = qg * 512
                S.dma('pool', st.MIX.ap[4 + h, :, c0:c0 + nqb * 128], stg.ap[:, 0:nqb * 128], [stg],
                      [st.MIXb[4 + h][gg] for gg in range(c0 // G, (c0 + nqb * 128 + G - 1) // G)])

    def phaseB_swa(l, st):
        li = l // 2
        T_ = st.T
        nq = T_ // 128
        koff = st.koff
        nkb = (koff + T_) // 128
        kb0 = koff // 128
        QZ = [[alloc("QZ%d_%d" % (m_, b_), [128, 512]) for b_ in range(2)] for m_ in range(2)]
        for m_ in range(2):
            for b_ in range(2):
                S.op('dve', 'memset', [], [QZ[m_][b_]], ap=QZ[m_][b_].ap, constant=0.0)
        KT = alloc("KT", [128, koff + T_])
        V = alloc("V", [128, nkb, 66])
        PTr = arot("PT", [128, 512], 3)
        mixtm = alloc("mixtm", [128, 4, 128])
        stgr = arot("stgq", [128, 512], 2)
        kst = arot("kst", [128, 128], 2)
        sm = alloc("sm", [128, 8])
        esink = alloc("esink", [128, 8])
        pS = Rot(P[4:6])
        S.dma('sp', esink.ap, swa_sink.ap[li].partition_broadcast(128), [swa_sink], [esink])
        S.act(esink, esink.ap, esink, esink.ap, AF.Exp)
        S.op('dve', 'memset', [], [V], ap=V.ap[:, :, 64:66], constant=1.0)
        trif = tri2_t.ap[:, 0:128]
        trib = tri2_t.ap[:, 256:384]
        for pc in range(4):
            kv = pc // 2
            if pc % 2 == 0:
                S.dma('sp', KT.ap[:, koff:koff + T_], st.FA.ap[4 + kv], st.FAb[4 + kv], [KT])
                S.dma('sp', V.ap[:, kb0:nkb, 0:64],
                      st.TA.ap[:, 1024 + kv * 64:1024 + (kv + 1) * 64].rearrange("(n t) d -> t n d", t=128),
                      st.TAb[2], [V])
                if st.sample:
                    S.dma('sp', V.ap[:, 0:2, 0:64], csv_i.ap[li, kv].rearrange("(n t) d -> t n d", t=128), [csv_i], [V])
                    for lb in range(2):
                        ks = kst.next()
                        for dup in range(2):
                            S.dma('sp', ks.ap[:, dup * 64:(dup + 1) * 64], csk_i.ap[li, kv, lb * 128:(lb + 1) * 128, :],
                                  [csk_i], [ks])
                        pb7 = pS.next()
                        S.transpose(pb7, pb7.ap[:, 0:128], ks, ks.ap, ident_t, ident_t.ap)
                        S.act(KT, KT.ap[:, lb * 128:(lb + 1) * 128], pb7, pb7.ap[:, 0:128], AF.Identity)
                        yield
            for qg in range((nq + 3) // 4):
                nqb = min(4, nq - 4 * qg)
                q0 = 4 * qg
                c0 = qg * 512
                ggs = list(range(c0 // G, (c0 + nqb * 128 + G - 1) // G))
                for m_ in range(2):
                    qz_ = QZ[m_][(pc * 4 + qg) % 2]
                    S.dma('sp', qz_.ap[m_ * 64:(m_ + 1) * 64, 0:nqb * 128],
                          st.FA.ap[pc, m_ * 64:(m_ + 1) * 64, c0:c0 + nqb * 128], [st.FAb[pc][gg] for gg in ggs], [qz_])
                for hh in range(2):
                    acc = P[0:nqb]
                    h = 2 * pc + hh
                    entries = []
                    if st.sample:
                        entries += [(kb, 0, nqb, {}) for kb in range(kb0)]
                        for j in range(q0 - 1, q0 + nqb + 1):
                            if j < 0 or j >= nq:
                                continue
                            qlo = max(j - 1, q0) - q0
                            qhi = min(j + 1, q0 + nqb - 1) - q0 + 1
                            masks = {}
                            if q0 <= j + 1 < q0 + nqb:
                                masks[j + 1 - q0] = trib
                            if q0 <= j - 1 < q0 + nqb:
                                masks[j - 1 - q0] = trif
                            entries.append((kb0 + j, qlo, qhi, masks))
                    else:
                        entries += [(kb, 0, nqb, {}) for kb in range(nkb)]
                    yield from run_attn(QZ[hh][(pc * 4 + qg) % 2], 0, nqb, entries, KT, V, 66, PTr, pS, acc)
                    for qb in range(nqb):
                        a = acc[qb]
                        tt('dve', sm, sm.ap[:, qb:qb + 1], a, a.ap[:, 64:65], esink, esink.ap[:, h:h + 1], ALU.add)
                        S.op('dve', 'reciprocal', [sm], [sm], out=sm.ap[:, 4 + qb:5 + qb], in_=sm.ap[:, qb:qb + 1])
                        ts('dve', mixtm, mixtm.ap[:, qb, hh * 64:(hh + 1) * 64], a, a.ap[:, 0:64], sm.ap[:, 4 + qb:5 + qb],
                           None, ALU.mult, reads=[sm])
                stg = stgr.next()
                for qb in range(nqb):
                    pb7 = pS.next()
                    S.transpose(pb7, pb7.ap[:, 0:128], mixtm, mixtm.ap[:, qb, :], ident_t, ident_t.ap)
                    S.act(stg, stg.ap[:, qb * 128:(qb + 1) * 128], pb7, pb7.ap[:, 0:128], AF.Identity)
                    yield
                c0 = qg * 512
                S.dma('pool', st.MIX.ap[pc, :, c0:c0 + nqb * 128], stg.ap[:, 0:nqb * 128], [stg],
                      [st.MIXb[pc][gg] for gg in range(c0 // G, (c0 + nqb * 128 + G - 1) // G)])

    def phaseC_layer(l):
        modt = modts[l % 2]
        li = l // 2
        Wout = w_out_even if l % 2 == 0 else w_out_odd
        new_phase()
        M8s = [alloc("M8_%d" % i, [128, 8, G]) for i in range(2)]
        zTs = [alloc("zT_%d" % i, [128, 8, G]) for i in range(2)]
        U = alloc("U", [128, 32, G])
        sq = alloc("sq", [128, 8, G])
        rstds = [alloc("rstd%d" % i, [128, G]) for i in range(2)]
        rr = arot("rr", [128, G], 2)
        pr = Rot(P[0:6])
        pm, pv = P[6], P[7]
        groups = [(st, gi) for st in streams for gi in range(st.ng)]
        n = len(groups)

        def xg(k):
            st, gi = groups[k]
            return xT[st.g0 + gi], st.c

        def partA(k):
            st, gi = groups[k]
            x, c = xg(k)
            t0 = gi * G
            M8 = M8s[k % 2]
            zT = zTs[0]
            S.dma('sp', M8.ap, st.MIX.ap[:, :, t0:t0 + G].rearrange("c p t -> p c t"),
                  [st.MIXb[cc][gi] for cc in range(8)], [M8])
            S.act(x, x.ap, x, x.ap, AF.Identity, scale=float(ALPHA))
            for ob in range(2):
                wb, wap = wload(Wout, Wout.ap[li, :, ob * 512:(ob + 1) * 512].rearrange("(k p) n -> p k n", p=128),
                                [128, 8, 512])
                for o4 in range(4):
                    oc = ob * 4 + o4
                    ps = pr.next()
                    for kc in range(8):
                        S.mm(ps, ps.ap[:, 0:G], wb, wap[:, kc, o4 * 128:(o4 + 1) * 128], M8, M8.ap[:, kc, :],
                             start=(kc == 0), stop=(kc == 7))
                    S.op('dve', 'scalar_tensor_tensor', [ps, modt, x], [zT], out=zT.ap[:, oc, :], in0=ps.ap[:, 0:G],
                         scalar=modt.ap[:, c, 16 + oc:17 + oc], in1=x.ap[:, oc, :], op0=ALU.mult, op1=ALU.add)

        def ln_gen(zT, rstd, dst, lc, post=None):
            for kc in range(8):
                S.mm(pm, pm.ap[:, 0:G], onesd_t, onesd_t.ap, zT, zT.ap[:, kc, :], start=(kc == 0), stop=(kc == 7))
            yield
            tt('dve', zT, zT.ap, zT, zT.ap, pm, pm.ap[:, 0:G].unsqueeze(1).broadcast_to([128, 8, G]), ALU.subtract)
            S.act(sq, sq.ap, zT, zT.ap, AF.Square)
            yield
            for kc in range(8):
                S.mm(pv, pv.ap[:, 0:G], onesd_t, onesd_t.ap, sq, sq.ap[:, kc, :], start=(kc == 0), stop=(kc == 7))
            yield
            S.act(rstd, rstd.ap, pv, pv.ap[:, 0:G], AF.Sqrt, bias=eps_t.ap[:, 0:1], reads=[eps_t])
            S.op('dve', 'reciprocal', [rstd], [rstd], out=rstd.ap, in_=rstd.ap)
            tt('dve', zT, zT.ap, zT, zT.ap, rstd, rstd.ap.unsqueeze(1).broadcast_to([128, 8, G]), ALU.mult)
            for kc in range(8):
                S.act(dst, dst.ap[:, kc, :], zT, zT.ap[:, kc, :], AF.Identity, scale=lng_t.ap[:, lc + kc:lc + kc + 1],
                      bias=lnb_t.ap[:, lc + kc:lc + kc + 1], reads=[lng_t, lnb_t])
            if post is not None:
                post()

        def step(gen):
            if gen is not None:
                next(gen, None)

        def finish(gen):
            if gen is not None:
                for _ in gen:
                    pass

        def make_ln1(k):
            x, c = xg(k)
            M8 = M8s[k % 2]

            def post():
                for kc in range(8):
                    S.act(M8, M8.ap[:, kc, :], x, x.ap[:, kc, :], AF.Identity, scale=modt.ap[:, c, 32 + kc:33 + kc],
                          bias=modt.ap[:, c, 24 + kc:25 + kc], reads=[modt])
            return ln_gen(zTs[0], rstds[0], x, (l * 2 + 0) * 8, post)

        def make_ln2(k):
            x, c = xg(k)
            return ln_gen(zTs[1], rstds[1], x, (l * 2 + 1) * 8)

        def ff1(k, filler):
            M8 = M8s[k % 2]
            for jb in range(8):
                wb, wap = wload(w_ff1, w_ff1.ap[l, :, jb * 512:(jb + 1) * 512].rearrange("(k p) n -> p k n", p=128),
                                [128, 8, 512])
                for jj in range(4):
                    j = jb * 4 + jj
                    ps = pr.next()
                    for kc in range(8):
                        S.mm(ps, ps.ap[:, 0:G], wb, wap[:, kc, jj * 128:(jj + 1) * 128], M8, M8.ap[:, kc, :],
                             start=(kc == 0), stop=(kc == 7))
                    r_ = rr.next()
                    S.act(r_, r_.ap, ps, ps.ap[:, 0:G], AF.Relu)
                    tt('pool', U, U.ap[:, j, :], r_, r_.ap, r_, r_.ap, ALU.mult)
                if jb % 2 == 0:
                    step(filler)

        def ff2(k, filler):
            x, c = xg(k)
            zT = zTs[1]
            for oc in range(8):
                wb, wap = wload(w_ff2, w_ff2.ap[l, :, oc * 128:(oc + 1) * 128].rearrange("(j p) n -> p j n", p=128),
                                [128, 32, 128])
                ps = pr.next()
                for j in range(32):
                    S.mm(ps, ps.ap[:, 0:G], wb, wap[:, j, :], U, U.ap[:, j, :], start=(j == 0), stop=(j == 31))
                S.op('dve', 'scalar_tensor_tensor', [ps, modt, x], [zT], out=zT.ap[:, oc, :], in0=ps.ap[:, 0:G],
                     scalar=modt.ap[:, c, 40 + oc:41 + oc], in1=x.ap[:, oc, :], op0=ALU.mult, op1=ALU.add)
                if oc % 2 == 0:
                    step(filler)

        last = (l == NL - 1)
        if last:
            orot = arot("xout", [128, 1024], 2)

        def store_group(k):
            st, gi = groups[k]
            x = xT[st.g0 + gi]
            for t2 in range(2):
                r0 = gi * G + t2 * 128
                ot_ = orot.next()
                for hf in range(2):
                    ps = pr.next()
                    for c4 in range(4):
                        cc = hf * 4 + c4
                        S.transpose(ps, ps.ap[:, c4 * 128:(c4 + 1) * 128], x, x.ap[:, cc, t2 * 128:(t2 + 1) * 128],
                                    ident_t, ident_t.ap)
                    S.act(ot_, ot_.ap[:, hf * 512:(hf + 1) * 512], ps, ps.ap, AF.Identity)
                if st.sample:
                    S.dma('pool', ys.ap[r0:r0 + 128, :], ot_.ap, [ot_], [ys])
                else:
                    S.dma('pool', yp.ap[st.pi, r0:r0 + 128, :], ot_.ap, [ot_], [yp])

        partA(0)
        finish(make_ln1(0))
        ln2_prev = None
        for k in range(n):
            ff1(k, ln2_prev)
            finish(ln2_prev)
            if last and k >= 1:
                store_group(k - 1)
            if k + 1 < n:
                partA(k + 1)
            x, c = xg(k)
            S.act(x, x.ap, x, x.ap, AF.Identity, scale=float(ALPHA))
            ln1_next = make_ln1(k + 1) if k + 1 < n else None
            ff2(k, ln1_next)
            finish(ln1_next)
            ln2_prev = make_ln2(k)
        finish(ln2_prev)
        if last:
            store_group(n - 1)

    import os
    stop = os.environ.get("KSTOP", "")
    if stop == "load":
        load_input()
    for l in range(NL):
        if stop == "load":
            break
        if l == 0:
            for _ in compute_mod_gen(0, P[6]):
                pass
        if stop == "mod":
            break
        for st in streams:
            phaseA(l, st)
        if stop == "A":
            break
        gm_ = compute_mod_gen(l + 1, P[3]) if l + 1 < NL else None
        for st in streams:
            new_phase()
            if l % 2 == 0:
                ga, gl_, ratio = phaseB_diff(l, st), phaseB_lin(l, st, 'ret'), 1
            else:
                ga, gl_, ratio = phaseB_swa(l, st), phaseB_lin(l, st, 'gla'), 5
            if os.environ.get("KSEQ"):
                for _ in ga:
                    pass
                ga = None
            while ga is not None or gl_ is not None:
                if ga is not None and next(ga, 'END') == 'END':
                    ga = None
                for _ in range(ratio):
                    if gl_ is not None and next(gl_, 'END') == 'END':
                        gl_ = None
                if gm_ is not None and not st.sample and next(gm_, 'END') == 'END':
                    gm_ = None
        if gm_ is not None:
            for _ in gm_:
                pass
        if stop in ("B", "B1"):
            break
        phaseC_layer(l)
    if stop:
        store_output()
    S.finish()
    return nc, S


OUT_NAMES = ["yp", "ys", "o_ret", "o_cdk", "o_cdv", "o_csk", "o_csv", "o_gla"]


def make_in_maps(inp, T=2048):
    hc = host_consts(T)
    oidx = odd_col_index()
    w_in_odd_x = np.ascontiguousarray(inp['w_in_odd'][:, :, oidx])
    maps = []
    f = lambda a: np.ascontiguousarray(np.asarray(a, dtype=np.float32))
    for i in range(8):
        b = i % 4
        m = dict(
            xs=f(inp['x_sample'][b]), xp=f(inp['x_prompt'][2 * i:2 * i + 2]),
            cond=f(np.stack([inp['c'][b], inp['c_ctx']])),
            st_ret=f(inp['state_ret'][b]), cdk_i=f(inp['cache_diff_k'][b]), cdv_i=f(inp['cache_diff_v'][b]),
            csk_i=f(inp['cache_swa_k'][b]), csv_i=f(inp['cache_swa_v'][b]), st_gla=f(inp['state_gla'][b]),
            w_mod=f(inp['w_mod']), b_mod=f(inp['b_mod']), ln_g=f(inp['ln_g']), ln_b=f(inp['ln_b']),
            w_in_even=f(inp['w_in_even']), w_out_even=f(inp['w_out_even']), ret_decay=f(inp['ret_decay']),
            diff_lam=f(np.reshape(inp['diff_lam'], (2, 256))), w_in_odd=w_in_odd_x, w_out_odd=f(inp['w_out_odd']),
            swa_sink=f(inp['swa_sink']), gla_w2=f(inp['gla_w2']), gla_b=f(inp['gla_b']),
            w_ff1=f(inp['w_ff1']), w_ff2=f(inp['w_ff2']), **hc)
        maps.append(m)
    return maps


_CACHE = {}


def kernel(**inputs):
    inp = {k: np.asarray(v) for k, v in inputs.items()}
    T = inp['x_sample'].shape[1]
    if T not in _CACHE:
        _CACHE[T] = build(T)[0]
    nc = _CACHE[T]
    maps = make_in_maps(inp, T)
    res = run_bass_kernel_spmd(nc, maps, core_ids=list(range(8)))
    R = res.results
    y_p = np.concatenate([R[i]['yp'] for i in range(8)], axis=0)
    y_s = np.stack([R[b]['ys'] for b in range(4)], axis=0)
    outs = [y_p, y_s]
    for nm in OUT_NAMES[2:]:
        outs.append(np.concatenate([R[i][nm] for i in range(8)], axis=0))
    return tuple(np.ascontiguousarray(o, dtype=np.float32) for o in outs)
```

```python
import numpy as np
from contextlib import ExitStack
import concourse.bass as bass
import concourse.mybir as mybir
from concourse.bass_utils import run_bass_kernel_spmd

F32 = mybir.dt.float32
F32R = mybir.dt.float32r
AF = mybir.ActivationFunctionType
ALU = mybir.AluOpType


class Buf:
    __slots__ = ("name", "ap", "w", "r")

    def __init__(self, name, ap):
        self.name = name
        self.ap = ap
        self.w = []
        self.r = []


class Sched:
    ENG = ('pe', 'act', 'dve', 'pool', 'sp')
    NS = 8

    def __init__(self, nc):
        self.nc = nc
        self.es = ExitStack()
        self.streams = {e: [] for e in self.ENG}
        self.cnt = {e: 0 for e in self.ENG}
        self.seen = {e: {} for e in self.ENG}
        self.dma_n = {}
        self.sems = {}
        self.nbuf = 0

    def dram_in(self, name, shape):
        return Buf(name, self.nc.dram_tensor(name, list(shape), F32, kind="ExternalInput").ap())

    def dram_out(self, name, shape):
        return Buf(name, self.nc.dram_tensor(name, list(shape), F32, kind="ExternalOutput").ap())

    def dram_scratch(self, name, shape):
        return Buf(name, self.nc.dram_tensor(name, list(shape), F32, kind="Internal").ap())

    def sbuf(self, name, shape):
        t = self.es.enter_context(self.nc.sbuf_tensor(name, list(shape), F32))
        return Buf(name, t[:])

    def psum(self, name):
        t = self.es.enter_context(self.nc.psum_tensor(name, [128, 512], F32))
        return Buf(name, t[:])

    def view(self, name, ap, olds=()):
        b = Buf(name, ap)
        for o in olds:
            b.r += o.w + o.r
        return b

    def _op(self, eng, fn, reads, writes, dma_q=None):
        deps = {}

        def add(tok):
            k, v = tok
            if deps.get(k, 0) < v:
                deps[k] = v
        for b in reads:
            for t in b.w:
                add(t)
        for b in writes:
            for t in b.w:
                add(t)
            for t in b.r:
                add(t)
        if dma_q is not None:
            n = self.dma_n.get(dma_q, 0)
            self.dma_n[dma_q] = n + 1
            slot, k = n % self.NS, n // self.NS
            key = ('d', dma_q, slot)
            if k > 0:
                add((key, 16 * k))
            tok = (key, 16 * (k + 1))
            inc = 16
        else:
            self.cnt[eng] += 1
            tok = (eng, self.cnt[eng])
            inc = 1
        seen = self.seen[eng]
        waits = []
        for k, v in deps.items():
            if eng == 'pe' and k == 'pe':
                continue
            if seen.get(k, 0) >= v:
                continue
            seen[k] = v
            waits.append((k, v))
        self.streams[eng].append((waits, fn, tok[0], inc))
        for b in writes:
            b.w = [tok]
            b.r = []
        for b in reads:
            if any(b is x for x in writes):
                continue
            b.r = [t for t in b.r if t[0] != tok[0]] + [tok]
        return tok

    def mm(self, pb, out_ap, lb, lhsT, rb, rhs, start=True, stop=True, r=False):
        if r:
            lhsT = lhsT.bitcast(F32R)
            rhs = rhs.bitcast(F32R)
        self._op('pe', lambda e: e.matmul(out_ap, lhsT, rhs, start=start, stop=stop), [lb, rb], [pb])

    def transpose(self, pb, out_ap, ib, in_ap, idb, id_ap):
        self._op('pe', lambda e: e.transpose(out_ap, in_ap, id_ap), [ib, idb], [pb])

    def act(self, ob, out_ap, ib, in_ap, func, scale=None, bias=None, reads=()):
        kw = {}
        if scale is not None:
            kw['scale'] = scale
        if bias is not None:
            kw['bias'] = bias
        self._op('act', lambda e: e.activation(out_ap, in_ap, func, **kw), [ib] + list(reads), [ob])

    def dve(self, fn, reads, writes):
        self._op('dve', fn, list(reads), list(writes))

    def pool(self, fn, reads, writes):
        self._op('pool', fn, list(reads), list(writes))

    def dma(self, q, out_ap, in_ap, reads, writes, **kw):
        self._op(q, lambda e: e.dma_start(out=out_ap, in_=in_ap, **kw), list(reads), list(writes), dma_q=q)

    def finish(self):
        nc = self.nc
        fw = []
        for q, n in self.dma_n.items():
            for slot in range(min(n, self.NS)):
                cnt = (n - slot + self.NS - 1) // self.NS
                fw.append((('d', q, slot), 16 * cnt))
        self.streams['sp'].append((fw, None, None, 0))
        keys = set()
        for e in self.ENG:
            for waits, fn, key, inc in self.streams[e]:
                if key is not None:
                    keys.add(key)
        for i, k in enumerate(sorted(keys, key=str)):
            self.sems[k] = self.es.enter_context(nc.semaphore("sem%d" % i))
        with nc.Block() as block:
            def mk(ename):
                def body(e):
                    for waits, fn, key, inc in self.streams[ename]:
                        for k, v in waits:
                            e.wait_ge(self.sems[k], v)
                        if fn is not None:
                            fn(e).then_inc(self.sems[key], inc)
                return body
            block.tensor(mk('pe'))
            block.scalar(mk('act'))
            block.vector(mk('dve'))
            block.gpsimd(mk('pool'))
            block.sync(mk('sp'))
        self.es.close()

    def op(self, eng, name, reads, writes, **kw):
        self._op(eng, lambda e: getattr(e, name)(**kw), list(reads), list(writes))


class Rot:
    def __init__(self, items):
        self.items = list(items)
        self.i = 0

    def next(self):
        x = self.items[self.i % len(self.items)]
        self.i += 1
        return x


AX = mybir.AxisListType.X
D = 1024
LCTX = 256
G = 256
ALPHA = 8.0 ** 0.25
EPS = 1e-5


def host_consts(T):
    ident = np.eye(128, dtype=np.float32)
    onesd = np.full((128, 128), 1.0 / 1024.0, dtype=np.float32)
    j = np.arange(128)[:, None]
    i = np.arange(128)[None, :]
    trif = (j <= i).astype(np.float32)
    trib = (j >= i).astype(np.float32)
    tri2 = np.concatenate([np.tile(trif, (1, 2)), np.tile(trib, (1, 2))], axis=1)
    rows = T // 64
    row = np.repeat(np.arange(rows), 64).astype(np.float32)
    col = np.tile(np.arange(64), rows).astype(np.float32)
    half = 32
    freqs = (np.float32(10000.0) ** (-np.arange(0, half, 2, dtype=np.float32) / np.float32(half))).astype(np.float32)
    ar = (row[:, None] * freqs).astype(np.float32)
    ac = (col[:, None] * freqs).astype(np.float32)
    cosT = np.zeros((64, T), np.float32)
    sinT = np.zeros((64, T), np.float32)
    perm = np.zeros((64, 64), np.float32)
    for d in range(64):
        blk, e = d // 32, d % 32
        ang = ar if blk == 0 else ac
        f = e % 16
        cosT[d] = np.cos(ang[:, f])
        sinT[d] = (-np.sin(ang[:, f])) if e < 16 else np.sin(ang[:, f])
        partner = d + 16 if e < 16 else d - 16
        perm[partner, d] = 1.0
    cos2 = np.concatenate([cosT, cosT], axis=0)
    sin2 = np.concatenate([sinT, sinT], axis=0)
    perm2 = np.zeros((128, 128), np.float32)
    perm2[:64, :64] = perm
    perm2[64:, 64:] = perm
    return dict(ident=ident, onesd=onesd, tri2=tri2, cos2=cos2, sin2=sin2, perm2=perm2)


ODD_W = 2464


def odd_col_index():
    sq = list(range(0, 512))
    sk = list(range(512, 640))
    sv = list(range(640, 768))
    gq = list(range(768, 1024))
    gk = list(range(1024, 1280))
    gv = list(range(1280, 1792))
    gr = list(range(1792, 2304))
    glr = list(range(2304, 2336))
    skdup = sk[0:64] + sk[0:64] + sk[64:128] + sk[64:128]
    idx = sq + skdup + gq + gk + glr + sv + gv + gr
    assert len(idx) == ODD_W
    return np.array(idx)


import math


def build(T=2048, NL=4, AR=22528):
    nc = bass.Bass("TRN2", target_bir_lowering=False)
    S = Sched(nc)
    NGS = T // G
    NGRP = NGS + 2
    xs = S.dram_in("xs", [T, D])
    xp = S.dram_in("xp", [2, 256, D])
    cond = S.dram_in("cond", [2, D])
    st_ret = S.dram_in("st_ret", [2, 2, 4, 64, 128])
    cdk_i = S.dram_in("cdk_i", [2, 4, 2, 256, 64])
    cdv_i = S.dram_in("cdv_i", [2, 4, 256, 128])
    csk_i = S.dram_in("csk_i", [2, 2, 256, 64])
    csv_i = S.dram_in("csv_i", [2, 2, 256, 64])
    st_gla = S.dram_in("st_gla", [2, 2, 4, 64, 128])
    w_mod = S.dram_in("w_mod", [4, D, 6144])
    b_mod = S.dram_in("b_mod", [4, 6144])
    ln_g = S.dram_in("ln_g", [4, 2, D])
    ln_b = S.dram_in("ln_b", [4, 2, D])
    w_in_even = S.dram_in("w_in_even", [2, D, 3072])
    w_out_even = S.dram_in("w_out_even", [2, D, D])
    ret_decay = S.dram_in("ret_decay", [2, 2, 4])
    diff_lam = S.dram_in("diff_lam", [2, 256])
    w_in_odd = S.dram_in("w_in_odd", [2, D, ODD_W])
    w_out_odd = S.dram_in("w_out_odd", [2, D, D])
    swa_sink = S.dram_in("swa_sink", [2, 8])
    gla_w2 = S.dram_in("gla_w2", [2, 2, 16, 256])
    gla_b = S.dram_in("gla_b", [2, 2, 256])
    w_ff1 = S.dram_in("w_ff1", [4, D, 4096])
    w_ff2 = S.dram_in("w_ff2", [4, 4096, D])
    c_ident = S.dram_in("ident", [128, 128])
    c_onesd = S.dram_in("onesd", [128, 128])
    c_tri2 = S.dram_in("tri2", [128, 512])
    c_cos = S.dram_in("cos2", [128, T])
    c_sin = S.dram_in("sin2", [128, T])
    c_perm = S.dram_in("perm2", [128, 128])
    yp = S.dram_out("yp", [2, 256, D])
    ys = S.dram_out("ys", [T, D])
    o_ret = S.dram_out("o_ret", [2, 2, 2, 4, 64, 128])
    o_cdk = S.dram_out("o_cdk", [2, 2, 4, 2, 256, 64])
    o_cdv = S.dram_out("o_cdv", [2, 2, 4, 256, 128])
    o_csk = S.dram_out("o_csk", [2, 2, 2, 256, 64])
    o_csv = S.dram_out("o_csv", [2, 2, 2, 256, 64])
    o_gla = S.dram_out("o_gla", [2, 2, 2, 4, 64, 128])

    class St:
        pass
    streams = []
    for i in range(3):
        st = St()
        st.i = i
        st.sample = (i == 0)
        st.T = T if i == 0 else 256
        st.c = 0 if i == 0 else 1
        st.g0 = 0 if i == 0 else NGS + (i - 1)
        st.ng = st.T // G
        st.pi = i - 1
        st.koff = 256 if i == 0 else 0
        st.FA = S.dram_scratch("FA%d" % i, [12, 128, st.T])
        st.TA = S.dram_scratch("TA%d" % i, [st.T, 1536])
        st.MIX = S.dram_scratch("MIX%d" % i, [8, 128, st.T])
        st.FAb = [[Buf("FA", None) for _ in range(st.ng)] for _ in range(12)]
        st.TAb = [[Buf("TA", None) for _ in range(st.ng)] for _ in range(3)]
        st.MIXb = [[Buf("MIX", None) for _ in range(st.ng)] for _ in range(8)]
        streams.append(st)

    xT = [S.sbuf("xT%d" % g, [128, 8, G]) for g in range(NGRP)]
    WB = [S.sbuf("wb%d" % i, [128, 4096]) for i in range(2)]
    wrot = Rot(WB)
    arena = S.sbuf("arena", [128, AR])
    ident_t = S.sbuf("ident_t", [128, 128])
    onesd_t = S.sbuf("onesd_t", [128, 128])
    tri2_t = S.sbuf("tri2_t", [128, 512])
    perm_t = S.sbuf("perm_t", [128, 128])
    ones_t = S.sbuf("ones_t", [128, 128])
    eps_t = S.sbuf("eps_t", [128, 1])
    stg0 = S.sbuf("stg0", [128, 128])
    sc_t = S.sbuf("sc_t", [128, 8, 2])
    modts = [S.sbuf("modt%d" % i, [128, 2, 48]) for i in range(2)]
    bm_t = S.sbuf("bm_t", [128, 48])
    lng_t = S.sbuf("lng_t", [128, 64])
    lnb_t = S.sbuf("lnb_t", [128, 64])
    P = [S.psum("ps%d" % i) for i in range(8)]

    ar = dict(off=0, views=[], summ={})

    def new_phase():
        for v in ar['views']:
            for k, val in v.w + v.r:
                if ar['summ'].get(k, 0) < val:
                    ar['summ'][k] = val
        ar['views'] = []
        ar['off'] = 0

    def alloc(name, shape):
        n = 1
        for s_ in shape[1:]:
            n *= s_
        off = ar['off']
        ar['off'] += n
        assert ar['off'] <= AR, (name, ar['off'])
        ap = arena.ap[0:shape[0], off:off + n]
        if len(shape) == 3:
            ap = ap.rearrange("p (a b) -> p a b", a=shape[1])
        b = Buf(name, ap)
        b.r = list(ar['summ'].items())
        ar['views'].append(b)
        return b

    def arot(name, shape, n):
        return Rot([alloc("%s%d" % (name, i), shape) for i in range(n)])

    def wload(src_buf, src_ap, shape):
        wb = wrot.next()
        n = 1
        for s_ in shape[1:]:
            n *= s_
        ap = wb.ap[:, 0:n]
        if len(shape) == 3:
            ap = ap.rearrange("p (a b) -> p a b", a=shape[1])
        S.dma('sp', ap, src_ap, [src_buf], [wb])
        return wb, ap

    def tt(eng, ob, out, ab, a, bb, b, op):
        S.op(eng, 'tensor_tensor', [ab, bb], [ob], out=out, in0=a, in1=b, op=op)

    def ts(eng, ob, out, ib, in0, s1, s2=None, op0=ALU.mult, op1=None, reads=()):
        kw = dict(out=out, in0=in0, scalar1=s1, scalar2=s2, op0=op0)
        if op1 is not None:
            kw['op1'] = op1
        S.op(eng, 'tensor_scalar', [ib] + list(reads), [ob], **kw)

    def load_cols(dst, dst_ap, src, src_rows_ap, R, func=AF.Identity):
        S.dma('sp', stg0.ap[0:R, :], src_rows_ap, [src], [stg0])
        S.transpose(P[7], P[7].ap[:, 0:R], stg0, stg0.ap[0:R, :], ident_t, ident_t.ap[0:R, 0:R])
        S.act(dst, dst_ap, P[7], P[7].ap[:, 0:R], func)

    for tb, cb in ((ident_t, c_ident), (onesd_t, c_onesd), (tri2_t, c_tri2), (perm_t, c_perm)):
        S.dma('sp', tb.ap, cb.ap, [cb], [tb])
    S.op('dve', 'memset', [], [ones_t], ap=ones_t.ap, constant=1.0)
    S.op('dve', 'memset', [], [eps_t], ap=eps_t.ap, constant=EPS)
    for c in range(2):
        load_cols(sc_t, sc_t.ap[:, :, c], cond, cond.ap[c].rearrange("(k p) -> k p", p=128), 8, AF.Silu)
    load_cols(lng_t, lng_t.ap, ln_g, ln_g.ap.rearrange("l i (k p) -> (l i k) p", p=128), 64)
    load_cols(lnb_t, lnb_t.ap, ln_b, ln_b.ap.rearrange("l i (k p) -> (l i k) p", p=128), 64)

    def load_input():
        new_phase()
        xr = arot("xin", [128, 1024], 2)
        pr = Rot(P[0:6])
        for st in streams:
            for gi in range(st.ng):
                g = st.g0 + gi
                for t2 in range(2):
                    r0 = gi * G + t2 * 128
                    src = xs.ap[r0:r0 + 128, :] if st.sample else xp.ap[st.pi, r0:r0 + 128, :]
                    sb = xs if st.sample else xp
                    xt_ = xr.next()
                    S.dma('sp', xt_.ap, src, [sb], [xt_])
                    for hf in range(2):
                        ps = pr.next()
                        for c4 in range(4):
                            cc = hf * 4 + c4
                            S.transpose(ps, ps.ap[:, c4 * 128:(c4 + 1) * 128], xt_, xt_.ap[:, cc * 128:(cc + 1) * 128],
                                        ident_t, ident_t.ap)
                        S.act(xT[g], xT[g].ap[:, hf * 4:(hf + 1) * 4, t2 * 128:(t2 + 1) * 128], ps,
                              ps.ap.rearrange("p (c t) -> p c t", c=4), AF.Identity)

    def store_output():
        new_phase()
        orot = arot("xout", [128, 1024], 2)
        pr = Rot(P[0:6])
        for st in streams:
            for gi in range(st.ng):
                g = st.g0 + gi
                for t2 in range(2):
                    r0 = gi * G + t2 * 128
                    ot_ = orot.next()
                    for hf in range(2):
                        ps = pr.next()
                        for c4 in range(4):
                            cc = hf * 4 + c4
                            S.transpose(ps, ps.ap[:, c4 * 128:(c4 + 1) * 128], xT[g],
                                        xT[g].ap[:, cc, t2 * 128:(t2 + 1) * 128], ident_t, ident_t.ap)
                        S.act(ot_, ot_.ap[:, hf * 512:(hf + 1) * 512], ps, ps.ap, AF.Identity)
                    if st.sample:
                        S.dma('pool', ys.ap[r0:r0 + 128, :], ot_.ap, [ot_], [ys])
                    else:
                        S.dma('pool', yp.ap[st.pi, r0:r0 + 128, :], ot_.ap, [ot_], [yp])

    def compute_mod_gen(l, pm_):
        modt = modts[l % 2]
        S.dma('sp', stg0.ap[0:48, :], b_mod.ap[l].rearrange("(j p) -> j p", p=128), [b_mod], [stg0])
        S.transpose(pm_, pm_.ap[:, 0:48], stg0, stg0.ap[0:48, :], ident_t, ident_t.ap[0:48, 0:48])
        S.act(bm_t, bm_t.ap, pm_, pm_.ap[:, 0:48], AF.Identity)
        yield
        for blk in range(12):
            wb, wap = wload(w_mod, w_mod.ap[l, :, blk * 512:(blk + 1) * 512].rearrange("(k p) n -> p k n", p=128),
                            [128, 8, 512])
            for e4 in range(4):
                ec = blk * 4 + e4
                for kc in range(8):
                    S.mm(pm_, pm_.ap[:, 2 * ec:2 * ec + 2], wb, wap[:, kc, e4 * 128:(e4 + 1) * 128], sc_t,
                         sc_t.ap[:, kc, :], start=(kc == 0), stop=(kc == 7))
                yield
        pmv = pm_.ap[:, 0:96].rearrange("p (e c) -> p e c", c=2)
        for c in range(2):
            tt('dve', modt, modt.ap[:, c, :], pm_, pmv[:, :, c], bm_t, bm_t.ap, ALU.add)
        for lo in (8, 32):
            ts('dve', modt, modt.ap[:, :, lo:lo + 8], modt, modt.ap[:, :, lo:lo + 8], 1.0, None, ALU.add)

    def phaseA(l, st):
        modt = modts[l % 2]
        even = (l % 2 == 0)
        li = l // 2
        new_phase()
        hTs = [alloc("hT%d" % i, [128, 8, G]) for i in range(2)]
        sfm = arot("sfm", [128, G], 4)
        stm = arot("stm", [128, 512], 3)
        cosg = arot("cosg", [128, G], 2)
        sing = arot("sing", [128, G], 2)
        t1r = arot("t1r", [128, G], 2)
        t2r = arot("t2r", [128, G], 2)
        pr = Rot(P[0:6])
        pp = Rot(P[6:8])
        c = st.c
        W = w_in_even if even else w_in_odd
        if even:
            blocks = [
                (0, 512, [('fm', 0, 128, 0, 1.0, False), ('fm', 128, 128, 1, 1.0, False),
                          ('fm', 256, 128, 2, 0.125, False), ('fm', 384, 128, 3, 0.125, False)]),
                (512, 512, [('tm', 0, 512, 0, 0, AF.Identity, None)]),
                (1024, 512, [('tm', 0, 512, 1, 0, AF.Silu, None)]),
                (1536, 512, [('fm', h * 128, 128, 4 + h, 1.0, True) for h in range(4)]),
                (2048, 512, [('fm', h * 128, 128, 8 + h, 1.0, True) for h in range(4)] +
                 ([] if st.sample else [('tm', 0, 512, None, 0, AF.Identity, 'cdk')])),
                (2560, 512, [('tm', 0, 512, 2, 0, AF.Identity, None if st.sample else 'cdv')]),
            ]
        else:
            blocks = [
                (0, 512, [('fm', p_ * 128, 128, p_, 1.0, True) for p_ in range(4)]),
                (512, 512, [('fm', 0, 128, 4, 1.0, True), ('fm', 128, 128, 5, 1.0, True),
                            ('fm', 256, 128, 6, 0.125, False), ('fm', 384, 128, 7, 0.125, False)] +
                 ([] if st.sample else [('tmk',)])),
                (1024, 416, [('fm', 0, 128, 8, 1.0, False), ('fm', 128, 128, 9, 1.0, False),
                             ('fm', 256, 16, 10, 1.0, False), ('fm', 272, 16, 11, 1.0, False),
                             ('tm', 288, 128, 2, 0, AF.Identity, None if st.sample else 'csv')]),
                (1440, 512, [('tm', 0, 512, 0, 0, AF.Identity, None)]),
                (1952, 512, [('tm', 0, 512, 1, 0, AF.Silu, None)]),
            ]
        pend_rope = [None]

        def flush_rope():
            if pend_rope[0] is not None:
                f_ = pend_rope[0]
                pend_rope[0] = None
                f_()
        if l == 0:
            xr = arot("xin", [128, 1024], 2)

        def load_group(gi_):
            g_ = st.g0 + gi_
            for t2 in range(2):
                r0 = gi_ * G + t2 * 128
                src = xs.ap[r0:r0 + 128, :] if st.sample else xp.ap[st.pi, r0:r0 + 128, :]
                sb = xs if st.sample else xp
                xt_ = xr.next()
                S.dma('sp', xt_.ap, src, [sb], [xt_])
                for hf in range(2):
                    ps = pr.next()
                    for c4 in range(4):
                        cc = hf * 4 + c4
                        S.transpose(ps, ps.ap[:, c4 * 128:(c4 + 1) * 128], xt_, xt_.ap[:, cc * 128:(cc + 1) * 128],
                                    ident_t, ident_t.ap)
                    S.act(xT[g_], xT[g_].ap[:, hf * 4:(hf + 1) * 4, t2 * 128:(t2 + 1) * 128], ps,
                          ps.ap.rearrange("p (c t) -> p c t", c=4), AF.Identity)
        if l == 0:
            load_group(0)
        for gi in range(st.ng):
            g = st.g0 + gi
            t0 = gi * G
            h = hTs[gi % 2]
            for kc in range(8):
                S.act(h, h.ap[:, kc, :], xT[g], xT[g].ap[:, kc, :], AF.Identity,
                      scale=modt.ap[:, c, 8 + kc:9 + kc], bias=modt.ap[:, c, kc:kc + 1], reads=[modt])
            if l == 0 and gi + 1 < st.ng:
                load_group(gi + 1)
            if st.sample:
                cg = cosg.next()
                sg_ = sing.next()
                S.dma('sp', cg.ap, c_cos.ap[:, t0:t0 + G], [c_cos], [cg])
                S.dma('sp', sg_.ap, c_sin.ap[:, t0:t0 + G], [c_sin], [sg_])
            for (wc0, wn, jobs) in blocks:
                wb, wap = wload(W, W.ap[li, :, wc0:wc0 + wn].rearrange("(k p) n -> p k n", p=128), [128, 8, wn])
                for job in jobs:
                    if job[0] == 'fm':
                        _, c0, M, slot, scale, rope = job
                        ps = pr.next()
                        for kc in range(8):
                            S.mm(ps, ps.ap[0:M, 0:G], wb, wap[:, kc, c0:c0 + M], h, h.ap[:, kc, :],
                                 start=(kc == 0), stop=(kc == 7))
                        flush_rope()
                        sg = sfm.next()
                        S.act(sg, sg.ap[0:M, :], ps, ps.ap[0:M, 0:G], AF.Identity, scale=float(scale))
                        if rope and st.sample:
                            def rope_tail(sg=sg, slot=slot, M=M, cg=cg, sg_=sg_, gi=gi, t0=t0):
                                pq = pp.next()
                                S.mm(pq, pq.ap[:, 0:G], perm_t, perm_t.ap, sg, sg.ap)
                                t1 = t1r.next()
                                t2 = t2r.next()
                                tt('dve', t1, t1.ap, sg, sg.ap, cg, cg.ap, ALU.mult)
                                tt('dve', t2, t2.ap, pq, pq.ap[:, 0:G], sg_, sg_.ap, ALU.mult)
                                sg2 = sfm.next()
                                tt('pool', sg2, sg2.ap, t1, t1.ap, t2, t2.ap, ALU.add)
                                S.dma('pool', st.FA.ap[slot, 0:M, t0:t0 + G], sg2.ap[0:M, :], [sg2], [st.FAb[slot][gi]])
                            pend_rope[0] = rope_tail
                        else:
                            S.dma('pool', st.FA.ap[slot, 0:M, t0:t0 + G], sg.ap[0:M, :], [sg], [st.FAb[slot][gi]])
                    elif job[0] == 'tm':
                        _, c0, N, tblk, _, func, outk = job
                        for t2_ in range(2):
                            ps = pr.next()
                            for kc in range(8):
                                S.mm(ps, ps.ap[:, 0:N], h, h.ap[:, kc, t2_ * 128:(t2_ + 1) * 128], wb,
                                     wap[:, kc, c0:c0 + N], start=(kc == 0), stop=(kc == 7))
                            flush_rope()
                            sg = stm.next()
                            S.act(sg, sg.ap[:, 0:N], ps, ps.ap[:, 0:N], func)
                            r0 = t0 + t2_ * 128
                            if tblk is not None:
                                dc = {0: 0, 1: 512, 2: 1024}[tblk]
                                S.dma('pool', st.TA.ap[r0:r0 + 128, dc:dc + N], sg.ap[:, 0:N], [sg],
                                      [st.TAb[tblk][gi]])
                            if outk == 'cdk':
                                S.dma('pool', o_cdk.ap[st.pi, li].rearrange("h m t d -> t (h m) d")[r0:r0 + 128, :, :],
                                      sg.ap[:, 0:512].rearrange("t (a d) -> t a d", d=64), [sg], [o_cdk])
                            elif outk == 'cdv':
                                S.dma('pool', o_cdv.ap[st.pi, li].rearrange("h t d -> t h d")[r0:r0 + 128, :, :],
                                      sg.ap[:, 0:512].rearrange("t (a d) -> t a d", d=128), [sg], [o_cdv])
                            elif outk == 'csv':
                                S.dma('pool', o_csv.ap[st.pi, li].rearrange("k t d -> t k d")[r0:r0 + 128, :, :],
                                      sg.ap[:, 0:128].rearrange("t (a d) -> t a d", d=64), [sg], [o_csv])
                    else:
                        for t2_ in range(2):
                            ps = pr.next()
                            for kv in range(2):
                                for kc in range(8):
                                    S.mm(ps, ps.ap[:, kv * 64:(kv + 1) * 64], h, h.ap[:, kc, t2_ * 128:(t2_ + 1) * 128],
                                         wb, wap[:, kc, kv * 128:kv * 128 + 64], start=(kc == 0), stop=(kc == 7))
                            sg = stm.next()
                            S.act(sg, sg.ap[:, 0:128], ps, ps.ap[:, 0:128], AF.Identity)
                            r0 = t0 + t2_ * 128
                            S.dma('pool', o_csk.ap[st.pi, li].rearrange("k t d -> t k d")[r0:r0 + 128, :, :],
                                  sg.ap[:, 0:128].rearrange("t (a d) -> t a d", d=64), [sg], [o_csk])

        flush_rope()

    def head_norm(o_t, o_n, stt, stt2, h):
        hs = slice(h * 128, (h + 1) * 128)
        S.op('dve', 'bn_stats', [o_t], [stt], out=stt.ap[:, h * 8:h * 8 + 6], in_=o_t.ap[:, hs])
        S.op('dve', 'bn_aggr', [stt], [stt], out=stt.ap[:, h * 8 + 6:h * 8 + 8], in_=stt.ap[:, h * 8:h * 8 + 6])
        S.act(stt2, stt2.ap[:, h:h + 1], stt, stt.ap[:, h * 8 + 7:h * 8 + 8], AF.Sqrt, bias=eps_t.ap[:, 0:1],
              reads=[eps_t])
        S.op('dve', 'reciprocal', [stt2], [stt2], out=stt2.ap[:, 2 + h:3 + h], in_=stt2.ap[:, h:h + 1])
        ts('dve', o_n, o_n.ap[:, hs], o_t, o_t.ap[:, hs], stt.ap[:, h * 8 + 6:h * 8 + 7], stt2.ap[:, 2 + h:3 + h],
           ALU.subtract, ALU.mult, reads=[stt, stt2])

    def phaseB_lin(l, st, kind):
        li = l // 2
        T_ = st.T
        nch = T_ // 128
        ret = (kind == 'ret')
        qslot, kslot = (0, 2) if ret else (6, 8)
        mix0 = 0 if ret else 4
        st_in = st_ret if ret else st_gla
        o_st = o_ret if ret else o_gla
        sc = 1.0 if ret else 1.0 / 16.0
        lin_tiles = {}
        for pc in range(2):
            if pc == 0:
                lin_tiles['Of'] = [alloc("Of%d" % n_, [128, 256]) for n_ in range(nch)]
                lin_tiles['Sst'] = alloc("Sst", [128, 128])
                lin_tiles['rots'] = dict(
                    QTc=arot("QTc", [128, 128], 3), KTc=arot("KTc", [128, 128], 3), Vc=arot("Vc", [128, 256], 2),
                    Gc=arot("Gc", [128, 256], 2), EQr=arot("EQ", [128, 128], 2), EKr=arot("EK", [128, 128], 2),
                    qzr=[arot("qz%d" % h_, [128, 128], 2) for h_ in range(2)], kdr=arot("kd", [128, 128], 2),
                    ktr=arot("kt", [128, 128], 2), WTr=arot("WT", [128, 256], 2), Usr=arot("Us", [128, 256], 2),
                    otr=arot("ot", [128, 256], 2), onr=arot("on", [128, 256], 2), mfr=arot("mf", [128, 128], 2),
                    lapr=arot("lap", [128, 128], 2), e1r=arot("e1", [128, 128], 2), glr_r=arot("glrc", [17, 128], 2),
                    stt=alloc("stt", [128, 16]), stt2=alloc("stt2", [128, 4]), rd=alloc("rd", [128, 4]),
                    w2a=[alloc("w2a%d" % d, [17, 128]) for d in range(2)])
            Of = lin_tiles['Of']
            Sst = lin_tiles['Sst']
            R_ = lin_tiles['rots']
            QTc, KTc, Vc = R_['QTc'], R_['KTc'], R_['Vc']
            Gc, EQr, EKr, qzr, kdr, ktr = R_['Gc'], R_['EQr'], R_['EKr'], R_['qzr'], R_['kdr'], R_['ktr']
            WTr, Usr, otr, onr, mfr = R_['WTr'], R_['Usr'], R_['otr'], R_['onr'], R_['mfr']
            lapr, e1r, glr_r, stt, stt2, rd, w2a = R_['lapr'], R_['e1r'], R_['glr_r'], R_['stt'], R_['stt2'], R_['rd'], R_['w2a']
            if pc == 0:
                for h_ in range(2):
                    for b_ in qzr[h_].items:
                        S.op('dve', 'memset', [], [b_], ap=b_.ap, constant=0.0)
            pA = Rot([P[6]])
            pO = Rot([P[7]])
            if not ret:
                if pc == 0:
                    for b_ in glr_r.items:
                        S.op('dve', 'memset', [], [b_], ap=b_.ap, constant=1.0)
                for d in range(2):
                    S.dma('sp', w2a[d].ap[0:16, :], gla_w2.ap[li, d, :, pc * 128:(pc + 1) * 128], [gla_w2], [w2a[d]])
                    S.dma('sp', w2a[d].ap[16:17, :], gla_b.ap[li, d:d + 1, pc * 128:(pc + 1) * 128], [gla_b], [w2a[d]])
            else:
                for d in range(2):
                    S.dma('sp', rd.ap[:, d * 2:d * 2 + 2], ret_decay.ap[li, d, 2 * pc:2 * pc + 2].partition_broadcast(128),
                          [ret_decay], [rd])
                S.act(rd, rd.ap, rd, rd.ap, AF.Exp)
            for d in range(2):
                tri_d = tri2_t.ap[:, d * 256:d * 256 + 128]
                tri_d2 = tri2_t.ap[:, d * 256:(d + 1) * 256]
                deccol = 127 if d == 0 else 0
                if st.sample:
                    for h in range(2):
                        S.dma('sp', Sst.ap[h * 64:(h + 1) * 64, :], st_in.ap[li, d, 2 * pc + h], [st_in], [Sst])
                else:
                    S.op('dve', 'memset', [], [Sst], ap=Sst.ap, constant=0.0)

                def decay_tiles(lap):
                    pb_ = pA.next()
                    S.mm(pb_, pb_.ap[:, 0:128], lap, lap.ap, tri2_t, tri_d)
                    EQ = EQr.next()
                    EK = EKr.next()
                    S.act(EQ, EQ.ap, pb_, pb_.ap[:, 0:128], AF.Exp, scale=-sc)
                    S.act(EK, EK.ap, pb_, pb_.ap[:, 0:128], AF.Exp, scale=sc)
                    return EQ, EK
                const_dec = None
                if ret:
                    lap = lapr.next()
                    for h in range(2):
                        ts('dve', lap, lap.ap[:, h * 64:(h + 1) * 64], ones_t, ones_t.ap[:, 0:64],
                           rd.ap[:, d * 2 + h:d * 2 + h + 1], None, ALU.mult, reads=[rd])
                    const_dec = decay_tiles(lap)

                def front(n, out):
                    cs = slice(n * 128, (n + 1) * 128)
                    gi = n // 2
                    qc = QTc.next()
                    kc_ = KTc.next()
                    vc = Vc.next()
                    S.dma('sp', qc.ap, st.FA.ap[qslot + pc, :, cs], [st.FAb[qslot + pc][gi]], [qc])
                    S.dma('sp', kc_.ap, st.FA.ap[kslot + pc, :, cs], [st.FAb[kslot + pc][gi]], [kc_])
                    S.dma('sp', vc.ap, st.TA.ap[cs, pc * 256:(pc + 1) * 256], [st.TAb[0][gi]], [vc])
                    if not ret:
                        gl = glr_r.next()
                        S.dma('sp', gl.ap[0:16, :], st.FA.ap[10 + d, 0:16, cs], [st.FAb[10 + d][gi]], [gl])
                        yield
                        pz = pA.next()
                        S.mm(pz, pz.ap[:, 0:128], gl, gl.ap[0:17, :], w2a[d], w2a[d].ap[0:17, :])
                        yield
                        e1 = e1r.next()
                        S.act(e1, e1.ap, pz, pz.ap[:, 0:128], AF.Exp, scale=-1.0)
                        yield
                        lap_ = lapr.next()
                        S.act(lap_, lap_.ap, e1, e1.ap, AF.Ln, bias=ones_t.ap[:, 0:1], reads=[ones_t])
                        yield
                        EQ, EK = decay_tiles(lap_)
                    else:
                        EQ, EK = const_dec
                    yield
                    kd = kdr.next()
                    qz = [qzr[0].next(), qzr[1].next()]
                    for h in range(2):
                        rs = slice(h * 64, (h + 1) * 64)
                        tt('dve', qz[h], qz[h].ap[rs, :], qc, qc.ap[rs, :], EQ, EQ.ap[rs, :], ALU.mult)
                    tt('pool', kd, kd.ap, kc_, kc_.ap, EK, EK.ap, ALU.mult)
                    yield
                    pk = pA.next()
                    S.transpose(pk, pk.ap[:, 0:128], kd, kd.ap, ident_t, ident_t.ap)
                    yield
                    kt = ktr.next()
                    S.act(kt, kt.ap, pk, pk.ap[:, 0:128], AF.Identity)
                    yield
                    pa = pA.next()
                    for h in range(2):
                        S.mm(pa, pa.ap[:, h * 128:(h + 1) * 128], kd, kd.ap, qz[h], qz[h].ap)
                    yield
                    wt = WTr.next()
                    tt('dve', wt, wt.ap, pa, pa.ap[:, 0:256], tri2_t, tri_d2, ALU.mult)
                    yield
                    pu = pA.next()
                    S.mm(pu, pu.ap[:, 0:256], kt, kt.ap, vc, vc.ap)
                    yield
                    us = Usr.next()
                    S.act(us, us.ap, pu, pu.ap[:, 0:256], AF.Identity)
                    out.update(dict(EQ=EQ, qz=qz, wt=wt, vc=vc, us=us))

                def back(n, tl, first_pass):
                    cs = slice(n * 128, (n + 1) * 128)
                    gi = n // 2
                    EQ, qz, wt, vc, us = tl['EQ'], tl['qz'], tl['wt'], tl['vc'], tl['us']
                    po = pO.next()
                    for h in range(2):
                        hs = slice(h * 128, (h + 1) * 128)
                        S.mm(po, po.ap[:, hs], wt, wt.ap[:, hs], vc, vc.ap[:, hs], start=True, stop=False)
                        S.mm(po, po.ap[:, hs], qz[h], qz[h].ap, Sst, Sst.ap, start=False, stop=True)
                    yield
                    for h in range(2):
                        rs = slice(h * 64, (h + 1) * 64)
                        tt('dve', Sst, Sst.ap[rs, :], Sst, Sst.ap[rs, :], us, us.ap[rs, h * 128:(h + 1) * 128], ALU.add)
                        ts('dve', Sst, Sst.ap[rs, :], Sst, Sst.ap[rs, :], EQ.ap[rs, deccol:deccol + 1], None, ALU.mult,
                           reads=[EQ])
                    yield
                    if first_pass:
                        S.act(Of[n], Of[n].ap, po, po.ap[:, 0:256], AF.Identity)
                    else:
                        o_t = otr.next()
                        o_n = onr.next()
                        tt('dve', o_t, o_t.ap, po, po.ap[:, 0:256], Of[n], Of[n].ap, ALU.add)
                        gc = Gc.next()
                        S.dma('sp', gc.ap, st.TA.ap[cs, 512 + pc * 256:512 + (pc + 1) * 256], [st.TAb[1][gi]], [gc])
                        yield
                        for h in range(2):
                            head_norm(o_t, o_n, stt, stt2, h)
                            yield
                        tt('pool', o_n, o_n.ap, o_n, o_n.ap, gc, gc.ap, ALU.mult)
                        yield
                        for h in range(2):
                            pt_ = pO.next()
                            S.transpose(pt_, pt_.ap[:, 0:128], o_n, o_n.ap[:, h * 128:(h + 1) * 128], ident_t, ident_t.ap)
                            mf = mfr.next()
                            S.act(mf, mf.ap, pt_, pt_.ap[:, 0:128], AF.Identity)
                            ch = mix0 + 2 * pc + h
                            S.dma('pool', st.MIX.ap[ch, :, cs], mf.ap, [mf], [st.MIXb[ch][gi]])
                            yield

                order = list(range(nch)) if d == 0 else list(range(nch - 1, -1, -1))
                tl = {}
                for _ in front(order[0], tl):
                    yield
                for i_, n in enumerate(order):
                    gb = back(n, tl, d == 0)
                    tl2 = {}
                    gf = front(order[i_ + 1], tl2) if i_ + 1 < nch else None
                    while gb is not None or gf is not None:
                        if gb is not None and next(gb, 'END') == 'END':
                            gb = None
                        if gf is not None and next(gf, 'END') == 'END':
                            gf = None
                        yield
                    tl = tl2
                if not st.sample:
                    for h in range(2):
                        S.dma('pool', o_st.ap[st.pi, li, d, 2 * pc + h], Sst.ap[h * 64:(h + 1) * 64, :], [Sst], [o_st])

    def run_attn(QT, q0, nqb, entries, KT, V, dv2, PTr, pS, acc):
        cnt = [0] * nqb
        tot = [0] * nqb
        for (kb, qlo, qhi, masks) in entries:
            for qb in range(qlo, qhi):
                tot[qb] += 1

        def pv(ent, pt):
            kb, qlo, qhi, masks = ent
            for qb in range(qlo, qhi):
                o_ = (qb - qlo) * 128
                S.mm(acc[qb], acc[qb].ap[:, 0:dv2], pt, pt.ap[:, o_:o_ + 128], V, V.ap[:, kb, :],
                     start=(cnt[qb] == 0), stop=(cnt[qb] == tot[qb] - 1))
                cnt[qb] += 1
        pend = None
        for ent in entries:
            kb, qlo, qhi, masks = ent
            ps = pS.next()
            w = (qhi - qlo) * 128
            S.mm(ps, ps.ap[:, 0:w], KT, KT.ap[:, kb * 128:(kb + 1) * 128], QT,
                 QT.ap[:, qlo * 128:qhi * 128])
            pt = PTr.next()
            S.act(pt, pt.ap[:, 0:w], ps, ps.ap[:, 0:w], AF.Exp, scale=0.125)
            for qb, mk in masks.items():
                o_ = (qb - qlo) * 128
                tt('pool', pt, pt.ap[:, o_:o_ + 128], pt, pt.ap[:, o_:o_ + 128], tri2_t, mk, ALU.mult)
            if pend is not None:
                pv(*pend)
            pend = (ent, pt)
            yield
        if pend is not None:
            pv(*pend)
        yield

    accsets = [P[0:2], P[2:4]]
    ucnt = [0]

    def phaseB_diff(l, st):
        li = l // 2
        lam_init = 0.8 - 0.6 * math.exp(-0.3 * l)
        T_ = st.T
        nq = T_ // 128
        koff = st.koff
        nkb = (koff + T_) // 128
        kb0 = koff // 128
        QZ = [[alloc("QZ%d_%d" % (m_, b_), [128, 512]) for b_ in range(2)] for m_ in range(2)]
        for m_ in range(2):
            for b_ in range(2):
                S.op('dve', 'memset', [], [QZ[m_][b_]], ap=QZ[m_][b_].ap, constant=0.0)
        KT = alloc("KT", [128, koff + T_])
        V = alloc("V", [128, nkb, 130])
        PTr = arot("PT", [128, 512], 3)
        o1n = alloc("o1n", [128, 4, 128])
        tr_ = arot("tq", [128, 128], 4)
        or_ = arot("oq", [128, 128], 2)
        onr = arot("onq", [128, 128], 2)
        stgr = arot("stgq", [128, 512], 2)
        kst = arot("kst", [128, 128], 2)
        sm = alloc("sm", [128, 8])
        stt = alloc("stt", [128, 16])
        stt2 = alloc("stt2", [128, 4])
        dl = alloc("dl", [128, 256])
        pr_ = alloc("prd", [128, 128])
        lam = alloc("lam", [128, 4])
        pS = Rot(P[4:6])
        S.dma('sp', dl.ap, diff_lam.ap[li].partition_broadcast(128), [diff_lam], [dl])
        dl4 = dl.ap.rearrange("p (a b d) -> p a b d", a=2, b=2)
        tt('dve', pr_, pr_.ap.rearrange("p (a d) -> p a d", a=2), dl, dl4[:, :, 0, :], dl, dl4[:, :, 1, :], ALU.mult)
        S.op('dve', 'tensor_reduce', [pr_], [lam], out=lam.ap[:, 0:2], in_=pr_.ap.rearrange("p (a d) -> p a d", a=2),
             axis=AX, op=ALU.add)
        S.act(lam, lam.ap[:, 0:2], lam, lam.ap[:, 0:2], AF.Exp)
        tt('dve', lam, lam.ap[:, 2:3], lam, lam.ap[:, 1:2], lam, lam.ap[:, 0:1], ALU.subtract)
        ts('dve', lam, lam.ap[:, 3:4], lam, lam.ap[:, 2:3], -lam_init, None, ALU.add)
        S.op('dve', 'memset', [], [V], ap=V.ap[:, :, 128:130], constant=1.0)
        for h in range(4):
            yield
            S.dma('sp', KT.ap[:, koff:koff + T_], st.FA.ap[8 + h], st.FAb[8 + h], [KT])
            S.dma('sp', V.ap[:, kb0:nkb, 0:128],
                  st.TA.ap[:, 1024 + h * 128:1024 + (h + 1) * 128].rearrange("(n t) d -> t n d", t=128),
                  st.TAb[2], [V])
            if st.sample:
                S.dma('sp', V.ap[:, 0:2, 0:128], cdv_i.ap[li, h].rearrange("(n t) d -> t n d", t=128), [cdv_i], [V])
                for lb in range(2):
                    ks = kst.next()
                    S.dma('sp', ks.ap.rearrange("t (m d) -> t m d", m=2),
                          cdk_i.ap[li, h, :, lb * 128:(lb + 1) * 128, :].rearrange("m t d -> t m d"), [cdk_i], [ks])
                    pb7 = pS.next()
                    S.transpose(pb7, pb7.ap[:, 0:128], ks, ks.ap, ident_t, ident_t.ap)
                    S.act(KT, KT.ap[:, lb * 128:(lb + 1) * 128], pb7, pb7.ap[:, 0:128], AF.Identity)
                    yield
            for qg in range((nq + 3) // 4):
                nqb = min(4, nq - 4 * qg)
                stg = stgr.next()
                c0 = qg * 512
                ggs = list(range(c0 // G, (c0 + nqb * 128 + G - 1) // G))
                for m_ in range(2):
                    qz_ = QZ[m_][qg % 2]
                    S.dma('sp', qz_.ap[m_ * 64:(m_ + 1) * 64, 0:nqb * 128],
                          st.FA.ap[4 + h, m_ * 64:(m_ + 1) * 64, c0:c0 + nqb * 128], [st.FAb[4 + h][gg] for gg in ggs], [qz_])
                for m in range(2):
                    acc = P[0:nqb]
                    entries = [(kb, 0, nqb, {}) for kb in range(nkb)]
                    yield from run_attn(QZ[m][qg % 2], 0, nqb, entries, KT, V, 130, PTr, pS, acc)
                    t_s = []
                    for qb in range(nqb):
                        a = acc[qb]
                        S.op('dve', 'reciprocal', [a], [sm], out=sm.ap[:, m * 4 + qb:m * 4 + qb + 1], in_=a.ap[:, 128:129])
                        if m == 0:
                            ts('dve', o1n, o1n.ap[:, qb, :], a, a.ap[:, 0:128], sm.ap[:, qb:qb + 1], None, ALU.mult,
                               reads=[sm])
                        else:
                            t_ = tr_.next()
                            ts('dve', t_, t_.ap, a, a.ap[:, 0:128], sm.ap[:, 4 + qb:5 + qb], lam.ap[:, 3:4], ALU.mult,
                               ALU.mult, reads=[sm, lam])
                            t_s.append(t_)
                    for qb in range(nqb):
                        if m == 1:
                            t_ = t_s[qb]
                            o_ = or_.next()
                            tt('dve', o_, o_.ap, t_, t_.ap, o1n, o1n.ap[:, qb, :], ALU.add)
                            on_ = onr.next()
                            head_norm(o_, on_, stt, stt2, 0)
                            pb7 = pS.next()
                            S.transpose(pb7, pb7.ap[:, 0:128], on_, on_.ap, ident_t, ident_t.ap)
                            S.act(stg, stg.ap[:, qb * 128:(qb + 1) * 128], pb7, pb7.ap[:, 0:128], AF.Identity,
                                  scale=float(1.0 - lam_init))
                            yield
                c0 = qg * 512
                S.dma('pool', st.MIX.ap[4 + h, :, c0:c0 + nqb * 128], stg.ap[:, 0:nqb * 128], [stg],
                      [st.MIXb[4 + h][gg] for gg in range(c0 // G, (c0 + nqb * 128 + G - 1) // G)])

    def phaseB_swa(l, st):
        li = l // 2
        T_ = st.T
        nq = T_ // 128
        koff = st.koff
        nkb = (koff + T_) // 128
        kb0 = koff // 128
        QZ = [[alloc("QZ%d_%d" % (m_, b_), [128, 512]) for b_ in range(2)] for m_ in range(2)]
        for m_ in range(2):
            for b_ in range(2):
                S.op('dve', 'memset', [], [QZ[m_][b_]], ap=QZ[m_][b_].ap, constant=0.0)
        KT = alloc("KT", [128, koff + T_])
        V = alloc("V", [128, nkb, 66])
        PTr = arot("PT", [128, 512], 3)
        mixtm = alloc("mixtm", [128, 4, 128])
        stgr = arot("stgq", [128, 512], 2)
        kst = arot("kst", [128, 128], 2)
        sm = alloc("sm", [128, 8])
        esink = alloc("esink", [128, 8])
        pS = Rot(P[4:6])
        S.dma('sp', esink.ap, swa_sink.ap[li].partition_broadcast(128), [swa_sink], [esink])
        S.act(esink, esink.ap, esink, esink.ap, AF.Exp)
        S.op('dve', 'memset', [], [V], ap=V.ap[:, :, 64:66], constant=1.0)
        trif = tri2_t.ap[:, 0:128]
        trib = tri2_t.ap[:, 256:384]
        for pc in range(4):
            kv = pc // 2
            if pc % 2 == 0:
                S.dma('sp', KT.ap[:, koff:koff + T_], st.FA.ap[4 + kv], st.FAb[4 + kv], [KT])
                S.dma('sp', V.ap[:, kb0:nkb, 0:64],
                      st.TA.ap[:, 1024 + kv * 64:1024 + (kv + 1) * 64].rearrange("(n t) d -> t n d", t=128),
                      st.TAb[2], [V])
                if st.sample:
                    S.dma('sp', V.ap[:, 0:2, 0:64], csv_i.ap[li, kv].rearrange("(n t) d -> t n d", t=128), [csv_i], [V])
                    for lb in range(2):
                        ks = kst.next()
                        for dup in range(2):
                            S.dma('sp', ks.ap[:, dup * 64:(dup + 1) * 64], csk_i.ap[li, kv, lb * 128:(lb + 1) * 128, :],
                                  [csk_i], [ks])
                        pb7 = pS.next()
                        S.transpose(pb7, pb7.ap[:, 0:128], ks, ks.ap, ident_t, ident_t.ap)
                        S.act(KT, KT.ap[:, lb * 128:(lb + 1) * 128], pb7, pb7.ap[:, 0:128], AF.Identity)
                        yield
            for qg in range((nq + 3) // 4):
                nqb = min(4, nq - 4 * qg)
                q0 = 4 * qg
                c0 = qg * 512
                ggs = list(range(c0 // G, (c0 + nqb * 128 + G - 1) // G))
                for m_ in range(2):
                    qz_ = QZ[m_][(pc * 4 + qg) % 2]
                    S.dma('sp', qz_.ap[m_ * 64:(m_ + 1) * 64, 0:nqb * 128],
                          st.FA.ap[pc, m_ * 64:(m_ + 1) * 64, c0:c0 + nqb * 128], [st.FAb[pc][gg] for gg in ggs], [qz_])
                for hh in range(2):
                    acc = P[0:nqb]
                    h = 2 * pc + hh
                    entries = []
                    if st.sample:
                        entries += [(kb, 0, nqb, {}) for kb in range(kb0)]
                        for j in range(q0 - 1, q0 + nqb + 1):
                            if j < 0 or j >= nq:
                                continue
                            qlo = max(j - 1, q0) - q0
                            qhi = min(j + 1, q0 + nqb - 1) - q0 + 1
                            masks = {}
                            if q0 <= j + 1 < q0 + nqb:
                                masks[j + 1 - q0] = trib
                            if q0 <= j - 1 < q0 + nqb:
                                masks[j - 1 - q0] = trif
                            entries.append((kb0 + j, qlo, qhi, masks))
                    else:
                        entries += [(kb, 0, nqb, {}) for kb in range(nkb)]
                    yield from run_attn(QZ[hh][(pc * 4 + qg) % 2], 0, nqb, entries, KT, V, 66, PTr, pS, acc)
                    for qb in range(nqb):
                        a = acc[qb]
                        tt('dve', sm, sm.ap[:, qb:qb + 1], a, a.ap[:, 64:65], esink, esink.ap[:, h:h + 1], ALU.add)
                        S.op('dve', 'reciprocal', [sm], [sm], out=sm.ap[:, 4 + qb:5 + qb], in_=sm.ap[:, qb:qb + 1])
                        ts('dve', mixtm, mixtm.ap[:, qb, hh * 64:(hh + 1) * 64], a, a.ap[:, 0:64], sm.ap[:, 4 + qb:5 + qb],
                           None, ALU.mult, reads=[sm])
                stg = stgr.next()
                for qb in range(nqb):
                    pb7 = pS.next()
                    S.transpose(pb7, pb7.ap[:, 0:128], mixtm, mixtm.ap[:, qb, :], ident_t, ident_t.ap)
                    S.act(stg, stg.ap[:, qb * 128:(qb + 1) * 128], pb7, pb7.ap[:, 0:128], AF.Identity)
                    yield
                c0 = qg * 512
                S.dma('pool', st.MIX.ap[pc, :, c0:c0 + nqb * 128], stg.ap[:, 0:nqb * 128], [stg],
                      [st.MIXb[pc][gg] for gg in range(c0 // G, (c0 + nqb * 128 + G - 1) // G)])

    def phaseC_layer(l):
        modt = modts[l % 2]
        li = l // 2
        Wout = w_out_even if l % 2 == 0 else w_out_odd
        new_phase()
        M8s = [alloc("M8_%d" % i, [128, 8, G]) for i in range(2)]
        zTs = [alloc("zT_%d" % i, [128, 8, G]) for i in range(2)]
        U = alloc("U", [128, 32, G])
        sq = alloc("sq", [128, 8, G])
        rstds = [alloc("rstd%d" % i, [128, G]) for i in range(2)]
        rr = arot("rr", [128, G], 2)
        csum = alloc("csum", [128, G])
        pr = Rot(P[0:6])
        pm, pv = P[6], P[7]
        groups = [(st, gi) for st in streams for gi in range(st.ng)]
        n = len(groups)

        def xg(k):
            st, gi = groups[k]
            return xT[st.g0 + gi], st.c

        def partA(k):
            st, gi = groups[k]
            x, c = xg(k)
            t0 = gi * G
            M8 = M8s[k % 2]
            zT = zTs[0]
            S.dma('sp', M8.ap, st.MIX.ap[:, :, t0:t0 + G].rearrange("c p t -> p c t"),
                  [st.MIXb[cc][gi] for cc in range(8)], [M8])
            S.act(x, x.ap, x, x.ap, AF.Identity, scale=float(ALPHA))
            for ob in range(2):
                wb, wap = wload(Wout, Wout.ap[li, :, ob * 512:(ob + 1) * 512].rearrange("(k p) n -> p k n", p=128),
                                [128, 8, 512])
                for o4 in range(4):
                    oc = ob * 4 + o4
                    ps = pr.next()
                    for kc in range(8):
                        S.mm(ps, ps.ap[:, 0:G], wb, wap[:, kc, o4 * 128:(o4 + 1) * 128], M8, M8.ap[:, kc, :],
                             start=(kc == 0), stop=(kc == 7))
                    S.op('dve', 'scalar_tensor_tensor', [ps, modt, x], [zT], out=zT.ap[:, oc, :], in0=ps.ap[:, 0:G],
                         scalar=modt.ap[:, c, 16 + oc:17 + oc], in1=x.ap[:, oc, :], op0=ALU.mult, op1=ALU.add)

        def ln_gen(zT, rstd, dst, lc, post=None):
            S.op('dve', 'tensor_reduce', [zT], [csum], out=csum.ap, in_=zT.ap.rearrange("p c t -> p t c"), axis=AX,
                 op=ALU.add)
            S.mm(pm, pm.ap[:, 0:G], onesd_t, onesd_t.ap, csum, csum.ap, start=True, stop=True)
            yield
            tt('dve', zT, zT.ap, zT, zT.ap, pm, pm.ap[:, 0:G].unsqueeze(1).broadcast_to([128, 8, G]), ALU.subtract)
            S.act(sq, sq.ap, zT, zT.ap, AF.Square)
            S.op('dve', 'tensor_reduce', [sq], [csum], out=csum.ap, in_=sq.ap.rearrange("p c t -> p t c"), axis=AX,
                 op=ALU.add)
            yield
            S.mm(pv, pv.ap[:, 0:G], onesd_t, onesd_t.ap, csum, csum.ap, start=True, stop=True)
            yield
            S.act(rstd, rstd.ap, pv, pv.ap[:, 0:G], AF.Sqrt, bias=eps_t.ap[:, 0:1], reads=[eps_t])
            S.op('dve', 'reciprocal', [rstd], [rstd], out=rstd.ap, in_=rstd.ap)
            tt('dve', zT, zT.ap, zT, zT.ap, rstd, rstd.ap.unsqueeze(1).broadcast_to([128, 8, G]), ALU.mult)
            for kc in range(8):
                S.act(dst, dst.ap[:, kc, :], zT, zT.ap[:, kc, :], AF.Identity, scale=lng_t.ap[:, lc + kc:lc + kc + 1],
                      bias=lnb_t.ap[:, lc + kc:lc + kc + 1], reads=[lng_t, lnb_t])
            if post is not None:
                post()

        def step(gen):
            if gen is not None:
                next(gen, None)

        def finish(gen):
            if gen is not None:
                for _ in gen:
                    pass

        def make_ln1(k):
            x, c = xg(k)
            M8 = M8s[k % 2]

            def post():
                for kc in range(8):
                    S.act(M8, M8.ap[:, kc, :], x, x.ap[:, kc, :], AF.Identity, scale=modt.ap[:, c, 32 + kc:33 + kc],
                          bias=modt.ap[:, c, 24 + kc:25 + kc], reads=[modt])
            return ln_gen(zTs[0], rstds[0], x, (l * 2 + 0) * 8, post)

        def make_ln2(k):
            x, c = xg(k)
            return ln_gen(zTs[1], rstds[1], x, (l * 2 + 1) * 8)

        def ff1(k, filler):
            M8 = M8s[k % 2]
            for jb in range(8):
                wb, wap = wload(w_ff1, w_ff1.ap[l, :, jb * 512:(jb + 1) * 512].rearrange("(k p) n -> p k n", p=128),
                                [128, 8, 512])
                for jj in range(4):
                    j = jb * 4 + jj
                    ps = pr.next()
                    for kc in range(8):
                        S.mm(ps, ps.ap[:, 0:G], wb, wap[:, kc, jj * 128:(jj + 1) * 128], M8, M8.ap[:, kc, :],
                             start=(kc == 0), stop=(kc == 7))
                    r_ = rr.next()
                    S.act(r_, r_.ap, ps, ps.ap[:, 0:G], AF.Relu)
                    tt('pool', U, U.ap[:, j, :], r_, r_.ap, r_, r_.ap, ALU.mult)
                if jb % 2 == 0:
                    step(filler)

        def ff2(k, filler):
            x, c = xg(k)
            zT = zTs[1]
            for oc in range(8):
                wb, wap = wload(w_ff2, w_ff2.ap[l, :, oc * 128:(oc + 1) * 128].rearrange("(j p) n -> p j n", p=128),
                                [128, 32, 128])
                ps = pr.next()
                for j in range(32):
                    S.mm(ps, ps.ap[:, 0:G], wb, wap[:, j, :], U, U.ap[:, j, :], start=(j == 0), stop=(j == 31))
                S.op('dve', 'scalar_tensor_tensor', [ps, modt, x], [zT], out=zT.ap[:, oc, :], in0=ps.ap[:, 0:G],
                     scalar=modt.ap[:, c, 40 + oc:41 + oc], in1=x.ap[:, oc, :], op0=ALU.mult, op1=ALU.add)
                if oc % 2 == 0:
                    step(filler)

        last = (l == NL - 1)
        if last:
            orot = arot("xout", [128, 1024], 2)

        def store_group(k):
            st, gi = groups[k]
            x = xT[st.g0 + gi]
            for t2 in range(2):
                r0 = gi * G + t2 * 128
                ot_ = orot.next()
                for hf in range(2):
                    ps = pr.next()
                    for c4 in range(4):
                        cc = hf * 4 + c4
                        S.transpose(ps, ps.ap[:, c4 * 128:(c4 + 1) * 128], x, x.ap[:, cc, t2 * 128:(t2 + 1) * 128],
                                    ident_t, ident_t.ap)
                    S.act(ot_, ot_.ap[:, hf * 512:(hf + 1) * 512], ps, ps.ap, AF.Identity)
                if st.sample:
                    S.dma('pool', ys.ap[r0:r0 + 128, :], ot_.ap, [ot_], [ys])
                else:
                    S.dma('pool', yp.ap[st.pi, r0:r0 + 128, :], ot_.ap, [ot_], [yp])

        partA(0)
        finish(make_ln1(0))
        ln2_prev = None
        for k in range(n):
            ff1(k, ln2_prev)
            finish(ln2_prev)
            if last and k >= 1:
                store_group(k - 1)
            if k + 1 < n:
                partA(k + 1)
            x, c = xg(k)
            S.act(x, x.ap, x, x.ap, AF.Identity, scale=float(ALPHA))
            ln1_next = make_ln1(k + 1) if k + 1 < n else None
            ff2(k, ln1_next)
            finish(ln1_next)
            ln2_prev = make_ln2(k)
        finish(ln2_prev)
        if last:
            store_group(n - 1)

    import os
    stop = os.environ.get("KSTOP", "")
    if stop == "load":
        load_input()
    for l in range(NL):
        if stop == "load":
            break
        if l == 0:
            for _ in compute_mod_gen(0, P[6]):
                pass
        if stop == "mod":
            break
        for st in streams:
            phaseA(l, st)
        if stop == "A":
            break
        gm_ = compute_mod_gen(l + 1, P[3]) if l + 1 < NL else None
        for st in streams:
            new_phase()
            if l % 2 == 0:
                ga, gl_, ratio = phaseB_diff(l, st), phaseB_lin(l, st, 'ret'), 1
            else:
                ga, gl_, ratio = phaseB_swa(l, st), phaseB_lin(l, st, 'gla'), 5
            if os.environ.get("KSEQ"):
                for _ in ga:
                    pass
                ga = None
            while ga is not None or gl_ is not None:
                if ga is not None and next(ga, 'END') == 'END':
                    ga = None
                for _ in range(ratio):
                    if gl_ is not None and next(gl_, 'END') == 'END':
                        gl_ = None
                if gm_ is not None and not st.sample and next(gm_, 'END') == 'END':
                    gm_ = None
        if gm_ is not None:
            for _ in gm_:
                pass
        if stop in ("B", "B1"):
            break
        phaseC_layer(l)
    if stop:
        store_output()
    S.finish()
    return nc, S


OUT_NAMES = ["yp", "ys", "o_ret", "o_cdk", "o_cdv", "o_csk", "o_csv", "o_gla"]


def make_in_maps(inp, T=2048):
    hc = host_consts(T)
    oidx = odd_col_index()
    w_in_odd_x = np.ascontiguousarray(inp['w_in_odd'][:, :, oidx])
    maps = []
    f = lambda a: np.ascontiguousarray(np.asarray(a, dtype=np.float32))
    for i in range(8):
        b = i % 4
        m = dict(
            xs=f(inp['x_sample'][b]), xp=f(inp['x_prompt'][2 * i:2 * i + 2]),
            cond=f(np.stack([inp['c'][b], inp['c_ctx']])),
            st_ret=f(inp['state_ret'][b]), cdk_i=f(inp['cache_diff_k'][b]), cdv_i=f(inp['cache_diff_v'][b]),
            csk_i=f(inp['cache_swa_k'][b]), csv_i=f(inp['cache_swa_v'][b]), st_gla=f(inp['state_gla'][b]),
            w_mod=f(inp['w_mod']), b_mod=f(inp['b_mod']), ln_g=f(inp['ln_g']), ln_b=f(inp['ln_b']),
            w_in_even=f(inp['w_in_even']), w_out_even=f(inp['w_out_even']), ret_decay=f(inp['ret_decay']),
            diff_lam=f(np.reshape(inp['diff_lam'], (2, 256))), w_in_odd=w_in_odd_x, w_out_odd=f(inp['w_out_odd']),
            swa_sink=f(inp['swa_sink']), gla_w2=f(inp['gla_w2']), gla_b=f(inp['gla_b']),
            w_ff1=f(inp['w_ff1']), w_ff2=f(inp['w_ff2']), **hc)
        maps.append(m)
    return maps


_CACHE = {}


def kernel(**inputs):
    inp = {k: np.asarray(v) for k, v in inputs.items()}
    T = inp['x_sample'].shape[1]
    if T not in _CACHE:
        _CACHE[T] = build(T)[0]
    nc = _CACHE[T]
    maps = make_in_maps(inp, T)
    res = run_bass_kernel_spmd(nc, maps, core_ids=list(range(8)))
    R = res.results
    y_p = np.concatenate([R[i]['yp'] for i in range(8)], axis=0)
    y_s = np.stack([R[b]['ys'] for b in range(4)], axis=0)
    outs = [y_p, y_s]
    for nm in OUT_NAMES[2:]:
        outs.append(np.concatenate([R[i][nm] for i in range(8)], axis=0))
    return tuple(np.ascontiguousarray(o, dtype=np.float32) for o in outs)
```

```python
import numpy as np
from contextlib import ExitStack
import concourse.bass as bass
import concourse.mybir as mybir
from concourse.bass_utils import run_bass_kernel_spmd

F32 = mybir.dt.float32
F32R = mybir.dt.float32r
AF = mybir.ActivationFunctionType
ALU = mybir.AluOpType


class Buf:
    __slots__ = ("name", "ap", "w", "r")

    def __init__(self, name, ap):
        self.name = name
        self.ap = ap
        self.w = []
        self.r = []


class Sched:
    ENG = ('pe', 'act', 'dve', 'pool', 'sp')
    NS = 8

    def __init__(self, nc):
        self.nc = nc
        self.es = ExitStack()
        self.streams = {e: [] for e in self.ENG}
        self.cnt = {e: 0 for e in self.ENG}
        self.seen = {e: {} for e in self.ENG}
        self.dma_n = {}
        self.sems = {}
        self.nbuf = 0

    def dram_in(self, name, shape):
        return Buf(name, self.nc.dram_tensor(name, list(shape), F32, kind="ExternalInput").ap())

    def dram_out(self, name, shape):
        return Buf(name, self.nc.dram_tensor(name, list(shape), F32, kind="ExternalOutput").ap())

    def dram_scratch(self, name, shape):
        return Buf(name, self.nc.dram_tensor(name, list(shape), F32, kind="Internal").ap())

    def sbuf(self, name, shape):
        t = self.es.enter_context(self.nc.sbuf_tensor(name, list(shape), F32))
        return Buf(name, t[:])

    def psum(self, name):
        t = self.es.enter_context(self.nc.psum_tensor(name, [128, 512], F32))
        return Buf(name, t[:])

    def view(self, name, ap, olds=()):
        b = Buf(name, ap)
        for o in olds:
            b.r += o.w + o.r
        return b

    def _op(self, eng, fn, reads, writes, dma_q=None):
        deps = {}

        def add(tok):
            k, v = tok
            if deps.get(k, 0) < v:
                deps[k] = v
        for b in reads:
            for t in b.w:
                add(t)
        for b in writes:
            for t in b.w:
                add(t)
            for t in b.r:
                add(t)
        if dma_q is not None:
            n = self.dma_n.get(dma_q, 0)
            self.dma_n[dma_q] = n + 1
            slot, k = n % self.NS, n // self.NS
            key = ('d', dma_q, slot)
            if k > 0:
                add((key, 16 * k))
            tok = (key, 16 * (k + 1))
            inc = 16
        else:
            self.cnt[eng] += 1
            tok = (eng, self.cnt[eng])
            inc = 1
        seen = self.seen[eng]
        waits = []
        for k, v in deps.items():
            if eng == 'pe' and k == 'pe':
                continue
            if seen.get(k, 0) >= v:
                continue
            seen[k] = v
            waits.append((k, v))
        self.streams[eng].append((waits, fn, tok[0], inc))
        for b in writes:
            b.w = [tok]
            b.r = []
        for b in reads:
            if any(b is x for x in writes):
                continue
            b.r = [t for t in b.r if t[0] != tok[0]] + [tok]
        return tok

    def mm(self, pb, out_ap, lb, lhsT, rb, rhs, start=True, stop=True, r=False):
        if r:
            lhsT = lhsT.bitcast(F32R)
            rhs = rhs.bitcast(F32R)
        self._op('pe', lambda e: e.matmul(out_ap, lhsT, rhs, start=start, stop=stop), [lb, rb], [pb])

    def transpose(self, pb, out_ap, ib, in_ap, idb, id_ap):
        self._op('pe', lambda e: e.transpose(out_ap, in_ap, id_ap), [ib, idb], [pb])

    def act(self, ob, out_ap, ib, in_ap, func, scale=None, bias=None, reads=()):
        kw = {}
        if scale is not None:
            kw['scale'] = scale
        if bias is not None:
            kw['bias'] = bias
        self._op('act', lambda e: e.activation(out_ap, in_ap, func, **kw), [ib] + list(reads), [ob])

    def dve(self, fn, reads, writes):
        self._op('dve', fn, list(reads), list(writes))

    def pool(self, fn, reads, writes):
        self._op('pool', fn, list(reads), list(writes))

    def dma(self, q, out_ap, in_ap, reads, writes, **kw):
        self._op(q, lambda e: e.dma_start(out=out_ap, in_=in_ap, **kw), list(reads), list(writes), dma_q=q)

    def finish(self):
        nc = self.nc
        fw = []
        for q, n in self.dma_n.items():
            for slot in range(min(n, self.NS)):
                cnt = (n - slot + self.NS - 1) // self.NS
                fw.append((('d', q, slot), 16 * cnt))
        self.streams['sp'].append((fw, None, None, 0))
        keys = set()
        for e in self.ENG:
            for waits, fn, key, inc in self.streams[e]:
                if key is not None:
                    keys.add(key)
        for i, k in enumerate(sorted(keys, key=str)):
            self.sems[k] = self.es.enter_context(nc.semaphore("sem%d" % i))
        with nc.Block() as block:
            def mk(ename):
                def body(e):
                    for waits, fn, key, inc in self.streams[ename]:
                        for k, v in waits:
                            e.wait_ge(self.sems[k], v)
                        if fn is not None:
                            fn(e).then_inc(self.sems[key], inc)
                return body
            block.tensor(mk('pe'))
            block.scalar(mk('act'))
            block.vector(mk('dve'))
            block.gpsimd(mk('pool'))
            block.sync(mk('sp'))
        self.es.close()

    def op(self, eng, name, reads, writes, **kw):
        self._op(eng, lambda e: getattr(e, name)(**kw), list(reads), list(writes))


class Rot:
    def __init__(self, items):
        self.items = list(items)
        self.i = 0

    def next(self):
        x = self.items[self.i % len(self.items)]
        self.i += 1
        return x


AX = mybir.AxisListType.X
D = 1024
LCTX = 256
G = 256
ALPHA = 8.0 ** 0.25
EPS = 1e-5


def host_consts(T):
    ident = np.eye(128, dtype=np.float32)
    onesd = np.full((128, 128), 1.0 / 1024.0, dtype=np.float32)
    j = np.arange(128)[:, None]
    i = np.arange(128)[None, :]
    trif = (j <= i).astype(np.float32)
    trib = (j >= i).astype(np.float32)
    tri2 = np.concatenate([np.tile(trif, (1, 2)), np.tile(trib, (1, 2))], axis=1)
    rows = T // 64
    row = np.repeat(np.arange(rows), 64).astype(np.float32)
    col = np.tile(np.arange(64), rows).astype(np.float32)
    half = 32
    freqs = (np.float32(10000.0) ** (-np.arange(0, half, 2, dtype=np.float32) / np.float32(half))).astype(np.float32)
    ar = (row[:, None] * freqs).astype(np.float32)
    ac = (col[:, None] * freqs).astype(np.float32)
    cosT = np.zeros((64, T), np.float32)
    sinT = np.zeros((64, T), np.float32)
    perm = np.zeros((64, 64), np.float32)
    for d in range(64):
        blk, e = d // 32, d % 32
        ang = ar if blk == 0 else ac
        f = e % 16
        cosT[d] = np.cos(ang[:, f])
        sinT[d] = (-np.sin(ang[:, f])) if e < 16 else np.sin(ang[:, f])
        partner = d + 16 if e < 16 else d - 16
        perm[partner, d] = 1.0
    cos2 = np.concatenate([cosT, cosT], axis=0)
    sin2 = np.concatenate([sinT, sinT], axis=0)
    perm2 = np.zeros((128, 128), np.float32)
    perm2[:64, :64] = perm
    perm2[64:, 64:] = perm
    return dict(ident=ident, onesd=onesd, tri2=tri2, cos2=cos2, sin2=sin2, perm2=perm2)


ODD_W = 2464


def odd_col_index():
    sq = list(range(0, 512))
    sk = list(range(512, 640))
    sv = list(range(640, 768))
    gq = list(range(768, 1024))
    gk = list(range(1024, 1280))
    gv = list(range(1280, 1792))
    gr = list(range(1792, 2304))
    glr = list(range(2304, 2336))
    skdup = sk[0:64] + sk[0:64] + sk[64:128] + sk[64:128]
    idx = sq + skdup + gq + gk + glr + sv + gv + gr
    assert len(idx) == ODD_W
    return np.array(idx)


import math


def build(T=2048, NL=4, AR=22528):
    nc = bass.Bass("TRN2", target_bir_lowering=False)
    S = Sched(nc)
    NGS = T // G
    NGRP = NGS + 2
    xs = S.dram_in("xs", [T, D])
    xp = S.dram_in("xp", [2, 256, D])
    cond = S.dram_in("cond", [2, D])
    st_ret = S.dram_in("st_ret", [2, 2, 4, 64, 128])
    cdk_i = S.dram_in("cdk_i", [2, 4, 2, 256, 64])
    cdv_i = S.dram_in("cdv_i", [2, 4, 256, 128])
    csk_i = S.dram_in("csk_i", [2, 2, 256, 64])
    csv_i = S.dram_in("csv_i", [2, 2, 256, 64])
    st_gla = S.dram_in("st_gla", [2, 2, 4, 64, 128])
    w_mod = S.dram_in("w_mod", [4, D, 6144])
    b_mod = S.dram_in("b_mod", [4, 6144])
    ln_g = S.dram_in("ln_g", [4, 2, D])
    ln_b = S.dram_in("ln_b", [4, 2, D])
    w_in_even = S.dram_in("w_in_even", [2, D, 3072])
    w_out_even = S.dram_in("w_out_even", [2, D, D])
    ret_decay = S.dram_in("ret_decay", [2, 2, 4])
    diff_lam = S.dram_in("diff_lam", [2, 256])
    w_in_odd = S.dram_in("w_in_odd", [2, D, ODD_W])
    w_out_odd = S.dram_in("w_out_odd", [2, D, D])
    swa_sink = S.dram_in("swa_sink", [2, 8])
    gla_w2 = S.dram_in("gla_w2", [2, 2, 16, 256])
    gla_b = S.dram_in("gla_b", [2, 2, 256])
    w_ff1 = S.dram_in("w_ff1", [4, D, 4096])
    w_ff2 = S.dram_in("w_ff2", [4, 4096, D])
    c_ident = S.dram_in("ident", [128, 128])
    c_onesd = S.dram_in("onesd", [128, 128])
    c_tri2 = S.dram_in("tri2", [128, 512])
    c_cos = S.dram_in("cos2", [128, T])
    c_sin = S.dram_in("sin2", [128, T])
    c_perm = S.dram_in("perm2", [128, 128])
    yp = S.dram_out("yp", [2, 256, D])
    ys = S.dram_out("ys", [T, D])
    o_ret = S.dram_out("o_ret", [2, 2, 2, 4, 64, 128])
    o_cdk = S.dram_out("o_cdk", [2, 2, 4, 2, 256, 64])
    o_cdv = S.dram_out("o_cdv", [2, 2, 4, 256, 128])
    o_csk = S.dram_out("o_csk", [2, 2, 2, 256, 64])
    o_csv = S.dram_out("o_csv", [2, 2, 2, 256, 64])
    o_gla = S.dram_out("o_gla", [2, 2, 2, 4, 64, 128])

    class St:
        pass
    streams = []
    for i in range(3):
        st = St()
        st.i = i
        st.sample = (i == 0)
        st.T = T if i == 0 else 256
        st.c = 0 if i == 0 else 1
        st.g0 = 0 if i == 0 else NGS + (i - 1)
        st.ng = st.T // G
        st.pi = i - 1
        st.koff = 256 if i == 0 else 0
        st.FA = S.dram_scratch("FA%d" % i, [12, 128, st.T])
        st.TA = S.dram_scratch("TA%d" % i, [st.T, 1536])
        st.MIX = S.dram_scratch("MIX%d" % i, [8, 128, st.T])
        st.FAb = [[Buf("FA", None) for _ in range(st.ng)] for _ in range(12)]
        st.TAb = [[Buf("TA", None) for _ in range(st.ng)] for _ in range(3)]
        st.MIXb = [[Buf("MIX", None) for _ in range(st.ng)] for _ in range(8)]
        streams.append(st)

    xT = [S.sbuf("xT%d" % g, [128, 8, G]) for g in range(NGRP)]
    WB = [S.sbuf("wb%d" % i, [128, 4096]) for i in range(2)]
    wrot = Rot(WB)
    arena = S.sbuf("arena", [128, AR])
    ident_t = S.sbuf("ident_t", [128, 128])
    onesd_t = S.sbuf("onesd_t", [128, 128])
    tri2_t = S.sbuf("tri2_t", [128, 512])
    perm_t = S.sbuf("perm_t", [128, 128])
    ones_t = S.sbuf("ones_t", [128, 128])
    eps_t = S.sbuf("eps_t", [128, 1])
    stg0 = S.sbuf("stg0", [128, 128])
    sc_t = S.sbuf("sc_t", [128, 8, 2])
    modts = [S.sbuf("modt%d" % i, [128, 2, 48]) for i in range(2)]
    bm_t = S.sbuf("bm_t", [128, 48])
    lng_t = S.sbuf("lng_t", [128, 64])
    lnb_t = S.sbuf("lnb_t", [128, 64])
    P = [S.psum("ps%d" % i) for i in range(8)]

    ar = dict(off=0, views=[], summ={})

    def new_phase():
        for v in ar['views']:
            for k, val in v.w + v.r:
                if ar['summ'].get(k, 0) < val:
                    ar['summ'][k] = val
        ar['views'] = []
        ar['off'] = 0

    def alloc(name, shape):
        n = 1
        for s_ in shape[1:]:
            n *= s_
        off = ar['off']
        ar['off'] += n
        assert ar['off'] <= AR, (name, ar['off'])
        ap = arena.ap[0:shape[0], off:off + n]
        if len(shape) == 3:
            ap = ap.rearrange("p (a b) -> p a b", a=shape[1])
        b = Buf(name, ap)
        b.r = list(ar['summ'].items())
        ar['views'].append(b)
        return b

    def arot(name, shape, n):
        return Rot([alloc("%s%d" % (name, i), shape) for i in range(n)])

    def wload(src_buf, src_ap, shape):
        wb = wrot.next()
        n = 1
        for s_ in shape[1:]:
            n *= s_
        ap = wb.ap[:, 0:n]
        if len(shape) == 3:
            ap = ap.rearrange("p (a b) -> p a b", a=shape[1])
        S.dma('sp', ap, src_ap, [src_buf], [wb])
        return wb, ap

    def tt(eng, ob, out, ab, a, bb, b, op):
        S.op(eng, 'tensor_tensor', [ab, bb], [ob], out=out, in0=a, in1=b, op=op)

    def ts(eng, ob, out, ib, in0, s1, s2=None, op0=ALU.mult, op1=None, reads=()):
        kw = dict(out=out, in0=in0, scalar1=s1, scalar2=s2, op0=op0)
        if op1 is not None:
            kw['op1'] = op1
        S.op(eng, 'tensor_scalar', [ib] + list(reads), [ob], **kw)

    def load_cols(dst, dst_ap, src, src_rows_ap, R, func=AF.Identity):
        S.dma('sp', stg0.ap[0:R, :], src_rows_ap, [src], [stg0])
        S.transpose(P[7], P[7].ap[:, 0:R], stg0, stg0.ap[0:R, :], ident_t, ident_t.ap[0:R, 0:R])
        S.act(dst, dst_ap, P[7], P[7].ap[:, 0:R], func)

    for tb, cb in ((ident_t, c_ident), (onesd_t, c_onesd), (tri2_t, c_tri2), (perm_t, c_perm)):
        S.dma('sp', tb.ap, cb.ap, [cb], [tb])
    S.op('dve', 'memset', [], [ones_t], ap=ones_t.ap, constant=1.0)
    S.op('dve', 'memset', [], [eps_t], ap=eps_t.ap, constant=EPS)
    for c in range(2):
        load_cols(sc_t, sc_t.ap[:, :, c], cond, cond.ap[c].rearrange("(k p) -> k p", p=128), 8, AF.Silu)
    load_cols(lng_t, lng_t.ap, ln_g, ln_g.ap.rearrange("l i (k p) -> (l i k) p", p=128), 64)
    load_cols(lnb_t, lnb_t.ap, ln_b, ln_b.ap.rearrange("l i (k p) -> (l i k) p", p=128), 64)

    def load_input():
        new_phase()
        xr = arot("xin", [128, 1024], 2)
        pr = Rot(P[0:6])
        for st in streams:
            for gi in range(st.ng):
                g = st.g0 + gi
                for t2 in range(2):
                    r0 = gi * G + t2 * 128
                    src = xs.ap[r0:r0 + 128, :] if st.sample else xp.ap[st.pi, r0:r0 + 128, :]
                    sb = xs if st.sample else xp
                    xt_ = xr.next()
                    S.dma('sp', xt_.ap, src, [sb], [xt_])
                    for hf in range(2):
                        ps = pr.next()
                        for c4 in range(4):
                            cc = hf * 4 + c4
                            S.transpose(ps, ps.ap[:, c4 * 128:(c4 + 1) * 128], xt_, xt_.ap[:, cc * 128:(cc + 1) * 128],
                                        ident_t, ident_t.ap)
                        S.act(xT[g], xT[g].ap[:, hf * 4:(hf + 1) * 4, t2 * 128:(t2 + 1) * 128], ps,
                              ps.ap.rearrange("p (c t) -> p c t", c=4), AF.Identity)

    def store_output():
        new_phase()
        orot = arot("xout", [128, 1024], 2)
        pr = Rot(P[0:6])
        for st in streams:
            for gi in range(st.ng):
                g = st.g0 + gi
                for t2 in range(2):
                    r0 = gi * G + t2 * 128
                    ot_ = orot.next()
                    for hf in range(2):
                        ps = pr.next()
                        for c4 in range(4):
                            cc = hf * 4 + c4
                            S.transpose(ps, ps.ap[:, c4 * 128:(c4 + 1) * 128], xT[g],
                                        xT[g].ap[:, cc, t2 * 128:(t2 + 1) * 128], ident_t, ident_t.ap)
                        S.act(ot_, ot_.ap[:, hf * 512:(hf + 1) * 512], ps, ps.ap, AF.Identity)
                    if st.sample:
                        S.dma('pool', ys.ap[r0:r0 + 128, :], ot_.ap, [ot_], [ys])
                    else:
                        S.dma('pool', yp.ap[st.pi, r0:r0 + 128, :], ot_.ap, [ot_], [yp])

    def compute_mod_gen(l, pm_):
        modt = modts[l % 2]
        S.dma('sp', stg0.ap[0:48, :], b_mod.ap[l].rearrange("(j p) -> j p", p=128), [b_mod], [stg0])
        S.transpose(pm_, pm_.ap[:, 0:48], stg0, stg0.ap[0:48, :], ident_t, ident_t.ap[0:48, 0:48])
        S.act(bm_t, bm_t.ap, pm_, pm_.ap[:, 0:48], AF.Identity)
        yield
        for blk in range(12):
            wb, wap = wload(w_mod, w_mod.ap[l, :, blk * 512:(blk + 1) * 512].rearrange("(k p) n -> p k n", p=128),
                            [128, 8, 512])
            for e4 in range(4):
                ec = blk * 4 + e4
                for kc in range(8):
                    S.mm(pm_, pm_.ap[:, 2 * ec:2 * ec + 2], wb, wap[:, kc, e4 * 128:(e4 + 1) * 128], sc_t,
                         sc_t.ap[:, kc, :], start=(kc == 0), stop=(kc == 7))
                yield
        pmv = pm_.ap[:, 0:96].rearrange("p (e c) -> p e c", c=2)
        for c in range(2):
            tt('dve', modt, modt.ap[:, c, :], pm_, pmv[:, :, c], bm_t, bm_t.ap, ALU.add)
        for lo in (8, 32):
            ts('dve', modt, modt.ap[:, :, lo:lo + 8], modt, modt.ap[:, :, lo:lo + 8], 1.0, None, ALU.add)

    def phaseA(l, st):
        modt = modts[l % 2]
        even = (l % 2 == 0)
        li = l // 2
        new_phase()
        hTs = [alloc("hT%d" % i, [128, 8, G]) for i in range(2)]
        sfm = arot("sfm", [128, G], 4)
        stm = arot("stm", [128, 512], 3)
        cosg = arot("cosg", [128, G], 2)
        sing = arot("sing", [128, G], 2)
        t1r = arot("t1r", [128, G], 2)
        t2r = arot("t2r", [128, G], 2)
        pr = Rot(P[0:6])
        pp = Rot(P[6:8])
        c = st.c
        W = w_in_even if even else w_in_odd
        if even:
            blocks = [
                (0, 512, [('fm', 0, 128, 0, 1.0, False), ('fm', 128, 128, 1, 1.0, False),
                          ('fm', 256, 128, 2, 0.125, False), ('fm', 384, 128, 3, 0.125, False)]),
                (512, 512, [('tm', 0, 512, 0, 0, AF.Identity, None)]),
                (1024, 512, [('tm', 0, 512, 1, 0, AF.Silu, None)]),
                (1536, 512, [('fm', h * 128, 128, 4 + h, 1.0, True) for h in range(4)]),
                (2048, 512, [('fm', h * 128, 128, 8 + h, 1.0, True) for h in range(4)] +
                 ([] if st.sample else [('tm', 0, 512, None, 0, AF.Identity, 'cdk')])),
                (2560, 512, [('tm', 0, 512, 2, 0, AF.Identity, None if st.sample else 'cdv')]),
            ]
        else:
            blocks = [
                (0, 512, [('fm', p_ * 128, 128, p_, 1.0, True) for p_ in range(4)]),
                (512, 512, [('fm', 0, 128, 4, 1.0, True), ('fm', 128, 128, 5, 1.0, True),
                            ('fm', 256, 128, 6, 0.125, False), ('fm', 384, 128, 7, 0.125, False)] +
                 ([] if st.sample else [('tmk',)])),
                (1024, 416, [('fm', 0, 128, 8, 1.0, False), ('fm', 128, 128, 9, 1.0, False),
                             ('fm', 256, 16, 10, 1.0, False), ('fm', 272, 16, 11, 1.0, False),
                             ('tm', 288, 128, 2, 0, AF.Identity, None if st.sample else 'csv')]),
                (1440, 512, [('tm', 0, 512, 0, 0, AF.Identity, None)]),
                (1952, 512, [('tm', 0, 512, 1, 0, AF.Silu, None)]),
            ]
        pend_rope = [None]

        def flush_rope():
            if pend_rope[0] is not None:
                f_ = pend_rope[0]
                pend_rope[0] = None
                f_()
        if l == 0:
            xr = arot("xin", [128, 1024], 2)

        def load_group(gi_):
            g_ = st.g0 + gi_
            for t2 in range(2):
                r0 = gi_ * G + t2 * 128
                src = xs.ap[r0:r0 + 128, :] if st.sample else xp.ap[st.pi, r0:r0 + 128, :]
                sb = xs if st.sample else xp
                xt_ = xr.next()
                S.dma('sp', xt_.ap, src, [sb], [xt_])
                for hf in range(2):
                    ps = pr.next()
                    for c4 in range(4):
                        cc = hf * 4 + c4
                        S.transpose(ps, ps.ap[:, c4 * 128:(c4 + 1) * 128], xt_, xt_.ap[:, cc * 128:(cc + 1) * 128],
                                    ident_t, ident_t.ap)
                    S.act(xT[g_], xT[g_].ap[:, hf * 4:(hf + 1) * 4, t2 * 128:(t2 + 1) * 128], ps,
                          ps.ap.rearrange("p (c t) -> p c t", c=4), AF.Identity)
        if l == 0:
            load_group(0)
        for gi in range(st.ng):
            g = st.g0 + gi
            t0 = gi * G
            h = hTs[gi % 2]
            for kc in range(8):
                S.act(h, h.ap[:, kc, :], xT[g], xT[g].ap[:, kc, :], AF.Identity,
                      scale=modt.ap[:, c, 8 + kc:9 + kc], bias=modt.ap[:, c, kc:kc + 1], reads=[modt])
            if l == 0 and gi + 1 < st.ng:
                load_group(gi + 1)
            if st.sample:
                cg = cosg.next()
                sg_ = sing.next()
                S.dma('sp', cg.ap, c_cos.ap[:, t0:t0 + G], [c_cos], [cg])
                S.dma('sp', sg_.ap, c_sin.ap[:, t0:t0 + G], [c_sin], [sg_])
            for (wc0, wn, jobs) in blocks:
                wb, wap = wload(W, W.ap[li, :, wc0:wc0 + wn].rearrange("(k p) n -> p k n", p=128), [128, 8, wn])
                for job in jobs:
                    if job[0] == 'fm':
                        _, c0, M, slot, scale, rope = job
                        ps = pr.next()
                        for kc in range(8):
                            S.mm(ps, ps.ap[0:M, 0:G], wb, wap[:, kc, c0:c0 + M], h, h.ap[:, kc, :],
                                 start=(kc == 0), stop=(kc == 7))
                        flush_rope()
                        sg = sfm.next()
                        S.act(sg, sg.ap[0:M, :], ps, ps.ap[0:M, 0:G], AF.Identity, scale=float(scale))
                        if rope and st.sample:
                            def rope_tail(sg=sg, slot=slot, M=M, cg=cg, sg_=sg_, gi=gi, t0=t0):
                                pq = pp.next()
                                S.mm(pq, pq.ap[:, 0:G], perm_t, perm_t.ap, sg, sg.ap)
                                t1 = t1r.next()
                                t2 = t2r.next()
                                tt('dve', t1, t1.ap, sg, sg.ap, cg, cg.ap, ALU.mult)
                                tt('dve', t2, t2.ap, pq, pq.ap[:, 0:G], sg_, sg_.ap, ALU.mult)
                                sg2 = sfm.next()
                                tt('pool', sg2, sg2.ap, t1, t1.ap, t2, t2.ap, ALU.add)
                                S.dma('pool', st.FA.ap[slot, 0:M, t0:t0 + G], sg2.ap[0:M, :], [sg2], [st.FAb[slot][gi]])
                            pend_rope[0] = rope_tail
                        else:
                            S.dma('pool', st.FA.ap[slot, 0:M, t0:t0 + G], sg.ap[0:M, :], [sg], [st.FAb[slot][gi]])
                    elif job[0] == 'tm':
                        _, c0, N, tblk, _, func, outk = job
                        for t2_ in range(2):
                            ps = pr.next()
                            for kc in range(8):
                                S.mm(ps, ps.ap[:, 0:N], h, h.ap[:, kc, t2_ * 128:(t2_ + 1) * 128], wb,
                                     wap[:, kc, c0:c0 + N], start=(kc == 0), stop=(kc == 7))
                            flush_rope()
                            sg = stm.next()
                            S.act(sg, sg.ap[:, 0:N], ps, ps.ap[:, 0:N], func)
                            r0 = t0 + t2_ * 128
                            if tblk is not None:
                                dc = {0: 0, 1: 512, 2: 1024}[tblk]
                                S.dma('pool', st.TA.ap[r0:r0 + 128, dc:dc + N], sg.ap[:, 0:N], [sg],
                                      [st.TAb[tblk][gi]])
                            if outk == 'cdk':
                                S.dma('pool', o_cdk.ap[st.pi, li].rearrange("h m t d -> t (h m) d")[r0:r0 + 128, :, :],
                                      sg.ap[:, 0:512].rearrange("t (a d) -> t a d", d=64), [sg], [o_cdk])
                            elif outk == 'cdv':
                                S.dma('pool', o_cdv.ap[st.pi, li].rearrange("h t d -> t h d")[r0:r0 + 128, :, :],
                                      sg.ap[:, 0:512].rearrange("t (a d) -> t a d", d=128), [sg], [o_cdv])
                            elif outk == 'csv':
                                S.dma('pool', o_csv.ap[st.pi, li].rearrange("k t d -> t k d")[r0:r0 + 128, :, :],
                                      sg.ap[:, 0:128].rearrange("t (a d) -> t a d", d=64), [sg], [o_csv])
                    else:
                        for t2_ in range(2):
                            ps = pr.next()
                            for kv in range(2):
                                for kc in range(8):
                                    S.mm(ps, ps.ap[:, kv * 64:(kv + 1) * 64], h, h.ap[:, kc, t2_ * 128:(t2_ + 1) * 128],
                                         wb, wap[:, kc, kv * 128:kv * 128 + 64], start=(kc == 0), stop=(kc == 7))
                            sg = stm.next()
                            S.act(sg, sg.ap[:, 0:128], ps, ps.ap[:, 0:128], AF.Identity)
                            r0 = t0 + t2_ * 128
                            S.dma('pool', o_csk.ap[st.pi, li].rearrange("k t d -> t k d")[r0:r0 + 128, :, :],
                                  sg.ap[:, 0:128].rearrange("t (a d) -> t a d", d=64), [sg], [o_csk])

        flush_rope()

    def head_norm(o_t, o_n, stt, stt2, h):
        hs = slice(h * 128, (h + 1) * 128)
        S.op('dve', 'bn_stats', [o_t], [stt], out=stt.ap[:, h * 8:h * 8 + 6], in_=o_t.ap[:, hs])
        S.op('dve', 'bn_aggr', [stt], [stt], out=stt.ap[:, h * 8 + 6:h * 8 + 8], in_=stt.ap[:, h * 8:h * 8 + 6])
        S.act(stt2, stt2.ap[:, h:h + 1], stt, stt.ap[:, h * 8 + 7:h * 8 + 8], AF.Sqrt, bias=eps_t.ap[:, 0:1],
              reads=[eps_t])
        S.op('dve', 'reciprocal', [stt2], [stt2], out=stt2.ap[:, 2 + h:3 + h], in_=stt2.ap[:, h:h + 1])
        ts('dve', o_n, o_n.ap[:, hs], o_t, o_t.ap[:, hs], stt.ap[:, h * 8 + 6:h * 8 + 7], stt2.ap[:, 2 + h:3 + h],
           ALU.subtract, ALU.mult, reads=[stt, stt2])

    def phaseB_lin(l, st, kind):
        li = l // 2
        T_ = st.T
        nch = T_ // 128
        ret = (kind == 'ret')
        qslot, kslot = (0, 2) if ret else (6, 8)
        mix0 = 0 if ret else 4
        st_in = st_ret if ret else st_gla
        o_st = o_ret if ret else o_gla
        sc = 1.0 if ret else 1.0 / 16.0
        lin_tiles = {}
        for pc in range(2):
            if pc == 0:
                lin_tiles['Of'] = [alloc("Of%d" % n_, [128, 256]) for n_ in range(nch)]
                lin_tiles['Sst'] = alloc("Sst", [128, 128])
                lin_tiles['rots'] = dict(
                    QTc=arot("QTc", [128, 128], 3), KTc=arot("KTc", [128, 128], 3), Vc=arot("Vc", [128, 256], 2),
                    Gc=arot("Gc", [128, 256], 2), EQr=arot("EQ", [128, 128], 2), EKr=arot("EK", [128, 128], 2),
                    qzr=[arot("qz%d" % h_, [128, 128], 2) for h_ in range(2)], kdr=arot("kd", [128, 128], 2),
                    ktr=arot("kt", [128, 128], 2), WTr=arot("WT", [128, 256], 2), Usr=arot("Us", [128, 256], 2),
                    otr=arot("ot", [128, 256], 2), onr=arot("on", [128, 256], 2), mfr=arot("mf", [128, 128], 2),
                    lapr=arot("lap", [128, 128], 2), e1r=arot("e1", [128, 128], 2), glr_r=arot("glrc", [17, 128], 2),
                    stt=alloc("stt", [128, 16]), stt2=alloc("stt2", [128, 4]), rd=alloc("rd", [128, 4]),
                    w2a=[alloc("w2a%d" % d, [17, 128]) for d in range(2)])
            Of = lin_tiles['Of']
            Sst = lin_tiles['Sst']
            R_ = lin_tiles['rots']
            QTc, KTc, Vc = R_['QTc'], R_['KTc'], R_['Vc']
            Gc, EQr, EKr, qzr, kdr, ktr = R_['Gc'], R_['EQr'], R_['EKr'], R_['qzr'], R_['kdr'], R_['ktr']
            WTr, Usr, otr, onr, mfr = R_['WTr'], R_['Usr'], R_['otr'], R_['onr'], R_['mfr']
            lapr, e1r, glr_r, stt, stt2, rd, w2a = R_['lapr'], R_['e1r'], R_['glr_r'], R_['stt'], R_['stt2'], R_['rd'], R_['w2a']
            if pc == 0:
                for h_ in range(2):
                    for b_ in qzr[h_].items:
                        S.op('dve', 'memset', [], [b_], ap=b_.ap, constant=0.0)
            pA = Rot([P[6]])
            pO = Rot([P[7]])
            if not ret:
                if pc == 0:
                    for b_ in glr_r.items:
                        S.op('dve', 'memset', [], [b_], ap=b_.ap, constant=1.0)
                for d in range(2):
                    S.dma('sp', w2a[d].ap[0:16, :], gla_w2.ap[li, d, :, pc * 128:(pc + 1) * 128], [gla_w2], [w2a[d]])
                    S.dma('sp', w2a[d].ap[16:17, :], gla_b.ap[li, d:d + 1, pc * 128:(pc + 1) * 128], [gla_b], [w2a[d]])
            else:
                for d in range(2):
                    S.dma('sp', rd.ap[:, d * 2:d * 2 + 2], ret_decay.ap[li, d, 2 * pc:2 * pc + 2].partition_broadcast(128),
                          [ret_decay], [rd])
                S.act(rd, rd.ap, rd, rd.ap, AF.Exp)
            for d in range(2):
                tri_d = tri2_t.ap[:, d * 256:d * 256 + 128]
                tri_d2 = tri2_t.ap[:, d * 256:(d + 1) * 256]
                deccol = 127 if d == 0 else 0
                if st.sample:
                    for h in range(2):
                        S.dma('sp', Sst.ap[h * 64:(h + 1) * 64, :], st_in.ap[li, d, 2 * pc + h], [st_in], [Sst])
                else:
                    S.op('dve', 'memset', [], [Sst], ap=Sst.ap, constant=0.0)

                def decay_tiles(lap):
                    pb_ = pA.next()
                    S.mm(pb_, pb_.ap[:, 0:128], lap, lap.ap, tri2_t, tri_d)
                    EQ = EQr.next()
                    EK = EKr.next()
                    S.act(EQ, EQ.ap, pb_, pb_.ap[:, 0:128], AF.Exp, scale=-sc)
                    S.act(EK, EK.ap, pb_, pb_.ap[:, 0:128], AF.Exp, scale=sc)
                    return EQ, EK
                const_dec = None
                if ret:
                    lap = lapr.next()
                    for h in range(2):
                        ts('dve', lap, lap.ap[:, h * 64:(h + 1) * 64], ones_t, ones_t.ap[:, 0:64],
                           rd.ap[:, d * 2 + h:d * 2 + h + 1], None, ALU.mult, reads=[rd])
                    const_dec = decay_tiles(lap)

                def front(n, out):
                    cs = slice(n * 128, (n + 1) * 128)
                    gi = n // 2
                    qc = QTc.next()
                    kc_ = KTc.next()
                    vc = Vc.next()
                    S.dma('sp', qc.ap, st.FA.ap[qslot + pc, :, cs], [st.FAb[qslot + pc][gi]], [qc])
                    S.dma('sp', kc_.ap, st.FA.ap[kslot + pc, :, cs], [st.FAb[kslot + pc][gi]], [kc_])
                    S.dma('sp', vc.ap, st.TA.ap[cs, pc * 256:(pc + 1) * 256], [st.TAb[0][gi]], [vc])
                    if not ret:
                        gl = glr_r.next()
                        S.dma('sp', gl.ap[0:16, :], st.FA.ap[10 + d, 0:16, cs], [st.FAb[10 + d][gi]], [gl])
                        yield
                        pz = pA.next()
                        S.mm(pz, pz.ap[:, 0:128], gl, gl.ap[0:17, :], w2a[d], w2a[d].ap[0:17, :])
                        yield
                        e1 = e1r.next()
                        S.act(e1, e1.ap, pz, pz.ap[:, 0:128], AF.Exp, scale=-1.0)
                        yield
                        lap_ = lapr.next()
                        S.act(lap_, lap_.ap, e1, e1.ap, AF.Ln, bias=ones_t.ap[:, 0:1], reads=[ones_t])
                        yield
                        EQ, EK = decay_tiles(lap_)
                    else:
                        EQ, EK = const_dec
                    yield
                    kd = kdr.next()
                    qz = [qzr[0].next(), qzr[1].next()]
                    for h in range(2):
                        rs = slice(h * 64, (h + 1) * 64)
                        tt('dve', qz[h], qz[h].ap[rs, :], qc, qc.ap[rs, :], EQ, EQ.ap[rs, :], ALU.mult)
                    tt('pool', kd, kd.ap, kc_, kc_.ap, EK, EK.ap, ALU.mult)
                    yield
                    pk = pA.next()
                    S.transpose(pk, pk.ap[:, 0:128], kd, kd.ap, ident_t, ident_t.ap)
                    yield
                    kt = ktr.next()
                    S.act(kt, kt.ap, pk, pk.ap[:, 0:128], AF.Identity)
                    yield
                    pa = pA.next()
                    for h in range(2):
                        S.mm(pa, pa.ap[:, h * 128:(h + 1) * 128], kd, kd.ap, qz[h], qz[h].ap)
                    yield
                    wt = WTr.next()
                    tt('dve', wt, wt.ap, pa, pa.ap[:, 0:256], tri2_t, tri_d2, ALU.mult)
                    yield
                    pu = pA.next()
                    S.mm(pu, pu.ap[:, 0:256], kt, kt.ap, vc, vc.ap)
                    yield
                    us = Usr.next()
                    S.act(us, us.ap, pu, pu.ap[:, 0:256], AF.Identity)
                    out.update(dict(EQ=EQ, qz=qz, wt=wt, vc=vc, us=us))

                def back(n, tl, first_pass):
                    cs = slice(n * 128, (n + 1) * 128)
                    gi = n // 2
                    EQ, qz, wt, vc, us = tl['EQ'], tl['qz'], tl['wt'], tl['vc'], tl['us']
                    po = pO.next()
                    for h in range(2):
                        hs = slice(h * 128, (h + 1) * 128)
                        S.mm(po, po.ap[:, hs], wt, wt.ap[:, hs], vc, vc.ap[:, hs], start=True, stop=False)
                        S.mm(po, po.ap[:, hs], qz[h], qz[h].ap, Sst, Sst.ap, start=False, stop=True)
                    yield
                    for h in range(2):
                        rs = slice(h * 64, (h + 1) * 64)
                        tt('dve', Sst, Sst.ap[rs, :], Sst, Sst.ap[rs, :], us, us.ap[rs, h * 128:(h + 1) * 128], ALU.add)
                        ts('dve', Sst, Sst.ap[rs, :], Sst, Sst.ap[rs, :], EQ.ap[rs, deccol:deccol + 1], None, ALU.mult,
                           reads=[EQ])
                    yield
                    if first_pass:
                        S.act(Of[n], Of[n].ap, po, po.ap[:, 0:256], AF.Identity)
                    else:
                        o_t = otr.next()
                        o_n = onr.next()
                        tt('dve', o_t, o_t.ap, po, po.ap[:, 0:256], Of[n], Of[n].ap, ALU.add)
                        gc = Gc.next()
                        S.dma('sp', gc.ap, st.TA.ap[cs, 512 + pc * 256:512 + (pc + 1) * 256], [st.TAb[1][gi]], [gc])
                        yield
                        for h in range(2):
                            head_norm(o_t, o_n, stt, stt2, h)
                            yield
                        tt('pool', o_n, o_n.ap, o_n, o_n.ap, gc, gc.ap, ALU.mult)
                        yield
                        for h in range(2):
                            pt_ = pO.next()
                            S.transpose(pt_, pt_.ap[:, 0:128], o_n, o_n.ap[:, h * 128:(h + 1) * 128], ident_t, ident_t.ap)
                            mf = mfr.next()
                            S.act(mf, mf.ap, pt_, pt_.ap[:, 0:128], AF.Identity)
                            ch = mix0 + 2 * pc + h
                            S.dma('pool', st.MIX.ap[ch, :, cs], mf.ap, [mf], [st.MIXb[ch][gi]])
                            yield

                order = list(range(nch)) if d == 0 else list(range(nch - 1, -1, -1))
                tl = {}
                for _ in front(order[0], tl):
                    yield
                for i_, n in enumerate(order):
                    gb = back(n, tl, d == 0)
                    tl2 = {}
                    gf = front(order[i_ + 1], tl2) if i_ + 1 < nch else None
                    while gb is not None or gf is not None:
                        if gb is not None and next(gb, 'END') == 'END':
                            gb = None
                        if gf is not None and next(gf, 'END') == 'END':
                            gf = None
                        yield
                    tl = tl2
                if not st.sample:
                    for h in range(2):
                        S.dma('pool', o_st.ap[st.pi, li, d, 2 * pc + h], Sst.ap[h * 64:(h + 1) * 64, :], [Sst], [o_st])

    def run_attn(QT, q0, nqb, entries, KT, V, dv2, PTr, pS, acc):
        cnt = [0] * nqb
        tot = [0] * nqb
        for (kb, qlo, qhi, masks) in entries:
            for qb in range(qlo, qhi):
                tot[qb] += 1

        def pv(ent, pt):
            kb, qlo, qhi, masks = ent
            for qb in range(qlo, qhi):
                o_ = (qb - qlo) * 128
                S.mm(acc[qb], acc[qb].ap[:, 0:dv2], pt, pt.ap[:, o_:o_ + 128], V, V.ap[:, kb, :],
                     start=(cnt[qb] == 0), stop=(cnt[qb] == tot[qb] - 1))
                cnt[qb] += 1
        pend = None
        for ent in entries:
            kb, qlo, qhi, masks = ent
            ps = pS.next()
            w = (qhi - qlo) * 128
            S.mm(ps, ps.ap[:, 0:w], KT, KT.ap[:, kb * 128:(kb + 1) * 128], QT,
                 QT.ap[:, qlo * 128:qhi * 128])
            pt = PTr.next()
            S.act(pt, pt.ap[:, 0:w], ps, ps.ap[:, 0:w], AF.Exp, scale=0.125)
            for qb, mk in masks.items():
                o_ = (qb - qlo) * 128
                tt('pool', pt, pt.ap[:, o_:o_ + 128], pt, pt.ap[:, o_:o_ + 128], tri2_t, mk, ALU.mult)
            if pend is not None:
                pv(*pend)
            pend = (ent, pt)
            yield
        if pend is not None:
            pv(*pend)
        yield

    accsets = [P[0:2], P[2:4]]
    ucnt = [0]

    def phaseB_diff(l, st):
        li = l // 2
        lam_init = 0.8 - 0.6 * math.exp(-0.3 * l)
        T_ = st.T
        nq = T_ // 128
        koff = st.koff
        nkb = (koff + T_) // 128
        kb0 = koff // 128
        QZ = [[alloc("QZ%d_%d" % (m_, b_), [128, 512]) for b_ in range(2)] for m_ in range(2)]
        for m_ in range(2):
            for b_ in range(2):
                S.op('dve', 'memset', [], [QZ[m_][b_]], ap=QZ[m_][b_].ap, constant=0.0)
        KT = alloc("KT", [128, koff + T_])
        V = alloc("V", [128, nkb, 130])
        PTr = arot("PT", [128, 512], 3)
        o1n = alloc("o1n", [128, 4, 128])
        tr_ = arot("tq", [128, 128], 4)
        or_ = arot("oq", [128, 128], 2)
        onr = arot("onq", [128, 128], 2)
        stgr = arot("stgq", [128, 512], 2)
        kst = arot("kst", [128, 128], 2)
        sm = alloc("sm", [128, 8])
        stt = alloc("stt", [128, 16])
        stt2 = alloc("stt2", [128, 4])
        dl = alloc("dl", [128, 256])
        pr_ = alloc("prd", [128, 128])
        lam = alloc("lam", [128, 4])
        pS = Rot(P[4:6])
        S.dma('sp', dl.ap, diff_lam.ap[li].partition_broadcast(128), [diff_lam], [dl])
        dl4 = dl.ap.rearrange("p (a b d) -> p a b d", a=2, b=2)
        tt('dve', pr_, pr_.ap.rearrange("p (a d) -> p a d", a=2), dl, dl4[:, :, 0, :], dl, dl4[:, :, 1, :], ALU.mult)
        S.op('dve', 'tensor_reduce', [pr_], [lam], out=lam.ap[:, 0:2], in_=pr_.ap.rearrange("p (a d) -> p a d", a=2),
             axis=AX, op=ALU.add)
        S.act(lam, lam.ap[:, 0:2], lam, lam.ap[:, 0:2], AF.Exp)
        tt('dve', lam, lam.ap[:, 2:3], lam, lam.ap[:, 1:2], lam, lam.ap[:, 0:1], ALU.subtract)
        ts('dve', lam, lam.ap[:, 3:4], lam, lam.ap[:, 2:3], -lam_init, None, ALU.add)
        S.op('dve', 'memset', [], [V], ap=V.ap[:, :, 128:130], constant=1.0)
        for h in range(4):
            yield
            S.dma('sp', KT.ap[:, koff:koff + T_], st.FA.ap[8 + h], st.FAb[8 + h], [KT])
            S.dma('sp', V.ap[:, kb0:nkb, 0:128],
                  st.TA.ap[:, 1024 + h * 128:1024 + (h + 1) * 128].rearrange("(n t) d -> t n d", t=128),
                  st.TAb[2], [V])
            if st.sample:
                S.dma('sp', V.ap[:, 0:2, 0:128], cdv_i.ap[li, h].rearrange("(n t) d -> t n d", t=128), [cdv_i], [V])
                for lb in range(2):
                    ks = kst.next()
                    S.dma('sp', ks.ap.rearrange("t (m d) -> t m d", m=2),
                          cdk_i.ap[li, h, :, lb * 128:(lb + 1) * 128, :].rearrange("m t d -> t m d"), [cdk_i], [ks])
                    pb7 = pS.next()
                    S.transpose(pb7, pb7.ap[:, 0:128], ks, ks.ap, ident_t, ident_t.ap)
                    S.act(KT, KT.ap[:, lb * 128:(lb + 1) * 128], pb7, pb7.ap[:, 0:128], AF.Identity)
                    yield
            for qg in range((nq + 3) // 4):
                nqb = min(4, nq - 4 * qg)
                stg = stgr.next()
                c0 = qg * 512
                ggs = list(range(c0 // G, (c0 + nqb * 128 + G - 1) // G))
                for m_ in range(2):
                    qz_ = QZ[m_][qg % 2]
                    S.dma('sp', qz_.ap[m_ * 64:(m_ + 1) * 64, 0:nqb * 128],
                          st.FA.ap[4 + h, m_ * 64:(m_ + 1) * 64, c0:c0 + nqb * 128], [st.FAb[4 + h][gg] for gg in ggs], [qz_])
                for m in range(2):
                    acc = P[0:nqb]
                    entries = [(kb, 0, nqb, {}) for kb in range(nkb)]
                    yield from run_attn(QZ[m][qg % 2], 0, nqb, entries, KT, V, 130, PTr, pS, acc)
                    t_s = []
                    for qb in range(nqb):
                        a = acc[qb]
                        S.op('dve', 'reciprocal', [a], [sm], out=sm.ap[:, m * 4 + qb:m * 4 + qb + 1], in_=a.ap[:, 128:129])
                        if m == 0:
                            ts('dve', o1n, o1n.ap[:, qb, :], a, a.ap[:, 0:128], sm.ap[:, qb:qb + 1], None, ALU.mult,
                               reads=[sm])
                        else:
                            t_ = tr_.next()
                            ts('dve', t_, t_.ap, a, a.ap[:, 0:128], sm.ap[:, 4 + qb:5 + qb], lam.ap[:, 3:4], ALU.mult,
                               ALU.mult, reads=[sm, lam])
                            t_s.append(t_)
                    for qb in range(nqb):
                        if m == 1:
                            t_ = t_s[qb]
                            o_ = or_.next()
                            tt('dve', o_, o_.ap, t_, t_.ap, o1n, o1n.ap[:, qb, :], ALU.add)
                            on_ = onr.next()
                            head_norm(o_, on_, stt, stt2, 0)
                            pb7 = pS.next()
                            S.transpose(pb7, pb7.ap[:, 0:128], on_, on_.ap, ident_t, ident_t.ap)
                            S.act(stg, stg.ap[:, qb * 128:(qb + 1) * 128], pb7, pb7.ap[:, 0:128], AF.Identity,
                                  scale=float(1.0 - lam_init))
                            yield
                c0 = qg * 512
                S.dma('pool', st.MIX.ap[4 + h, :, c0:c0 + nqb * 128], stg.ap[:, 0:nqb * 128], [stg],
                      [st.MIXb[4 + h][gg] for gg in range(c0 // G, (c0 + nqb * 128 + G - 1) // G)])

    def phaseB_swa(l, st):
        li = l // 2
        T_ = st.T
        nq = T_ // 128
        koff = st.koff
        nkb = (koff + T_) // 128
        kb0 = koff // 128
        QZ = [[alloc("QZ%d_%d" % (m_, b_), [128, 512]) for b_ in range(2)] for m_ in range(2)]
        for m_ in range(2):
            for b_ in range(2):
                S.op('dve', 'memset', [], [QZ[m_][b_]], ap=QZ[m_][b_].ap, constant=0.0)
        KT = alloc("KT", [128, koff + T_])
        V = alloc("V", [128, nkb, 66])
        PTr = arot("PT", [128, 512], 3)
        mixtm = alloc("mixtm", [128, 4, 128])
        stgr = arot("stgq", [128, 512], 2)
        kst = arot("kst", [128, 128], 2)
        sm = alloc("sm", [128, 8])
        esink = alloc("esink", [128, 8])
        pS = Rot(P[4:6])
        S.dma('sp', esink.ap, swa_sink.ap[li].partition_broadcast(128), [swa_sink], [esink])
        S.act(esink, esink.ap, esink, esink.ap, AF.Exp)
        S.op('dve', 'memset', [], [V], ap=V.ap[:, :, 64:66], constant=1.0)
        trif = tri2_t.ap[:, 0:128]
        trib = tri2_t.ap[:, 256:384]
        for pc in range(4):
            kv = pc // 2
            if pc % 2 == 0:
                S.dma('sp', KT.ap[:, koff:koff + T_], st.FA.ap[4 + kv], st.FAb[4 + kv], [KT])
                S.dma('sp', V.ap[:, kb0:nkb, 0:64],
                      st.TA.ap[:, 1024 + kv * 64:1024 + (kv + 1) * 64].rearrange("(n t) d -> t n d", t=128),
                      st.TAb[2], [V])
                if st.sample:
                    S.dma('sp', V.ap[:, 0:2, 0:64], csv_i.ap[li, kv].rearrange("(n t) d -> t n d", t=128), [csv_i], [V])
                    for lb in range(2):
                        ks = kst.next()
                        for dup in range(2):
                            S.dma('sp', ks.ap[:, dup * 64:(dup + 1) * 64], csk_i.ap[li, kv, lb * 128:(lb + 1) * 128, :],
                                  [csk_i], [ks])
                        pb7 = pS.next()
                        S.transpose(pb7, pb7.ap[:, 0:128], ks, ks.ap, ident_t, ident_t.ap)
                        S.act(KT, KT.ap[:, lb * 128:(lb + 1) * 128], pb7, pb7.ap[:, 0:128], AF.Identity)
                        yield
            for qg in range((nq + 3) // 4):
                nqb = min(4, nq - 4 * qg)
                q0 = 4 * qg
                c0 = qg * 512
                ggs = list(range(c0 // G, (c0 + nqb * 128 + G - 1) // G))
                for m_ in range(2):
                    qz_ = QZ[m_][(pc * 4 + qg) % 2]
                    S.dma('sp', qz_.ap[m_ * 64:(m_ + 1) * 64, 0:nqb * 128],
                          st.FA.ap[pc, m_ * 64:(m_ + 1) * 64, c0:c0 + nqb * 128], [st.FAb[pc][gg] for gg in ggs], [qz_])
                for hh in range(2):
                    acc = P[0:nqb]
                    h = 2 * pc + hh
                    entries = []
                    if st.sample:
                        entries += [(kb, 0, nqb, {}) for kb in range(kb0)]
                        for j in range(q0 - 1, q0 + nqb + 1):
                            if j < 0 or j >= nq:
                                continue
                            qlo = max(j - 1, q0) - q0
                            qhi = min(j + 1, q0 + nqb - 1) - q0 + 1
                            masks = {}
                            if q0 <= j + 1 < q0 + nqb:
                                masks[j + 1 - q0] = trib
                            if q0 <= j - 1 < q0 + nqb:
                                masks[j - 1 - q0] = trif
                            entries.append((kb0 + j, qlo, qhi, masks))
                    else:
                        entries += [(kb, 0, nqb, {}) for kb in range(nkb)]
                    yield from run_attn(QZ[hh][(pc * 4 + qg) % 2], 0, nqb, entries, KT, V, 66, PTr, pS, acc)
                    for qb in range(nqb):
                        a = acc[qb]
                        tt('dve', sm, sm.ap[:, qb:qb + 1], a, a.ap[:, 64:65], esink, esink.ap[:, h:h + 1], ALU.add)
                        S.op('dve', 'reciprocal', [sm], [sm], out=sm.ap[:, 4 + qb:5 + qb], in_=sm.ap[:, qb:qb + 1])
                        ts('dve', mixtm, mixtm.ap[:, qb, hh * 64:(hh + 1) * 64], a, a.ap[:, 0:64], sm.ap[:, 4 + qb:5 + qb],
                           None, ALU.mult, reads=[sm])
                stg = stgr.next()
                for qb in range(nqb):
                    pb7 = pS.next()
                    S.transpose(pb7, pb7.ap[:, 0:128], mixtm, mixtm.ap[:, qb, :], ident_t, ident_t.ap)
                    S.act(stg, stg.ap[:, qb * 128:(qb + 1) * 128], pb7, pb7.ap[:, 0:128], AF.Identity)
                    yield
                c0 = qg * 512
                S.dma('pool', st.MIX.ap[pc, :, c0:c0 + nqb * 128], stg.ap[:, 0:nqb * 128], [stg],
                      [st.MIXb[pc][gg] for gg in range(c0 // G, (c0 + nqb * 128 + G - 1) // G)])

    def phaseC_layer(l):
        modt = modts[l % 2]
        li = l // 2
        Wout = w_out_even if l % 2 == 0 else w_out_odd
        new_phase()
        M8s = [alloc("M8_%d" % i, [128, 8, G]) for i in range(2)]
        zTs = [alloc("zT_%d" % i, [128, 8, G]) for i in range(2)]
        U = alloc("U", [128, 32, G])
        sq = alloc("sq", [128, 8, G])
        rstds = [alloc("rstd%d" % i, [128, G]) for i in range(2)]
        rr = arot("rr", [128, G], 2)
        csum = alloc("csum", [128, G])
        pr = Rot(P[0:6])
        pm, pv = P[6], P[7]
        groups = [(st, gi) for st in streams for gi in range(st.ng)]
        n = len(groups)

        def xg(k):
            st, gi = groups[k]
            return xT[st.g0 + gi], st.c

        def partA(k):
            st, gi = groups[k]
            x, c = xg(k)
            t0 = gi * G
            M8 = M8s[k % 2]
            zT = zTs[0]
            S.dma('sp', M8.ap, st.MIX.ap[:, :, t0:t0 + G].rearrange("c p t -> p c t"),
                  [st.MIXb[cc][gi] for cc in range(8)], [M8])
            S.act(x, x.ap, x, x.ap, AF.Identity, scale=float(ALPHA))
            for ob in range(2):
                wb, wap = wload(Wout, Wout.ap[li, :, ob * 512:(ob + 1) * 512].rearrange("(k p) n -> p k n", p=128),
                                [128, 8, 512])
                for o4 in range(4):
                    oc = ob * 4 + o4
                    ps = pr.next()
                    for kc in range(8):
                        S.mm(ps, ps.ap[:, 0:G], wb, wap[:, kc, o4 * 128:(o4 + 1) * 128], M8, M8.ap[:, kc, :],
                             start=(kc == 0), stop=(kc == 7))
                    S.op('dve', 'scalar_tensor_tensor', [ps, modt, x], [zT], out=zT.ap[:, oc, :], in0=ps.ap[:, 0:G],
                         scalar=modt.ap[:, c, 16 + oc:17 + oc], in1=x.ap[:, oc, :], op0=ALU.mult, op1=ALU.add)

        def ln_gen(zT, rstd, dst, lc, post=None):
            S.op('dve', 'tensor_reduce', [zT], [csum], out=csum.ap, in_=zT.ap.rearrange("p c t -> p t c"), axis=AX,
                 op=ALU.add)
            S.mm(pm, pm.ap[:, 0:G], onesd_t, onesd_t.ap, csum, csum.ap, start=True, stop=True)
            yield
            tt('dve', zT, zT.ap, zT, zT.ap, pm, pm.ap[:, 0:G].unsqueeze(1).broadcast_to([128, 8, G]), ALU.subtract)
            S.act(sq, sq.ap, zT, zT.ap, AF.Square)
            S.op('dve', 'tensor_reduce', [sq], [csum], out=csum.ap, in_=sq.ap.rearrange("p c t -> p t c"), axis=AX,
                 op=ALU.add)
            yield
            S.mm(pv, pv.ap[:, 0:G], onesd_t, onesd_t.ap, csum, csum.ap, start=True, stop=True)
            yield
            S.act(rstd, rstd.ap, pv, pv.ap[:, 0:G], AF.Sqrt, bias=eps_t.ap[:, 0:1], reads=[eps_t])
            S.op('dve', 'reciprocal', [rstd], [rstd], out=rstd.ap, in_=rstd.ap)
            tt('dve', zT, zT.ap, zT, zT.ap, rstd, rstd.ap.unsqueeze(1).broadcast_to([128, 8, G]), ALU.mult)
            for kc in range(8):
                S.act(dst, dst.ap[:, kc, :], zT, zT.ap[:, kc, :], AF.Identity, scale=lng_t.ap[:, lc + kc:lc + kc + 1],
                      bias=lnb_t.ap[:, lc + kc:lc + kc + 1], reads=[lng_t, lnb_t])
            if post is not None:
                post()

        def step(gen):
            if gen is not None:
                next(gen, None)

        def finish(gen):
            if gen is not None:
                for _ in gen:
                    pass

        def make_ln1(k):
            x, c = xg(k)
            M8 = M8s[k % 2]

            def post():
                for kc in range(8):
                    S.act(M8, M8.ap[:, kc, :], x, x.ap[:, kc, :], AF.Identity, scale=modt.ap[:, c, 32 + kc:33 + kc],
                          bias=modt.ap[:, c, 24 + kc:25 + kc], reads=[modt])
            return ln_gen(zTs[0], rstds[0], x, (l * 2 + 0) * 8, post)

        def make_ln2(k):
            x, c = xg(k)
            return ln_gen(zTs[1], rstds[1], x, (l * 2 + 1) * 8)

        def ff1(k, filler):
            M8 = M8s[k % 2]
            for jb in range(8):
                wb, wap = wload(w_ff1, w_ff1.ap[l, :, jb * 512:(jb + 1) * 512].rearrange("(k p) n -> p k n", p=128),
                                [128, 8, 512])
                for jj in range(4):
                    j = jb * 4 + jj
                    ps = pr.next()
                    for kc in range(8):
                        S.mm(ps, ps.ap[:, 0:G], wb, wap[:, kc, jj * 128:(jj + 1) * 128], M8, M8.ap[:, kc, :],
                             start=(kc == 0), stop=(kc == 7))
                    r_ = rr.next()
                    S.act(r_, r_.ap, ps, ps.ap[:, 0:G], AF.Relu)
                    tt('pool', U, U.ap[:, j, :], r_, r_.ap, r_, r_.ap, ALU.mult)
                if jb % 2 == 0:
                    step(filler)

        def ff2(k, filler):
            x, c = xg(k)
            zT = zTs[1]
            for oc in range(8):
                wb, wap = wload(w_ff2, w_ff2.ap[l, :, oc * 128:(oc + 1) * 128].rearrange("(j p) n -> p j n", p=128),
                                [128, 32, 128])
                ps = pr.next()
                for j in range(32):
                    S.mm(ps, ps.ap[:, 0:G], wb, wap[:, j, :], U, U.ap[:, j, :], start=(j == 0), stop=(j == 31))
                S.op('dve', 'scalar_tensor_tensor', [ps, modt, x], [zT], out=zT.ap[:, oc, :], in0=ps.ap[:, 0:G],
                     scalar=modt.ap[:, c, 40 + oc:41 + oc], in1=x.ap[:, oc, :], op0=ALU.mult, op1=ALU.add)
                if oc % 2 == 0:
                    step(filler)

        last = (l == NL - 1)
        if last:
            orot = arot("xout", [128, 1024], 2)

        def store_group(k):
            st, gi = groups[k]
            x = xT[st.g0 + gi]
            for t2 in range(2):
                r0 = gi * G + t2 * 128
                ot_ = orot.next()
                for hf in range(2):
                    ps = pr.next()
                    for c4 in range(4):
                        cc = hf * 4 + c4
                        S.transpose(ps, ps.ap[:, c4 * 128:(c4 + 1) * 128], x, x.ap[:, cc, t2 * 128:(t2 + 1) * 128],
                                    ident_t, ident_t.ap)
                    S.act(ot_, ot_.ap[:, hf * 512:(hf + 1) * 512], ps, ps.ap, AF.Identity)
                if st.sample:
                    S.dma('pool', ys.ap[r0:r0 + 128, :], ot_.ap, [ot_], [ys])
                else:
                    S.dma('pool', yp.ap[st.pi, r0:r0 + 128, :], ot_.ap, [ot_], [yp])

        partA(0)
        finish(make_ln1(0))
        ln2_prev = None
        for k in range(n):
            ff1(k, ln2_prev)
            finish(ln2_prev)
            if last and k >= 1:
                store_group(k - 1)
            if k + 1 < n:
                partA(k + 1)
            x, c = xg(k)
            S.act(x, x.ap, x, x.ap, AF.Identity, scale=float(ALPHA))
            ln1_next = make_ln1(k + 1) if k + 1 < n else None
            ff2(k, ln1_next)
            finish(ln1_next)
            ln2_prev = make_ln2(k)
        finish(ln2_prev)
        if last:
            store_group(n - 1)

    import os
    stop = os.environ.get("KSTOP", "")
    if stop == "load":
        load_input()
    for l in range(NL):
        if stop == "load":
            break
        if l == 0:
            for _ in compute_mod_gen(0, P[6]):
                pass
        if stop == "mod":
            break
        for st in streams:
            phaseA(l, st)
        if stop == "A":
            break
        gm_ = compute_mod_gen(l + 1, P[3]) if l + 1 < NL else None
        for st in streams:
            new_phase()
            if l % 2 == 0:
                ga, gl_, ratio = phaseB_diff(l, st), phaseB_lin(l, st, 'ret'), 2
            else:
                ga, gl_, ratio = phaseB_swa(l, st), phaseB_lin(l, st, 'gla'), 5
            if os.environ.get("KSEQ"):
                for _ in ga:
                    pass
                ga = None
            while ga is not None or gl_ is not None:
                if ga is not None and next(ga, 'END') == 'END':
                    ga = None
                for _ in range(ratio):
                    if gl_ is not None and next(gl_, 'END') == 'END':
                        gl_ = None
                if gm_ is not None and not st.sample and next(gm_, 'END') == 'END':
                    gm_ = None
        if gm_ is not None:
            for _ in gm_:
                pass
        if stop in ("B", "B1"):
            break
        phaseC_layer(l)
    if stop:
        store_output()
    S.finish()
    return nc, S


OUT_NAMES = ["yp", "ys", "o_ret", "o_cdk", "o_cdv", "o_csk", "o_csv", "o_gla"]


def make_in_maps(inp, T=2048):
    hc = host_consts(T)
    oidx = odd_col_index()
    w_in_odd_x = np.ascontiguousarray(inp['w_in_odd'][:, :, oidx])
    maps = []
    f = lambda a: np.ascontiguousarray(np.asarray(a, dtype=np.float32))
    for i in range(8):
        b = i % 4
        m = dict(
            xs=f(inp['x_sample'][b]), xp=f(inp['x_prompt'][2 * i:2 * i + 2]),
            cond=f(np.stack([inp['c'][b], inp['c_ctx']])),
            st_ret=f(inp['state_ret'][b]), cdk_i=f(inp['cache_diff_k'][b]), cdv_i=f(inp['cache_diff_v'][b]),
            csk_i=f(inp['cache_swa_k'][b]), csv_i=f(inp['cache_swa_v'][b]), st_gla=f(inp['state_gla'][b]),
            w_mod=f(inp['w_mod']), b_mod=f(inp['b_mod']), ln_g=f(inp['ln_g']), ln_b=f(inp['ln_b']),
            w_in_even=f(inp['w_in_even']), w_out_even=f(inp['w_out_even']), ret_decay=f(inp['ret_decay']),
            diff_lam=f(np.reshape(inp['diff_lam'], (2, 256))), w_in_odd=w_in_odd_x, w_out_odd=f(inp['w_out_odd']),
            swa_sink=f(inp['swa_sink']), gla_w2=f(inp['gla_w2']), gla_b=f(inp['gla_b']),
            w_ff1=f(inp['w_ff1']), w_ff2=f(inp['w_ff2']), **hc)
        maps.append(m)
    return maps


_CACHE = {}


def kernel(**inputs):
    inp = {k: np.asarray(v) for k, v in inputs.items()}
    T = inp['x_sample'].shape[1]
    if T not in _CACHE:
        _CACHE[T] = build(T)[0]
    nc = _CACHE[T]
    maps = make_in_maps(inp, T)
    res = run_bass_kernel_spmd(nc, maps, core_ids=list(range(8)))
    R = res.results
    y_p = np.concatenate([R[i]['yp'] for i in range(8)], axis=0)
    y_s = np.stack([R[b]['ys'] for b in range(4)], axis=0)
    outs = [y_p, y_s]
    for nm in OUT_NAMES[2:]:
        outs.append(np.concatenate([R[i][nm] for i in range(8)], axis=0))
    return tuple(np.ascontiguousarray(o, dtype=np.float32) for o in outs)
```
